# Optimizing a Trainium2 kernel written in Bass

```python
import math
import jax, jax.numpy as jnp
from jax import lax
import numpy as np

D_MODEL = 1024
BATCH = 8
SEQ = 2048
DEPTH = 2

SSM_WIDTH = D_MODEL
MLSTM_WIDTH = D_MODEL
MIX_WIDTH = SSM_WIDTH + MLSTM_WIDTH
SSM_GROUP = 16
SSM_GROUPS = SSM_WIDTH // SSM_GROUP
SSM_STATE = 64
MLSTM_HEADS = 4
MLSTM_HEAD_DIM = MLSTM_WIDTH // MLSTM_HEADS
CONV_WIDTH = 4
MLSTM_CHUNK = 64
IN_COLS = 2 * SSM_WIDTH + 3 * MLSTM_WIDTH
EPS = 1e-6

kernel_name = "hybrid_s5_mlstm_parallel_heads"


def rms_norm(x, gain):
    xf = x.astype(jnp.float32)
    y = xf * lax.rsqrt(jnp.mean(xf * xf, axis=-1, keepdims=True) + EPS)
    return y * gain.astype(jnp.float32)


def s5_branch(u, lam_re, lam_im, log_dt, b_re, b_im, c_re, c_im, d_skip, w_glu, b_glu):
    bsz, seq, _ = u.shape
    f32 = jnp.float32
    uf = u.astype(f32).reshape(bsz, seq, SSM_GROUPS, SSM_GROUP)
    lam = lax.complex(lam_re.astype(f32), lam_im.astype(f32))
    dt = jnp.exp(log_dt.astype(f32))[:, None]
    a_bar = jnp.exp(lam * dt)
    b_cplx = lax.complex(b_re.astype(f32), b_im.astype(f32))
    b_bar = ((a_bar - 1.0) / lam)[..., None] * b_cplx
    bu = jnp.einsum('bsgc,gpc->bsgp', uf.astype(jnp.complex64), b_bar)
    a_seq = jnp.broadcast_to(a_bar, (1, seq) + a_bar.shape)

    def combine(left, right):
        a_l, b_l = left
        a_r, b_r = right
        return a_r * a_l, a_r * b_l + b_r

    _, states = lax.associative_scan(combine, (a_seq, bu), axis=1)
    c_cplx = lax.complex(c_re.astype(f32), c_im.astype(f32))
    y = jnp.real(jnp.einsum('bsgp,gcp->bsgc', states, c_cplx))
    y = y + d_skip.astype(f32).reshape(SSM_GROUPS, SSM_GROUP) * uf
    y = jax.nn.gelu(y.reshape(bsz, seq, SSM_WIDTH))
    return y * jax.nn.sigmoid(y @ w_glu.astype(f32) + b_glu.astype(f32))


def mlstm_cell(q, k, v, i_pre, log_f):
    bsz, nh, seq, dh = q.shape
    L = MLSTM_CHUNK
    nc = seq // L

    def to_chunks(t):
        t = t.reshape((bsz, nh, nc, L) + t.shape[3:])
        return jnp.moveaxis(t, 2, 0)

    causal = jnp.tril(jnp.ones((L, L), dtype=bool))

    def step(carry, inp):
        c_mat, n_vec, m_prev = carry
        q_, k_, v_, i_, lf = inp
        b = jnp.cumsum(lf, axis=-1)
        b_tot = b[..., -1]
        log_d = jnp.where(causal, b[..., :, None] - b[..., None, :] + i_[..., None, :], -jnp.inf)
        m_inter = b + m_prev[..., None]
        m_t = jnp.maximum(m_inter, jnp.max(log_d, axis=-1))
        w_inter = jnp.exp(m_inter - m_t)
        s = jnp.einsum('bhld,bhsd->bhls', q_, k_) * jnp.exp(log_d - m_t[..., None])
        num = (w_inter[..., None] * jnp.einsum('bhvd,bhld->bhlv', c_mat, q_)
               + jnp.einsum('bhls,bhsv->bhlv', s, v_))
        den = w_inter * jnp.einsum('bhd,bhld->bhl', n_vec, q_) + jnp.sum(s, axis=-1)
        h = num / jnp.maximum(jnp.abs(den), jnp.exp(-m_t))[..., None]
        log_w = b_tot[..., None] - b + i_
        m_next = jnp.maximum(b_tot + m_prev, jnp.max(log_w, axis=-1))
        decay = jnp.exp(b_tot + m_prev - m_next)
        w = jnp.exp(log_w - m_next[..., None])
        c_mat = decay[..., None, None] * c_mat + jnp.einsum('bhs,bhsv,bhsd->bhvd', w, v_, k_)
        n_vec = decay[..., None] * n_vec + jnp.einsum('bhs,bhsd->bhd', w, k_)
        return (c_mat, n_vec, m_next), h

    init = (jnp.zeros((bsz, nh, dh, dh), jnp.float32),
            jnp.zeros((bsz, nh, dh), jnp.float32),
            jnp.zeros((bsz, nh), jnp.float32))
    _, hs = lax.scan(step, init, (to_chunks(q), to_chunks(k), to_chunks(v),
                                  to_chunks(i_pre), to_chunks(log_f)))
    return jnp.moveaxis(hs, 0, 2).reshape(bsz, nh, seq, dh)


def mlstm_branch(xm, o_pre, conv_w, conv_b, wq, wk, wv, w_gates, b_igate, b_fgate, norm_gain, skip):
    bsz, seq, _ = xm.shape
    f32 = jnp.float32
    H, dh = MLSTM_HEADS, MLSTM_HEAD_DIM
    xc = lax.conv_general_dilated(
        xm, conv_w[:, None, :].astype(xm.dtype), window_strides=(1,),
        padding=[(CONV_WIDTH - 1, 0)], dimension_numbers=('NWC', 'WIO', 'NWC'),
        feature_group_count=MLSTM_WIDTH)
    xc = jax.nn.silu(xc.astype(f32) + conv_b.astype(f32))
    xmf = xm.astype(f32)
    q = jnp.einsum('bshd,hde->bshe', xc.reshape(bsz, seq, H, dh), wq.astype(f32))
    k = jnp.einsum('bshd,hde->bshe', xc.reshape(bsz, seq, H, dh), wk.astype(f32)) * (dh ** -0.5)
    v = jnp.einsum('bshd,hde->bshe', xmf.reshape(bsz, seq, H, dh), wv.astype(f32))
    qkv = jnp.concatenate([q.reshape(bsz, seq, -1), k.reshape(bsz, seq, -1),
                           v.reshape(bsz, seq, -1)], axis=-1)
    gates = qkv @ w_gates.astype(f32)
    i_pre = gates[..., :H] + b_igate.astype(f32)
    log_f = jax.nn.log_sigmoid(gates[..., H:] + b_fgate.astype(f32))
    h = mlstm_cell(q.transpose(0, 2, 1, 3), k.transpose(0, 2, 1, 3), v.transpose(0, 2, 1, 3),
                   i_pre.transpose(0, 2, 1), log_f.transpose(0, 2, 1))
    h = h.transpose(0, 2, 1, 3) * jax.nn.sigmoid(o_pre.astype(f32)).reshape(bsz, seq, H, dh)
    mu = jnp.mean(h, axis=-1, keepdims=True)
    var = jnp.mean(jnp.square(h - mu), axis=-1, keepdims=True)
    hn = ((h - mu) * lax.rsqrt(var + EPS)).reshape(bsz, seq, MLSTM_WIDTH) * norm_gain.astype(f32)
    return hn + skip.astype(f32) * xc


def hybrid_layer(x, cond, norm_gain, w_mod, b_mod, w_in,
                 ssm_lambda_re, ssm_lambda_im, ssm_log_dt, ssm_b_re, ssm_b_im, ssm_c_re, ssm_c_im,
                 ssm_d, ssm_w_glu, ssm_b_glu, ssm_out_gain,
                 m_conv_w, m_conv_b, m_wq, m_wk, m_wv, m_w_gates, m_b_igate, m_b_fgate,
                 m_norm_gain, m_skip, w_out):
    mod = jax.nn.silu(cond) @ w_mod + b_mod
    shift, scale, gate = jnp.split(mod.astype(jnp.float32), 3, axis=-1)
    h = (rms_norm(x, norm_gain) * (1.0 + scale[:, None, :]) + shift[:, None, :]).astype(x.dtype)
    proj = h @ w_in
    w = SSM_WIDTH
    ssm_in, ssm_gate, m_in, m_o, m_gate = jnp.split(
        proj, [w, 2 * w, 2 * w + MLSTM_WIDTH, 2 * w + 2 * MLSTM_WIDTH], axis=-1)
    ssm_y = s5_branch(ssm_in, ssm_lambda_re, ssm_lambda_im, ssm_log_dt, ssm_b_re, ssm_b_im,
                      ssm_c_re, ssm_c_im, ssm_d, ssm_w_glu, ssm_b_glu)
    ssm_y = rms_norm(ssm_y, ssm_out_gain) * jax.nn.silu(ssm_gate.astype(jnp.float32))
    m_y = mlstm_branch(m_in, m_o, m_conv_w, m_conv_b, m_wq, m_wk, m_wv, m_w_gates,
                       m_b_igate, m_b_fgate, m_norm_gain, m_skip)
    m_y = m_y * jax.nn.silu(m_gate.astype(jnp.float32))
    mixed = jnp.concatenate([ssm_y, m_y], axis=-1).astype(x.dtype)
    out = mixed @ w_out
    return (x.astype(jnp.float32) + gate[:, None, :] * out.astype(jnp.float32)).astype(x.dtype)


def setup_inputs(seed: int = 0) -> dict:
    key = jax.random.key(seed)
    ks = list(jax.random.split(key, 32))
    ctr = [0]

    def nk():
        ctr[0] += 1
        return ks[ctr[0] - 1]

    def nrm(shape, scale):
        return jax.random.normal(nk(), shape, jnp.float32) * scale

    L, D, W, MW = DEPTH, D_MODEL, SSM_WIDTH, MLSTM_WIDTH
    G, P, Cg = SSM_GROUPS, SSM_STATE, SSM_GROUP
    H, dh = MLSTM_HEADS, MLSTM_HEAD_DIM
    n_idx = jnp.arange(P, dtype=jnp.float32)[None, None, :]
    return {
        "x": nrm((BATCH, SEQ, D), 1.0),
        "c": nrm((BATCH, D), 1.0),
        "norm_gain": 1.0 + nrm((L, D), 0.02),
        "w_mod": nrm((L, D, 3 * D), 0.5 * D ** -0.5),
        "b_mod": nrm((L, 3 * D), 0.02),
        "w_in": nrm((L, D, IN_COLS), D ** -0.5),
        "ssm_lambda_re": -0.5 + nrm((L, G, P), 0.01),
        "ssm_lambda_im": math.pi * n_idx + nrm((L, G, P), 0.01),
        "ssm_log_dt": jax.random.uniform(nk(), (L, G), jnp.float32,
                                         minval=math.log(1e-3), maxval=math.log(1e-1)),
        "ssm_b_re": nrm((L, G, P, Cg), (2 * Cg) ** -0.5),
        "ssm_b_im": nrm((L, G, P, Cg), (2 * Cg) ** -0.5),
        "ssm_c_re": nrm((L, G, Cg, P), 0.5),
        "ssm_c_im": nrm((L, G, Cg, P), 0.5),
        "ssm_d": nrm((L, W), 1.0),
        "ssm_w_glu": nrm((L, W, W), W ** -0.5),
        "ssm_b_glu": nrm((L, W), 0.02),
        "ssm_out_gain": 1.0 + nrm((L, W), 0.02),
        "m_conv_w": nrm((L, CONV_WIDTH, MW), CONV_WIDTH ** -0.5),
        "m_conv_b": nrm((L, MW), 0.02),
        "m_wq": nrm((L, H, dh, dh), dh ** -0.5),
        "m_wk": nrm((L, H, dh, dh), dh ** -0.5),
        "m_wv": nrm((L, H, dh, dh), dh ** -0.5),
        "m_w_gates": nrm((L, 3 * MW, 2 * H), 0.1 * (3 * MW) ** -0.5),
        "m_b_igate": nrm((L, H), 0.1),
        "m_b_fgate": jnp.linspace(3.0, 6.0, H, dtype=jnp.float32)[None, :] + nrm((L, H), 0.1),
        "m_norm_gain": 1.0 + nrm((L, MW), 0.02),
        "m_skip": 1.0 + nrm((L, MW), 0.02),
        "w_out": nrm((L, MIX_WIDTH, D), MIX_WIDTH ** -0.5),
        "final_gain": 1.0 + nrm((D,), 0.02),
    }


def reference(x, c, norm_gain, w_mod, b_mod, w_in,
              ssm_lambda_re, ssm_lambda_im, ssm_log_dt, ssm_b_re, ssm_b_im, ssm_c_re, ssm_c_im,
              ssm_d, ssm_w_glu, ssm_b_glu, ssm_out_gain,
              m_conv_w, m_conv_b, m_wq, m_wk, m_wv, m_w_gates, m_b_igate, m_b_fgate,
              m_norm_gain, m_skip, w_out, final_gain):
    h = x
    for l in range(DEPTH):
        h = hybrid_layer(h, c, norm_gain[l], w_mod[l], b_mod[l], w_in[l],
                         ssm_lambda_re[l], ssm_lambda_im[l], ssm_log_dt[l], ssm_b_re[l], ssm_b_im[l],
                         ssm_c_re[l], ssm_c_im[l], ssm_d[l], ssm_w_glu[l], ssm_b_glu[l], ssm_out_gain[l],
                         m_conv_w[l], m_conv_b[l], m_wq[l], m_wk[l], m_wv[l], m_w_gates[l],
                         m_b_igate[l], m_b_fgate[l], m_norm_gain[l], m_skip[l], w_out[l])
    return rms_norm(h, final_gain).astype(x.dtype)
```

```python
import numpy as np
from contextlib import ExitStack
import concourse.bass as bass
import concourse.mybir as mybir
from concourse.bass_utils import run_bass_kernel_spmd

F32 = mybir.dt.float32
F32R = mybir.dt.float32r


def R(ap):
    return ap.bitcast(F32R)
AF = mybir.ActivationFunctionType
ALU = mybir.AluOpType

S = 2048
D = 1024
T = 256
NT = S // T
SL = 2
NS5 = max(1, SL // 2)
PPS = 32 // NS5
FPS = 8 // NS5
NB = T // SL
CH = 64
NCH = T // CH
DEPTH = 2
EPS = 1e-6
NV = 112
CHAIN_ENG = "dve"
CHAIN_STEP = 3
DEBUG = False
DBG_L = 0
DBG_T = 0
NDBG = 16
SYNC_SAME = {"pe": False, "act": True, "dve": True, "pool": False, "sp": False}


class Sched:
    def __init__(self, nc, es):
        self.nc = nc
        self.es = es
        self.eng = {"pe": nc.tensor, "act": nc.scalar, "dve": nc.vector, "pool": nc.gpsimd, "sp": nc.sync}
        self.sem = {k: es.enter_context(nc.semaphore("sem_" + k)) for k in self.eng}
        self.cnt = {k: 0 for k in self.eng}
        self.waited = {k: {} for k in self.eng}
        self.lastw = {}
        self.readers = {}
        self.streams = {}
        self.scnt = {}
        self.ninst = 0

    def _deps(self, r, w):
        deps = []
        for x in r:
            if x in self.lastw:
                deps.append(self.lastw[x])
        for x in w:
            if x in self.lastw:
                deps.append(self.lastw[x])
            deps.extend(self.readers.get(x, ()))
        return deps

    def _wait(self, en, deps, nosync=False):
        e = self.eng[en]
        best = {}
        for (sname, sem, val) in deps:
            if sname == en and (nosync or not SYNC_SAME[en]):
                continue
            if self.waited[en].get(sname, 0) >= val:
                continue
            if best.get(sname, (None, 0))[1] < val:
                best[sname] = (sem, val)
        for sname, (sem, val) in best.items():
            e.wait_ge(sem, val)
            self.waited[en][sname] = val
            self.ninst += 1

    def _record(self, dep, r, w):
        for x in w:
            self.lastw[x] = dep
            self.readers[x] = []
        for x in r:
            self.readers.setdefault(x, []).append(dep)

    def op(self, en, fn, r=(), w=(), nosync=False):
        self._wait(en, self._deps(r, w), nosync)
        inst = fn(self.eng[en])
        self.cnt[en] += 1
        inst.then_inc(self.sem[en], 1)
        self.ninst += 1
        self._record((en, self.sem[en], self.cnt[en]), r, w)

    def dma(self, out, in_, r=(), w=(), stream=None, q="sp", **kw):
        if stream not in self.streams:
            self.streams[stream] = self.es.enter_context(self.nc.semaphore("dq_" + stream))
            self.scnt[stream] = 0
        self._wait(q, self._deps(r, w))
        self.scnt[stream] += 16
        self.eng[q].dma_start(out=out, in_=in_, **kw).then_inc(self.streams[stream], 16)
        self.ninst += 1
        self._record(("dq_" + stream, self.streams[stream], self.scnt[stream]), r, w)

    def sync_stream(self, stream, resources):
        dep = ("dq_" + stream, self.streams[stream], self.scnt[stream])
        for x in resources:
            self.lastw[x] = dep

    def merge(self, dst, srcs):
        for x in srcs:
            if x in self.lastw:
                self.readers.setdefault(dst, []).append(self.lastw[x])
            self.readers.setdefault(dst, []).extend(self.readers.get(x, ()))

    def barrier(self):
        deps = list(self.lastw.values())
        for v in self.readers.values():
            deps.extend(v)
        for en in self.eng:
            self._wait(en, deps)

    def finish(self, en="sp"):
        deps = list(self.lastw.values())
        for v in self.readers.values():
            deps.extend(v)
        self._wait(en, deps)


def fm(v, n):
    return np.ascontiguousarray(np.asarray(v, np.float32).reshape(n, 128).T)


def build_program():
    nc = bass.Bass("TRN2", target_bir_lowering=False)
    L = DEPTH

    def din(name, shape):
        return nc.dram_tensor(name, list(shape), F32, kind="ExternalInput")

    x_h = din("x", [S, D])
    cT_h = din("cT", [128, 8])
    vecs_h = din("vecs", [L, 128, NV])
    fvec_h = din("fvec", [128, 8])
    gb_h = din("gb", [L, 4, 2])
    wg_h = din("wg", [L, 128, 24 * 8])
    wqkv_h = din("wqkv", [L, 128, 24 * 256])
    wmod_h = din("w_mod", [L, D, 3 * D])
    win_h = din("w_in", [L, D, 5 * D])
    wglu_h = din("w_glu", [L, D, D])
    wout_h = din("w_out", [L, 2 * D, D])
    lre_h = din("lam_re", [L, 64, 64])
    lim_h = din("lam_im", [L, 64, 64])
    ldt_h = din("log_dt", [L, 64])
    bre_h = din("b_re", [L, 64, 64, 16])
    bim_h = din("b_im", [L, 64, 64, 16])
    cre_h = din("c_re", [L, 64, 16, 64])
    cim_h = din("c_im", [L, 64, 16, 64])
    NCON = 128 + 128 + 128 + 3 * T + 64
    con_h = din("consts", [128, NCON])
    sel_h = din("sel", [4, 4 * 128])
    y_h = nc.dram_tensor("y", [S, D], F32, kind="ExternalOutput")
    dbg_h = nc.dram_tensor("dbg", [NDBG, 128, 8 * T + 32], F32, kind="ExternalOutput") if DEBUG else None
    dbgi = [0]
    dbgnames = []
    xs_h = nc.dram_tensor("xs", [128, 8 * S], F32, kind="Internal")
    winT_h = nc.dram_tensor("winT_d", [L, 128, 16 * SL * 128], F32, kind="Internal")
    woutS_h = nc.dram_tensor("woutS_d", [L, 128, 64 * SL * 32], F32, kind="Internal")

    AP = bass.AP
    with ExitStack() as es:
        sc = Sched(nc, es)
        op = sc.op

        def sb(name, shape, st=es):
            return st.enter_context(nc.sbuf_tensor("s_" + name, list(shape), F32))

        con = sb("con", [128, NCON])
        ident = con[:, 0:128]
        ones = con[:, 128:256]
        pmask = con[:, 256:384]
        mask01 = con[:, 384:384 + T]
        cmask = con[:, 384 + T:384 + 2 * T]
        negmask = con[:, 384 + 2 * T:384 + 3 * T]
        causal = con[0:64, 384 + 3 * T:384 + 3 * T + 64]
        sel = sb("sel", [4, 4 * 128])
        vecs = sb("vecs", [128, L, NV])
        fvec = sb("fvec", [128, 8])
        gb = sb("gb", [4, L, 2])
        wg = sb("wg", [128, L, 24 * 8])
        cT = sb("cT", [128, 8])
        aL = sb("aL", [128, L, 2, 32])
        sc.dma(con[:], con_h.ap(), w=["con"], stream="c0")
        sc.dma(sel[:], sel_h.ap(), w=["sel"], stream="c0")
        sc.dma(fvec[:], fvec_h.ap(), w=["fvec"], stream="c0")
        sc.dma(cT[:], cT_h.ap(), w=["cT"], stream="c0")
        for l in range(L):
            sc.dma(vecs[:, l, :], vecs_h.ap()[l], w=["vecs%d" % l], stream="c0")
            sc.dma(gb[:, l, :], gb_h.ap()[l], w=["gb%d" % l], stream="c0")
            sc.dma(wg[:, l, :], wg_h.ap()[l], w=["wg%d" % l], stream="c0")

        sc.sync_stream("c0", ["con", "sel", "fvec", "cT", "vecs", "gb", "wg"])
        psum = [es.enter_context(nc.psum_tensor("ps%d" % i, [128, 512], F32)) for i in range(8)]
        pctr = [0]

        def pnext():
            i = pctr[0] % 6
            pctr[0] += 1
            return psum[i], "ps%d" % i

        p8ctr = [0]

        def pnext8():
            i = p8ctr[0] % 8
            p8ctr[0] += 1
            return psum[i], "ps%d" % i

        with ExitStack() as ps_:
            def sp(name, shape):
                return sb(name, shape, ps_)
            lamre = sp("lamre", [128, 32]); lamim = sp("lamim", [128, 32]); dtb = sp("dtb", [128, 32])
            breT = sp("breT", [128, 32, 16]); bimT = sp("bimT", [128, 32, 16])
            cnre = sp("cnre", [128, 8, 64]); cnim = sp("cnim", [128, 8, 64])
            tq = [sp("tq%d" % i, [128, 32]) for i in range(26)]
            apw = sp("apw", [128, 5, 2, 32]); aiw = sp("aiw", [128, 5, 2, 32])
            tb = [sp("tb%d" % i, [128, 32, 16]) for i in range(4)]
            bpad = [sp("bpad%d" % i, [128, 32, 32]) for i in range(2)]
            cpad = [sp("cpad%d" % i, [128, 32, 32]) for i in range(2)]
            xm = sp("xm", [128, 128])
            tcb = [sp("tc%d" % i, [128, 32, 32]) for i in range(4)]
            stW = sp("stW", [128, 16 * SL, 128])
            stO = sp("stO", [128, 64 * SL, 32])

            def tt(o, a, b, o_, rr, ww, en="dve"):
                op(en, lambda e: e.tensor_tensor(o, a, b, o_), r=rr, w=ww)

            def ts(o, a, s1, s2, o0, o1, rr, ww, en="dve"):
                if o1 is None:
                    op(en, lambda e: e.tensor_scalar(o, a, s1, None, o0), r=rr, w=ww)
                else:
                    op(en, lambda e: e.tensor_scalar(o, a, s1, s2, o0, o1), r=rr, w=ww)

            for l in range(L):
                sc.dma(lamre[:], AP(lre_h, l * 4096, [[1, 128], [128, 32]]), w=["lamre"], stream="pc%d" % l,
                       allow_slow_non_contiguous=True)
                sc.dma(lamim[:], AP(lim_h, l * 4096, [[1, 128], [128, 32]]), w=["lamim"], stream="pc%d" % l,
                       allow_slow_non_contiguous=True)
                for gl in range(2):
                    sc.dma(dtb[gl * 64:(gl + 1) * 64, :], AP(ldt_h, l * 64 + gl, [[0, 64], [2, 32]]), w=["dtb%d" % gl],
                           stream="pc%d" % l, allow_slow_non_contiguous=True)
                sc.dma(breT[:], AP(bre_h, l * 65536, [[16, 128], [2048, 32], [1, 16]]), w=["breT"], stream="pc%d" % l)
                sc.dma(bimT[:], AP(bim_h, l * 65536, [[16, 128], [2048, 32], [1, 16]]), w=["bimT"], stream="pc%d" % l)
                sc.dma(cnre[:], AP(cre_h, l * 65536, [[64, 128], [8192, 8], [1, 64]]), w=["cnre"], stream="pc%d" % l)
                sc.dma(cnim[:], AP(cim_h, l * 65536, [[64, 128], [8192, 8], [1, 64]]), w=["cnim"], stream="pc%d" % l)
                sc.sync_stream("pc%d" % l, ["lamre", "lamim", "dtb", "breT", "bimT", "cnre", "cnim"])
                names = ["tq%d" % i for i in range(24)]
                (dt_, lr, th, r1, rinv, xx, x2, sn, cs, cc, ss, cs2, are, aim, am1, nr, ni, den, cfr, cfi, t0, t1, t2, t3) = \
                    [(tq[i][:], names[i]) for i in range(24)]

                def A(o, a, b, o_):
                    tt(o[0], a[0], b[0], o_, [a[1], b[1]], [o[1]])

                def Sx(o, a, s1, s2, o0, o1=None):
                    ts(o[0], a[0], s1, s2, o0, o1, [a[1]], [o[1]])

                LR = (lamre[:], "lamre"); LI = (lamim[:], "lamim")
                op("act", lambda e: e.activation(out=dt_[0], in_=dtb[:], func=AF.Exp), r=["dtb"], w=[dt_[1]])
                A(lr, LR, dt_, ALU.mult)
                A(th, LI, dt_, ALU.mult)
                op("act", lambda e: e.activation(out=r1[0], in_=lr[0], func=AF.Exp), r=[lr[1]], w=[r1[1]])
                op("act", lambda e: e.activation(out=rinv[0], in_=lr[0], func=AF.Exp, scale=-1.0), r=[lr[1]], w=[rinv[1]])
                Sx(xx, th, 1.0 / 32, None, ALU.mult)
                A(x2, xx, xx, ALU.mult)
                Sx(sn, x2, -1.0 / 5040, 1.0 / 120, ALU.mult, ALU.add)
                A(sn, sn, x2, ALU.mult); Sx(sn, sn, -1.0 / 6, None, ALU.add)
                A(sn, sn, x2, ALU.mult); Sx(sn, sn, 1.0, None, ALU.add)
                A(sn, sn, xx, ALU.mult)
                Sx(cs, x2, 1.0 / 40320, -1.0 / 720, ALU.mult, ALU.add)
                A(cs, cs, x2, ALU.mult); Sx(cs, cs, 1.0 / 24, None, ALU.add)
                A(cs, cs, x2, ALU.mult); Sx(cs, cs, -0.5, None, ALU.add)
                A(cs, cs, x2, ALU.mult); Sx(cs, cs, 1.0, None, ALU.add)
                for _ in range(5):
                    A(cc, cs, cs, ALU.mult); A(ss, sn, sn, ALU.mult); A(cs2, cs, sn, ALU.mult)
                    A(cs, cc, ss, ALU.subtract); Sx(sn, cs2, 2.0, None, ALU.mult)
                A(are, r1, cs, ALU.mult); A(aim, r1, sn, ALU.mult)
                Sx(am1, are, -1.0, None, ALU.add)
                A(t0, am1, LR, ALU.mult); A(t1, aim, LI, ALU.mult); A(nr, t0, t1, ALU.add)
                A(t0, aim, LR, ALU.mult); A(t1, am1, LI, ALU.mult); A(ni, t0, t1, ALU.subtract)
                A(t0, LR, LR, ALU.mult); A(t1, LI, LI, ALU.mult); A(den, t0, t1, ALU.add)
                op("dve", lambda e: e.reciprocal(den[0], den[0]), r=[den[1]], w=[den[1]])
                A(cfr, nr, den, ALU.mult); A(cfi, ni, den, ALU.mult)

                def PW(t_, k, ri):
                    return (t_[:, k, ri, :], t_.name if hasattr(t_, "name") else "pw")
                APW = lambda k, ri: (apw[:, k, ri, :], "apw")
                AIW = lambda k, ri: (aiw[:, k, ri, :], "aiw")
                op("dve", lambda e: e.tensor_copy(apw[:, 1, 0, :], are[0]), r=[are[1]], w=["apw"])
                op("dve", lambda e: e.tensor_copy(apw[:, 1, 1, :], aim[0]), r=[aim[1]], w=["apw"])
                A(AIW(1, 0), rinv, cs, ALU.mult)
                A(t0, rinv, sn, ALU.mult); Sx(AIW(1, 1), t0, -1.0, None, ALU.mult)

                def cmul(o_re, o_im, a_re, a_im, b_re, b_im):
                    A(t0, a_re, b_re, ALU.mult); A(t1, a_im, b_im, ALU.mult)
                    A(t2, a_re, b_im, ALU.mult); A(t3, a_im, b_re, ALU.mult)
                    A(o_re, t0, t1, ALU.subtract); A(o_im, t2, t3, ALU.add)
                for k in range(2, 5):
                    cmul(APW(k, 0), APW(k, 1), APW(k - 1, 0), APW(k - 1, 1), APW(1, 0), APW(1, 1))
                    cmul(AIW(k, 0), AIW(k, 1), AIW(k - 1, 0), AIW(k - 1, 1), AIW(1, 0), AIW(1, 1))
                op("dve", lambda e: e.tensor_copy(aL[:, l, :, :], apw[:, SL, :, :]), r=["apw"], w=["aL"])

                for ri, cn_ in enumerate((cnre, cnim)):
                    for fc in range(8):
                        op("dve", lambda e: e.tensor_tensor(xm[:].rearrange("p (g q) -> p g q", g=2),
                                                            cn_[:, fc, :].unsqueeze(1).to_broadcast([128, 2, 64]),
                                                            pmask.rearrange("p (g q) -> p g q", g=2), ALU.mult),
                           r=[cn_.name if False else ("cnre" if ri == 0 else "cnim"), "con"], w=["xm"])
                        pt, pn = pnext()
                        op("pe", lambda e: e.transpose(pt[:, 0:128], xm[:], ident), r=["xm", "con"], w=[pn])
                        op("act", lambda e: e.activation(
                            out=cpad[ri][:, fc * 4:(fc + 1) * 4, :].rearrange("p a b -> p (a b)"),
                            in_=pt[:, 0:128], func=AF.Copy), r=[pn], w=["cpad%d" % ri])
                for j in range(SL):
                    Gre = (tq[24][:], "tq24"); Gim = (tq[25][:], "tq25")
                    cmul(Gre, Gim, AIW(j + 1, 0), AIW(j + 1, 1), cfr, cfi)
                    gb_re = Gre[0].unsqueeze(2).to_broadcast([128, 32, 16])
                    gb_im = Gim[0].unsqueeze(2).to_broadcast([128, 32, 16])
                    tt(tb[0][:], breT[:], gb_re, ALU.mult, ["breT", Gre[1]], ["tb0"])
                    tt(tb[1][:], bimT[:], gb_im, ALU.mult, ["bimT", Gim[1]], ["tb1"])
                    tt(tb[2][:], bimT[:], gb_re, ALU.mult, ["bimT", Gre[1]], ["tb2"])
                    tt(tb[3][:], breT[:], gb_im, ALU.mult, ["breT", Gim[1]], ["tb3"])
                    for ri in range(2):
                        op("pool", lambda e: e.memset(bpad[ri][:], 0.0), w=["bpad%d" % ri])
                    for gl in range(2):
                        pr = slice(gl * 64, (gl + 1) * 64)
                        cr = slice(gl * 16, (gl + 1) * 16)
                        tt(bpad[0][pr, :, cr], tb[0][pr], tb[1][pr], ALU.subtract, ["tb0", "tb1"], ["bpad0"])
                        tt(bpad[1][pr, :, cr], tb[2][pr], tb[3][pr], ALU.add, ["tb2", "tb3"], ["bpad1"])
                    for ri in range(2):
                        for fc in range(8):
                            pt, pn = pnext()
                            op("pe", lambda e: e.transpose(
                                pt[:, 0:128], bpad[ri][:, fc * 4:(fc + 1) * 4, :].rearrange("p a b -> p (a b)"), ident),
                               r=["bpad%d" % ri, "con"], w=[pn])
                            idx = (fc * SL + j) * 2 + ri
                            op("act", lambda e: e.activation(out=stW[:, idx, :], in_=pt[:, 0:128], func=AF.Copy),
                               r=[pn], w=["stW"])
                    a_re = apw[:, j + 1, 0, :].unsqueeze(2).to_broadcast([128, 32, 32])
                    a_im = apw[:, j + 1, 1, :].unsqueeze(2).to_broadcast([128, 32, 32])
                    tt(tcb[0][:], cpad[0][:], a_re, ALU.mult, ["cpad0", "apw"], ["tc0"])
                    tt(tcb[1][:], cpad[1][:], a_im, ALU.mult, ["cpad1", "apw"], ["tc1"])
                    tt(tcb[2][:], cpad[0][:], a_im, ALU.mult, ["cpad0", "apw"], ["tc2"])
                    tt(tcb[3][:], cpad[1][:], a_re, ALU.mult, ["cpad1", "apw"], ["tc3"])
                    stOv = stO[:].rearrange("p (q j r) c -> p q j r c", j=SL, r=2)
                    tt(stOv[:, :, j, 0, :], tcb[0][:], tcb[1][:], ALU.subtract, ["tc0", "tc1"], ["stO"])
                    tt(tcb[2][:], tcb[2][:], tcb[3][:], ALU.add, ["tc2", "tc3"], ["tc2"])
                    ts(stOv[:, :, j, 1, :], tcb[2][:], -1.0, None, ALU.mult, None, ["tc2"], ["stO"])
                sc.dma(winT_h.ap()[l], stW[:].rearrange("p a b -> p (a b)"), r=["stW"], w=["d_winT%d" % l], stream="pcwa%d" % l)
                sc.dma(woutS_h.ap()[l], stO[:].rearrange("p a b -> p (a b)"), r=["stO"], w=["d_wout%d" % l], stream="pcwb%d" % l)

        sc.barrier()
        xt = sb("xt", [128, 8, T]); hT = sb("hT", [128, 8, T])
        secA = sb("secA", [128, 8, T]); secB = sb("secB", [128, 8, T]); secC = sb("secC", [128, 8, T])
        xin = secC[:].rearrange("p a b -> p (a b)").rearrange("p (b d) -> p b d", b=2)
        minT = sb("minT", [128, 8, T + 4]); mixT = sb("mixT", [128, 16, T])
        wb = [sb("wb0", [128, 4096])]
        cs = [sb("cs%d" % i, [128, 4096]) for i in range(3)]
        xprev = sb("xprev", [128, 2, 32, NB + 1])
        CT = sb("CT", [128, 8, 260])
        zl = [[sb("zl%d%d" % (a, b), [128, T]) for b in range(2)] for a in range(2)]
        tA = sb("tA", [128, T]); tB = sb("tB", [128, T]); rstd = sb("rstd", [128, T])
        qT = sb("qT", [128, 2, T]); kT = sb("kT", [128, 2, T])
        Kw = [sb("Kw%d" % i, [64, 256]) for i in range(2)]
        Va = [sb("Va%d" % i, [64, 260]) for i in range(2)]
        DTs = [sb("DT%d" % i, [64, 64]) for i in range(2)]; SwTs = [sb("SwT%d" % i, [64, 64]) for i in range(2)]; sbA = sb("sbA", [64, 260]); nd = sb("nd", [64, 260])
        hTMs = [sb("hTM%d" % i, [64, 256]) for i in range(2)]; dd = sb("dd", [64, 2])
        modv = sb("modv", [128, 24]); g1 = sb("g1", [128, 8]); csil = sb("csil", [128, 8, 2])
        ig = sb("ig", [4, T]); lf = sb("lf", [4, T]); bcs = sb("bcs", [4, T]); gg = sb("gg", [4, T]); Mx = lf
        mm = ig; rows = sb("rows", [4, 4, T]); mprev = sb("mprev", [4, NCH + 1]); nml = sb("nml", [4, NCH])
        colsS = sb("colsS", [64, 64]); decB = sb("decB", [128, 16])

        wl = []
        wstate = {"issued": 0, "released": 0, "next": 0}
        wcast = set()

        NCS = len(cs)
        cfree = [True] * NCS
        bfree = [True]
        cptr = [0]
        pptr = [0]
        slot_of = {}
        outstanding = []

        def w_top():
            cast_ids = wstate.setdefault("cast_ids", [i for i in range(len(wl)) if wl[i][3]])
            plain_ids = wstate.setdefault("plain_ids", [i for i in range(len(wl)) if not wl[i][3]])
            while cptr[0] < len(cast_ids) and cfree[cptr[0] % NCS]:
                i = cast_ids[cptr[0]]
                s_ = cptr[0] % NCS
                src, shp, rdeps, _c = wl[i]
                dst = cs[s_][:, 0:int(np.prod(shp))]
                if len(shp) == 2:
                    dst = dst.rearrange("p (a b) -> p a b", a=shp[0])
                sc.dma(R(dst), src, r=rdeps, w=["cs%d" % s_], stream="cs%d" % s_, q="pool")
                cfree[s_] = False
                slot_of[i] = ("cs", s_)
                cptr[0] += 1
            while pptr[0] < len(plain_ids) and bfree[0]:
                i = plain_ids[pptr[0]]
                src, shp, rdeps, _c = wl[i]
                dst = wb[0][:, 0:int(np.prod(shp))]
                if len(shp) == 2:
                    dst = dst.rearrange("p (a b) -> p a b", a=shp[0])
                sc.dma(dst, src, r=rdeps, w=["wb0"], stream="wb0")
                bfree[0] = False
                slot_of[i] = ("wb", 0)
                pptr[0] += 1

        def w_get():
            i = wstate["next"]
            wstate["next"] += 1
            w_top()
            assert i in slot_of, "weight not issued"
            outstanding.append(i)
            kind, s_ = slot_of[i]
            return (cs[s_], "cs%d" % s_) if kind == "cs" else (wb[0], "wb0")

        def w_rel():
            i = outstanding.pop(0)
            kind, s_ = slot_of[i]
            if kind == "cs":
                cfree[s_] = True
            else:
                bfree[0] = True
            w_top()

        def wsec(h, l, k0, nk, c0, ncol):
            t_ = h.ap()[l]
            return t_[k0 * 128:(k0 + nk) * 128, c0:c0 + ncol].rearrange("(k p) n -> p k n", p=128)

        for l in range(L):
            for i in range(6):
                wl.append((wsec(wmod_h, l, 0, 8, i * 512, 512), (8, 512), [], False))
            for t in range(NT):
                for i in range(4):
                    wl.append((wsec(win_h, l, 0, 8, i * 512, 512), (8, 512), [], True))
                for hf in range(NS5):
                    wl.append((winT_h.ap()[l][:, hf * 4096:(hf + 1) * 4096], (4096,), ["d_winT%d" % l], True))
                    wl.append((woutS_h.ap()[l][:, hf * 4096:(hf + 1) * 4096], (4096,), ["d_wout%d" % l], False))
                for i in range(4, 10):
                    wl.append((wsec(win_h, l, 0, 8, i * 512, 512), (8, 512), [], True))
                for hp in range(2):
                    wl.append((wqkv_h.ap()[l][:, hp * 3072:(hp + 1) * 3072], (3072,), [], True))
                for hf in range(NS5):
                    wl.append((woutS_h.ap()[l][:, hf * 4096:(hf + 1) * 4096], (4096,), ["d_wout%d" % l], False))
                for i in range(2):
                    wl.append((wsec(wglu_h, l, 0, 8, i * 512, 512), (8, 512), [], True))
                for hp in range(2):
                    wl.append((wqkv_h.ap()[l][:, hp * 3072:(hp + 1) * 3072], (3072,), [], True))
                for i in range(4):
                    wl.append((wsec(wout_h, l, 0, 16, i * 256, 256), (16, 256), [], True))

        def dbg(name, ap, res, l, t, n=8 * T):
            if not DEBUG or l != DBG_L or t != DBG_T:
                return
            i = dbgi[0]; dbgi[0] += 1
            dbgnames.append(name)
            sc.dma(dbg_h.ap()[i][0:ap.shape[0], 0:n], ap, r=[res], w=["d_dbg"], stream="dbg")

        chain = [None]

        def chain_adv(k):
            if chain[0] is None:
                return
            for _ in range(k):
                try:
                    next(chain[0])
                except StopIteration:
                    chain[0] = None
                    return

        def evac(i, out, in_, r, w, func=AF.Copy, force_act=False, **kw):
            if func == AF.Copy and i % 2 == 0 and not force_act:
                op("dve", lambda e: e.tensor_copy(out, in_), r=r, w=w)
            else:
                op("act", lambda e: e.activation(out=out, in_=in_, func=func, **kw), r=r, w=w)

        def rms_rstd(src, srcn, tmp8, tmp8n, scale):
            op("act", lambda e: e.activation(out=R(tmp8[:]), in_=(src if isinstance(src, bass.AP) else src[:]), func=AF.Square), r=[srcn], w=[tmp8n])
            op("pe", lambda e: e.matmul(psum[6][:, 0:T], ones, tmp8[:, 0, :], start=True, stop=False), r=["con", tmp8n], w=["ps6"])
            for kc in range(1, 8):
                op("pe", lambda e: e.matmul(psum[6][:, 0:T], ones, tmp8[:, kc, :], start=False, stop=(kc == 7)),
                   r=["con", tmp8n], w=["ps6"])
            op("act", lambda e: e.activation(out=tA[:], in_=psum[6][:, 0:T], func=AF.Sqrt, scale=scale, bias=EPS),
               r=["ps6"], w=["tA"])
            op("dve", lambda e: e.reciprocal(rstd[:], tA[:]), r=["tA"], w=["rstd"])

        for l in range(L):
            V = lambda c0, n=1: vecs[:, l, c0:c0 + n]
            op("act", lambda e: e.activation(out=csil[:, :, 0], in_=cT[:], func=AF.Silu), r=["cT"], w=["csil"])
            op("act", lambda e: e.activation(out=csil[:, :, 1], in_=cT[:], func=AF.Silu), r=["cT"], w=["csil"])
            for i in range(6):
                wt, wn = w_get()
                wv = wt[:, 0:4096].rearrange("p (a b) -> p a b", a=8)
                for nl in range(4):
                    n = i * 4 + nl
                    for kc in range(8):
                        op("pe", lambda e: e.matmul(psum[7][:, 2 * n:2 * n + 2], wv[:, kc, nl * 128:(nl + 1) * 128],
                                                    csil[:, kc, :], start=(kc == 0), stop=(kc == 7)),
                           r=[wn, "csil"], w=["ps7"])
                w_rel()
            op("dve", lambda e: e.tensor_tensor(modv[:], psum[7][:, 0:48:2], V(8, 24), ALU.add), r=["ps7", "vecs"], w=["modv"])
            op("dve", lambda e: e.tensor_scalar(g1[:], modv[:, 8:16], 1.0, None, ALU.add), r=["modv"], w=["g1"])
            op("dve", lambda e: e.tensor_tensor(g1[:], g1[:], V(0, 8), ALU.mult), r=["g1", "vecs"], w=["g1"])
            op("pool", lambda e: e.memset(xprev[:], 0.0), w=["xprev"])
            op("pool", lambda e: e.memset(CT[:], 0.0), w=["CT"])
            op("pool", lambda e: e.tensor_copy(R(minT[:, :, 0:3]), xprev[:, 0, 0:8, 0:3]), r=["xprev"], w=["minT"])
            op("pool", lambda e: e.memset(mprev[:], 0.0), w=["mprev"])
            for i in range(2):
                op("act", lambda e: e.activation(out=R(Va[i][:, 256:260]), in_=con[0:64, 128:132], func=AF.Identity, scale=0.0, bias=1.0),
                   r=["con"], w=["Va%d" % i])

            for t in range(NT):
                t0 = t * T
                if l == 0:
                    sc.dma(xin, x_h.ap()[t0:t0 + T, :].rearrange("(b p) d -> p b d", p=128), w=["secC"], stream="xin")
                    for kc in range(8):
                        pt, pn = pnext()
                        for b in range(2):
                            op("pe", lambda e: e.transpose(pt[:, b * 128:(b + 1) * 128], xin[:, b, kc * 128:(kc + 1) * 128], ident),
                               r=["secC", "con"], w=[pn])
                        evac(kc, xt[:, kc, :], pt[:, 0:T], [pn], ["xt"])
                else:
                    sc.dma(xt[:], xs_h.ap().rearrange("p (k s) -> p k s", k=8)[:, :, t0:t0 + T], r=["d_xs"], w=["xt"], stream="xt")
                rms_rstd(xt, "xt", secB, "secB", 1.0 / D)
                for kc in range(8):
                    op("dve", lambda e: e.scalar_tensor_tensor(tB[:], xt[:, kc, :], g1[:, kc:kc + 1], rstd[:], ALU.mult, ALU.mult),
                       r=["xt", "g1", "rstd"], w=["tB"])
                    op("act", lambda e: e.activation(out=R(hT[:, kc, :]), in_=tB[:], func=AF.Identity, bias=modv[:, kc:kc + 1]),
                       r=["tB", "modv"], w=["hT"])

                def proj(dst_fn, dstn, func, **kw):
                    for hf in range(2):
                        wt, wn = w_get()
                        wv = wt[:, 0:4096].rearrange("p (a b) -> p a b", a=8)
                        for mc in range(4):
                            pt, pn = pnext()
                            for kc in range(8):
                                op("pe", lambda e: e.matmul(pt[:, 0:T], R(wv[:, kc, mc * 128:(mc + 1) * 128]), R(hT[:, kc, :]),
                                                            start=(kc == 0), stop=(kc == 7)), r=[wn, "hT"], w=[pn])
                            evac(mc, dst_fn(hf * 4 + mc), pt[:, 0:T], [pn], (dstn if isinstance(dstn, list) else [dstn]), func=func, **kw)
                            chain_adv(CHAIN_STEP)
                        w_rel()

                dbg("hT", hT[:].rearrange("p a b -> p (a b)"), "hT", l, t)
                proj(lambda n: R(secA[:, n, :]), "secA", AF.Copy)
                proj(lambda n: R(mixT[:, n, :]), "mixT", AF.Silu)

                assert NS5 == 1 and SL == 2
                wi = w_get(); wo = w_get()
                wiv = wi[0][:, 0:4096].rearrange("p (a b) -> p a b", b=128)
                wov = wo[0][:, 0:4096].rearrange("p (a b) -> p a b", b=32)

                def s5_in(q):
                    fc, qq = q // 4, q % 4
                    pr = slice(32 * qq, 32 * qq + 32)
                    pt, pn = pnext()
                    for ri in range(2):
                        for j in range(SL):
                            idx = (fc * SL + j) * 2 + ri
                            op("pe", lambda e: e.matmul(pt[:, (ri * SL + j) * NB:(ri * SL + j + 1) * NB], R(wiv[pr, idx, :]), R(secA[pr, fc, j:T:SL]),
                                                        start=True, stop=True, tile_position=(32 * qq, 0)), r=[wi[1], "secA"], w=[pn])
                    zz = zl[q % 2]
                    for ri in range(2):
                        zn = "zl%d%d" % (q % 2, ri)
                        op("dve", lambda e: e.tensor_copy(zz[ri][:, 0:NB], pt[:, (ri * SL) * NB:(ri * SL + 1) * NB]), r=[pn], w=[zn])
                        op("dve", lambda e: e.tensor_tensor(zz[ri][:, NB:2 * NB], zz[ri][:, 0:NB], pt[:, (ri * SL + 1) * NB:(ri * SL + 2) * NB], ALU.add),
                           r=[pn, zn], w=[zn])
                        op("act", lambda e: e.activation(out=xprev[:, ri, q, 1:NB + 1], in_=zz[ri][:, NB:2 * NB], func=AF.Copy), r=[zn], w=["xprev"])

                def s5_out(q):
                    fc, qq = q // 4, q % 4
                    pr = slice(32 * qq, 32 * qq + 32)
                    zz = zl[q % 2]
                    yb, ybn = psum[6 + (fc % 2)], "ps%d" % (6 + (fc % 2))
                    for j in range(SL):
                        for ri in range(2):
                            idx = (q * SL + j) * 2 + ri
                            op("pe", lambda e: e.matmul(yb[pr, j:T:SL], wov[:, idx, :], zz[ri][:, j * NB:(j + 1) * NB], start=(ri == 0), stop=(ri == 1), tile_position=(0, 32 * qq)),
                               r=[wo[1], "zl%d%d" % (q % 2, ri)], w=[ybn])
                    if qq == 3:
                        op("dve", lambda e: e.scalar_tensor_tensor(R(secB[:, fc, :]), secA[:, fc, :], V(32 + fc), yb[:, 0:T], ALU.mult, ALU.add),
                           r=["secA", "vecs", ybn], w=["secB"])

                s5_in(0)
                for q in range(1, 32):
                    s5_in(q)
                    s5_out(q - 1)
                s5_out(31)
                w_rel(); w_rel()
                dbg("u", secA[:].rearrange("p a b -> p (a b)"), "secA", l, t)
                dbg("y1", secB[:].rearrange("p a b -> p (a b)"), "secB", l, t)
                sB = tA[:, 0:64].rearrange("p (r q) -> p r q", r=2)
                p1 = tA[:, 64:128].rearrange("p (r q) -> p r q", r=2)
                p2 = tA[:, 128:192].rearrange("p (r q) -> p r q", r=2)
                alr = aL[:, l, 0, :].unsqueeze(1).to_broadcast([128, 2, 32])
                ali = aL[:, l, 1, :].unsqueeze(1).to_broadcast([128, 2, 32])

                def chain_gen():
                    for m in range(NB):
                        ns = (m > 0)
                        op(CHAIN_ENG, lambda e: e.tensor_tensor(sB, xprev[:, :, :, m], xprev[:, :, :, m + 1], ALU.add), r=["xprev"], w=["tA"], nosync=ns)
                        op(CHAIN_ENG, lambda e: e.tensor_tensor(p1, sB, alr, ALU.mult), r=["tA", "aL"], w=["tA"], nosync=True)
                        op(CHAIN_ENG, lambda e: e.tensor_tensor(p2, sB, ali, ALU.mult), r=["tA", "aL"], w=["tA"], nosync=True)
                        op(CHAIN_ENG, lambda e: e.tensor_tensor(xprev[:, 0, :, m + 1], p1[:, 0, :], p2[:, 1, :], ALU.subtract), r=["tA"], w=["xprev"], nosync=True)
                        op(CHAIN_ENG, lambda e: e.tensor_tensor(xprev[:, 1, :, m + 1], p1[:, 1, :], p2[:, 0, :], ALU.add), r=["tA"], w=["xprev"], nosync=True)
                        yield

                chain[0] = chain_gen()
                proj(lambda n: R(minT[:, n, 3:3 + T]), "minT", AF.Copy, force_act=True)
                for n in range(8):
                    op("pool", lambda e: e.tensor_scalar(tB[:], minT[:, n, 0:T], V(56 + n), None, ALU.mult), r=["minT", "vecs"], w=["tB"])
                    for w_ in range(1, 4):
                        op("pool", lambda e: e.tensor_scalar(rstd[:], minT[:, n, w_:w_ + T], V(56 + w_ * 8 + n), None, ALU.mult), r=["minT", "vecs"], w=["rstd"])
                        op("pool", lambda e: e.tensor_tensor(tB[:], tB[:], rstd[:], ALU.add), r=["tB", "rstd"], w=["tB"])
                    op("act", lambda e: e.activation(out=R(secA[:, n, :]), in_=tB[:], func=AF.Silu, bias=V(88 + n)), r=["tB", "vecs"], w=["secA"])
                    chain_adv(CHAIN_STEP)
                dbg("xc", secA[:].rearrange("p a b -> p (a b)"), "secA", l, t)
                dbg("minT", minT[:].rearrange("p a b -> p (a b)"), "minT", l, t, 8 * (T + 4))
                proj(lambda n: R(mixT[:, 8 + n, :]), ["mixTh0", "mixTh1", "mixTh2", "mixTh3"], AF.Sigmoid)
                proj(lambda n: secC[:, n, :], "secC", AF.Silu)
                sc.merge("qT0", ["qT"]); sc.merge("qT1", ["qT"])
                groups = [(h, which, mc) for h in range(4) for which in range(3) for mc in range(2)]
                wslot = [None]

                def p1_mm(g):
                    h, which, mc = groups[g]
                    hl = h % 2
                    if g % 12 == 0:
                        if g:
                            w_rel()
                        wslot[0] = w_get()
                    wt, wn = wslot[0]
                    wv = wt[:, 0:3072].rearrange("p (a b) -> p a b", b=256)
                    pt, pn = pnext()
                    for kc in range(2):
                        src = secA[:, 2 * h + kc, :] if which < 2 else minT[:, 2 * h + kc, 3:3 + T]
                        op("pe", lambda e: e.matmul(pt[:, 0:T], R(wv[:, (hl * 3 + which) * 2 + kc, mc * 128:(mc + 1) * 128]), R(src),
                                                    start=(kc == 0), stop=(kc == 1)), r=[wn, "secA", "minT"], w=[pn])
                    qb = qT[:, g % 2, :]; qbn = "qT%d" % (g % 2)
                    op("act", lambda e: e.activation(out=R(qb), in_=pt[:, 0:T], func=AF.Copy, scale=(1.0 / 16 if which == 1 else 1.0)), r=[pn], w=[qbn])

                def p1_gates(g):
                    h, which, mc = groups[g]
                    qb = qT[:, g % 2, :]; qbn = "qT%d" % (g % 2)
                    chunk = which * 8 + 2 * h + mc
                    for gi in range(2):
                        op("pe", lambda e: e.matmul(psum[6 + gi][0:4, 0:T], wg[:, l, chunk * 8 + gi * 4:chunk * 8 + gi * 4 + 4],
                                                    qb, start=(g == 0), stop=(g == 23)),
                           r=["wg", qbn], w=["ps%d" % (6 + gi)])

                p1_mm(0)
                for g in range(1, 24):
                    p1_mm(g)
                    p1_gates(g - 1)
                    chain_adv(CHAIN_STEP)
                p1_gates(23)
                w_rel()
                sc.merge("qT", ["qT0", "qT1"])
                op("act", lambda e: e.activation(out=ig[:], in_=psum[6][0:4, 0:T], func=AF.Identity, bias=gb[:, l, 0:1]), r=["ps6", "gb"], w=["ig"])
                op("act", lambda e: e.activation(out=lf[:], in_=psum[7][0:4, 0:T], func=AF.Identity, bias=gb[:, l, 1:2]), r=["ps7", "gb"], w=["lf"])
                op("act", lambda e: e.activation(out=lf[:], in_=lf[:], func=AF.Exp, scale=-1.0), r=["lf"], w=["lf"])
                op("act", lambda e: e.activation(out=lf[:], in_=lf[:], func=AF.Ln, bias=1.0), r=["lf"], w=["lf"])
                op("dve", lambda e: e.tensor_scalar(lf[:], lf[:], -1.0, None, ALU.mult), r=["lf"], w=["lf"])
                op("dve", lambda e: e.tensor_tensor_scan(bcs[:], cmask[0:4, :], lf[:], 0.0, ALU.mult, ALU.add), r=["lf", "con"], w=["bcs"])
                op("dve", lambda e: e.tensor_tensor(gg[:], ig[:], bcs[:], ALU.subtract), r=["ig", "bcs"], w=["gg"])
                op("dve", lambda e: e.tensor_tensor_scan(Mx[:], negmask[0:4, :], gg[:], -1e30, ALU.add, ALU.max), r=["gg", "con"], w=["lf"])
                for c in range(NCH):
                    cs_ = slice(c * CH, (c + 1) * CH)
                    op("dve", lambda e: e.tensor_scalar(mm[:, cs_], Mx[:, cs_], mprev[:, c:c + 1], None, ALU.max), r=["lf", "mprev"], w=["ig"])
                    op("dve", lambda e: e.tensor_tensor(mprev[:, c + 1:c + 2], bcs[:, c * CH + CH - 1:c * CH + CH], mm[:, c * CH + CH - 1:c * CH + CH], ALU.add),
                       r=["bcs", "ig"], w=["mprev"])
                    op("dve", lambda e: e.tensor_scalar(nml[:, c:c + 1], mm[:, c * CH + CH - 1:c * CH + CH], -1.0, -2.772588722239781, ALU.mult, ALU.add), r=["ig"], w=["nml"])
                    op("act", lambda e: e.activation(out=rows[:, 1, cs_], in_=gg[:, cs_], func=AF.Exp, bias=nml[:, c:c + 1]), r=["gg", "nml"], w=["rows"])
                    op("act", lambda e: e.activation(out=rows[:, 2, cs_], in_=mm[:, cs_], func=AF.Exp, scale=-1.0, bias=mprev[:, c:c + 1]),
                       r=["ig", "mprev"], w=["rows"])
                op("dve", lambda e: e.tensor_copy(rows[:, 0, :], gg[:]), r=["gg"], w=["rows"])
                op("dve", lambda e: e.tensor_tensor(rows[:, 3, :], bcs[:], mm[:], ALU.add), r=["bcs", "ig"], w=["rows"])
                op("act", lambda e: e.activation(out=rows[:, 3, :], in_=rows[:, 3, :], func=AF.Exp, scale=-1.0), r=["rows"], w=["rows"])
                op("dve", lambda e: e.tensor_copy(mprev[:, 0:1], mprev[:, NCH:NCH + 1]), r=["mprev"], w=["mprev"])
                dbg("rows", rows[:].rearrange("p a b -> p (a b)"), "rows", l, t, 4 * T)
                dbg("ig", ig[:], "ig", l, t, T)
                dbg("lf", bcs[:], "bcs", l, t, T)
                dbg("ig", mm[:], "ig", l, t, T)
                for c in range(NCH):
                    for a in range(4):
                        op("pe", lambda e: e.transpose(psum[7][0:64, T + (c * 4 + a) * 4:T + (c * 4 + a) * 4 + 4],
                                                       rows[:, a, c * CH:(c + 1) * CH], ident[0:4, 0:4]), r=["rows", "con"], w=["ps7"])
                op("dve", lambda e: e.tensor_copy(colsS[:], psum[7][0:64, T:T + 64]), r=["ps7"], w=["colsS"])
                for h in range(4):
                    op("pe", lambda e: e.matmul(psum[7][:, T + 64 + h * 4:T + 64 + h * 4 + 4], sel[:, h * 128:(h + 1) * 128],
                                                rows[:, 2, CH - 1:T:CH], start=True, stop=True), r=["sel", "rows"], w=["ps7"])
                op("dve", lambda e: e.tensor_copy(decB[:], psum[7][:, T + 64:T + 80]), r=["ps7"], w=["decB"])
                chain_adv(NB)
                for fc in range(8):
                    if fc % FPS == 0:
                        if fc:
                            w_rel()
                        wo = w_get()
                        wov = wo[0][:, 0:4096].rearrange("p (a b) -> p a b", b=32)
                    pt, pn = pnext()
                    for qq in range(4):
                        q = fc * 4 + qq
                        pr = slice(32 * qq, 32 * qq + 32)
                        for j in range(SL):
                            for ri in range(2):
                                idx = ((q % PPS) * SL + j) * 2 + ri
                                op("pe", lambda e: e.matmul(pt[pr, j:T:SL], wov[:, idx, :], xprev[:, ri, q, 0:NB], start=(ri == 0), stop=(ri == 1), tile_position=(0, 32 * qq)),
                                   r=[wo[1], "xprev"], w=[pn])
                    yv = secB[:, fc, :]
                    op("dve", lambda e: e.tensor_tensor(R(yv), yv, pt[:, 0:T], ALU.add), r=["secB", pn], w=["secB"])
                    op("act", lambda e: e.activation(out=tB[:], in_=yv, func=AF.Square), r=["secB"], w=["tB"])
                    op("dve", lambda e: e.tensor_scalar(tB[:], tB[:], 0.044715, 1.0, ALU.mult, ALU.add), r=["tB"], w=["tB"])
                    op("dve", lambda e: e.tensor_tensor(tB[:], tB[:], yv, ALU.mult), r=["tB", "secB"], w=["tB"])
                    op("act", lambda e: e.activation(out=tB[:], in_=tB[:], func=AF.Sigmoid, scale=1.5957691216057308), r=["tB"], w=["tB"])
                    op("dve", lambda e: e.tensor_tensor(R(yv), yv, tB[:], ALU.mult), r=["tB", "secB"], w=["secB"])
                w_rel()
                op("pool", lambda e: e.tensor_copy(xprev[:, :, :, 0], xprev[:, :, :, NB]), r=["xprev"], w=["xprev"])
                dbg("gelu", secB[:].rearrange("p a b -> p (a b)"), "secB", l, t)
                for hf in range(2):
                    wt, wn = w_get()
                    wv = wt[:, 0:4096].rearrange("p (a b) -> p a b", a=8)
                    for mc in range(4):
                        n = hf * 4 + mc
                        pt, pn = pnext()
                        for kc in range(8):
                            op("pe", lambda e: e.matmul(pt[:, 0:T], R(wv[:, kc, mc * 128:(mc + 1) * 128]), R(secB[:, kc, :]),
                                                        start=(kc == 0), stop=(kc == 7)), r=[wn, "secB"], w=[pn])
                        op("act", lambda e: e.activation(out=tB[:], in_=pt[:, 0:T], func=AF.Sigmoid, bias=V(40 + n)), r=[pn, "vecs"], w=["tB"])
                        op("dve", lambda e: e.tensor_tensor(R(hT[:, n, :]), secB[:, n, :], tB[:], ALU.mult), r=["secB", "tB"], w=["hT"])
                    w_rel()
                dbg("glu", hT[:].rearrange("p a b -> p (a b)"), "hT", l, t)
                rms_rstd(hT, "hT", secB, "secB", 1.0 / D)
                for n in range(8):
                    op("dve", lambda e: e.scalar_tensor_tensor(tB[:], hT[:, n, :], V(48 + n), rstd[:], ALU.mult, ALU.mult),
                       r=["hT", "vecs", "rstd"], w=["tB"])
                    op("dve", lambda e: e.tensor_tensor(R(mixT[:, n, :]), tB[:], mixT[:, n, :], ALU.mult), r=["tB", "mixT"], w=["mixT"])

                dbg("mix0", mixT[:, 0:8, :].rearrange("p a b -> p (a b)"), "mixT", l, t)
                def head_ln(h):
                    ho = hT[:, 2 * h:2 * h + 2, :]
                    sq = secB[:, 0:2, :]
                    k0 = secB[:, 2, :]; k1 = secB[:, 3, :]
                    op("act", lambda e: e.activation(out=R(sq), in_=ho, func=AF.Square), r=["hT"], w=["secB"])
                    for mc in range(2):
                        op("pe", lambda e: e.matmul(psum[6][:, 0:T], ones, hT[:, 2 * h + mc, :], start=(mc == 0), stop=(mc == 1)), r=["con", "hT"], w=["ps6"])
                    for mc in range(2):
                        op("pe", lambda e: e.matmul(psum[6][:, T:2 * T], ones, sq[:, mc, :], start=(mc == 0), stop=(mc == 1), skip_group_check=True),
                           r=["con", "secB"], w=["ps6"])
                    op("act", lambda e: e.activation(out=tA[:], in_=psum[6][:, 0:T], func=AF.Copy, scale=1.0 / 256), r=["ps6"], w=["tA"])
                    op("dve", lambda e: e.tensor_tensor(tB[:], tA[:], tA[:], ALU.mult), r=["tA"], w=["tB"])
                    op("dve", lambda e: e.scalar_tensor_tensor(tB[:], psum[6][:, T:2 * T], 1.0 / 256, tB[:], ALU.mult, ALU.subtract), r=["ps6", "tB"], w=["tB"])
                    op("act", lambda e: e.activation(out=tB[:], in_=tB[:], func=AF.Sqrt, bias=EPS), r=["tB"], w=["tB"])
                    op("dve", lambda e: e.reciprocal(rstd[:], tB[:]), r=["tB"], w=["rstd"])
                    for mc in range(2):
                        n = 2 * h + mc
                        op("dve", lambda e: e.tensor_tensor(R(k0), hT[:, n, :], tA[:], ALU.subtract), r=["hT", "tA"], w=["secB"])
                        op("dve", lambda e: e.scalar_tensor_tensor(R(k0), k0, V(96 + n), rstd[:], ALU.mult, ALU.mult), r=["secB", "vecs", "rstd"], w=["secB"])
                        op("dve", lambda e: e.scalar_tensor_tensor(R(k1), secA[:, n, :], V(104 + n), k0, ALU.mult, ALU.add), r=["secA", "vecs", "secB"], w=["secB"])
                        op("dve", lambda e: e.tensor_tensor(R(mixT[:, 8 + n, :]), k1, secC[:, n, :], ALU.mult), r=["secB", "secC"], w=["mixTh%d" % h])

                for hp in range(2):
                    wt, wn = w_get()
                    wv = wt[:, 0:3072].rearrange("p (a b) -> p a b", b=256)
                    for hl in range(2):
                        h = hp * 2 + hl
                        for which, dstT in ((0, qT), (1, kT)):
                            for mc in range(2):
                                pt, pn = pnext()
                                for kc in range(2):
                                    op("pe", lambda e: e.matmul(pt[:, 0:T], R(wv[:, (hl * 3 + which) * 2 + kc, mc * 128:(mc + 1) * 128]), R(secA[:, 2 * h + kc, :]),
                                                                start=(kc == 0), stop=(kc == 1)), r=[wn, "secA"], w=[pn])
                                dn = "qT" if which == 0 else "kT"
                                op("act", lambda e: e.activation(out=R(dstT[:, mc, :]), in_=pt[:, 0:T], func=AF.Copy, scale=(1.0 if which == 0 else 1.0 / 16)),
                                   r=[pn], w=[dn])
                        def col(c, a):
                            return colsS[:, (c * 4 + a) * 4 + h:(c * 4 + a) * 4 + h + 1]

                        def cellA(c):
                            cs_ = slice(c * CH, (c + 1) * CH)
                            bi = c % 2
                            pt, pn = pnext8()
                            for kc in range(2):
                                op("pe", lambda e: e.matmul(pt[0:64, 0:256], R(secA[:, 2 * h + kc, cs_]), R(wv[:, (hl * 3 + 1) * 2 + kc, :]), start=(kc == 0), stop=(kc == 1)),
                                   r=[wn, "secA"], w=[pn])
                            op("act", lambda e: e.activation(out=R(Kw[bi][:]), in_=pt[0:64, 0:256], func=AF.Copy, scale=col(c, 1)), r=[pn, "colsS"], w=["Kw%d" % bi])
                            pt, pn = pnext8()
                            for kc in range(2):
                                op("pe", lambda e: e.matmul(pt[0:64, 0:256], R(minT[:, 2 * h + kc, 3 + c * CH:3 + (c + 1) * CH]), R(wv[:, (hl * 3 + 2) * 2 + kc, :]),
                                                            start=(kc == 0), stop=(kc == 1)), r=[wn, "minT"], w=[pn])
                            op("act", lambda e: e.activation(out=R(Va[bi][:, 0:256]), in_=pt[0:64, 0:256], func=AF.Copy), r=[pn], w=["Va%d" % bi])
                            pt, pn = pnext8()
                            for mc in range(2):
                                op("pe", lambda e: e.matmul(pt[0:64, 0:64], R(kT[:, mc, cs_]), R(qT[:, mc, cs_]), start=(mc == 0), stop=(mc == 1)), r=["kT", "qT"], w=[pn])
                            op("pe", lambda e: e.matmul(pt[0:64, 64:128], sel[:, h * 128:h * 128 + 64], mm[:, cs_], start=True, stop=False), r=["sel", "ig"], w=[pn])
                            op("pe", lambda e: e.matmul(pt[0:64, 64:128], ident[0:64, 0:64], causal, start=False, stop=True), r=["con"], w=[pn])
                            op("act", lambda e: e.activation(out=DTs[bi][:], in_=pt[0:64, 64:128], func=AF.Exp, scale=-1.0, bias=col(c, 0)), r=[pn, "colsS"], w=["DT%d" % bi])
                            op("dve", lambda e: e.tensor_tensor(R(SwTs[bi][:]), DTs[bi][:], pt[0:64, 0:64], ALU.mult), r=["DT%d" % bi, pn], w=["SwT%d" % bi])

                        def cellB(c):
                            cs_ = slice(c * CH, (c + 1) * CH)
                            bi = c % 2
                            pi_, pin = pnext8()
                            for mc in range(2):
                                op("pe", lambda e: e.matmul(pi_[0:64, 0:257], qT[:, mc, cs_], CT[:, 2 * h + mc, 0:257], start=(mc == 0), stop=(mc == 1)),
                                   r=["qT", "CT"], w=[pin])
                            pus = []
                            for mc in range(2):
                                pu, pun = pnext8()
                                op("pe", lambda e: e.matmul(pu[:, 0:258], R(Kw[bi][:, mc * 128:(mc + 1) * 128]), R(Va[bi][:, 0:258]), start=True, stop=True),
                                   r=["Kw%d" % bi, "Va%d" % bi], w=[pun])
                                pus.append((pu, pun))
                            pa, pan = pnext8()
                            op("pe", lambda e: e.matmul(pa[0:64, 0:258], R(SwTs[bi][:]), R(Va[bi][:, 0:258]), start=True, stop=True), r=["SwT%d" % bi, "Va%d" % bi], w=[pan])
                            op("act", lambda e: e.activation(out=sbA[:, 0:257], in_=pa[0:64, 0:257], func=AF.Copy), r=[pan], w=["sbA"])
                            op("dve", lambda e: e.scalar_tensor_tensor(nd[:, 0:257], pi_[0:64, 0:257], col(c, 2), sbA[:, 0:257], ALU.mult, ALU.add),
                               r=[pin, "colsS", "sbA"], w=["nd"])
                            for mc in range(2):
                                pu, pun = pus[mc]
                                op("dve", lambda e: e.scalar_tensor_tensor(CT[:, 2 * h + mc, 0:257], CT[:, 2 * h + mc, 0:257], decB[:, h * 4 + c:h * 4 + c + 1],
                                                                           pu[:, 0:257], ALU.mult, ALU.add), r=["CT", "decB", pun], w=["CT"])
                            op("act", lambda e: e.activation(out=dd[:, 1:2], in_=nd[:, 256:257], func=AF.Copy, scale=-1.0), r=["nd"], w=["dd"])
                            op("dve", lambda e: e.tensor_scalar(dd[:, 0:1], nd[:, 256:257], col(c, 3), dd[:, 1:2], ALU.max, ALU.max), r=["nd", "dd", "colsS"], w=["dd"])
                            op("dve", lambda e: e.reciprocal(dd[:, 1:2], dd[:, 0:1]), r=["dd"], w=["dd"])
                            op("act", lambda e: e.activation(out=hTMs[bi][:], in_=nd[:, 0:256], func=AF.Copy, scale=dd[:, 1:2]), r=["nd", "dd"], w=["hTM%d" % bi])

                        def cellC(c):
                            cs_ = slice(c * CH, (c + 1) * CH)
                            bi = c % 2
                            pt, pn = pnext8()
                            for mc in range(2):
                                op("pe", lambda e: e.transpose(pt[:, mc * 64:(mc + 1) * 64], hTMs[bi][:, mc * 128:(mc + 1) * 128], ident[0:64, 0:64]), r=["hTM%d" % bi, "con"], w=[pn])
                            for mc in range(2):
                                op("dve", lambda e: e.tensor_tensor(R(hT[:, 2 * h + mc, cs_]), pt[:, mc * 64:(mc + 1) * 64], mixT[:, 8 + 2 * h + mc, cs_], ALU.mult),
                                   r=[pn, "mixTh%d" % h], w=["hT"])

                        cellA(0)
                        for c in range(NCH):
                            if c + 1 < NCH:
                                cellA(c + 1)
                            cellB(c)
                            if c >= 1:
                                cellC(c - 1)
                        cellC(NCH - 1)
                        head_ln(h)
                    w_rel()
                dbg("ho", hT[:].rearrange("p a b -> p (a b)"), "hT", l, t)
                op("pool", lambda e: e.tensor_copy(R(minT[:, :, 0:3]), minT[:, :, T:T + 3]), r=["minT"], w=["minT"])
                dbg("mix1", mixT[:, 8:16, :].rearrange("p a b -> p (a b)"), "mixTh3", l, t)
                for i in range(4):
                    wt, wn = w_get()
                    wv = wt[:, 0:4096].rearrange("p (a b) -> p a b", a=16)
                    for ml in range(2):
                        n = i * 2 + ml
                        pt, pn = pnext()
                        for kc in range(16):
                            op("pe", lambda e: e.matmul(pt[:, 0:T], R(wv[:, kc, ml * 128:(ml + 1) * 128]), R(mixT[:, kc, :]), start=(kc == 0), stop=(kc == 15)),
                               r=[wn, "mixT", "mixTh0", "mixTh1", "mixTh2", "mixTh3"], w=[pn])
                        op("dve", lambda e: e.scalar_tensor_tensor(xt[:, n, :], pt[:, 0:T], modv[:, 16 + n:17 + n], xt[:, n, :], ALU.mult, ALU.add),
                           r=[pn, "modv", "xt"], w=["xt"])
                    w_rel()
                dbg("xo", xt[:].rearrange("p a b -> p (a b)"), "xt", l, t)
                if l < L - 1:
                    sc.dma(xs_h.ap().rearrange("p (k s) -> p k s", k=8)[:, :, t0:t0 + T], xt[:], r=["xt"], w=["d_xs"], stream="xt")
                else:
                    rms_rstd(xt, "xt", secB, "secB", 1.0 / D)
                    for n in range(8):
                        op("dve", lambda e: e.scalar_tensor_tensor(R(secB[:, n, :]), xt[:, n, :], fvec[:, n:n + 1], rstd[:], ALU.mult, ALU.mult),
                           r=["xt", "fvec", "rstd"], w=["secB"])
                    for b in range(2):
                        for k4 in range(2):
                            pt, pn = pnext()
                            for kk in range(4):
                                kc = k4 * 4 + kk
                                op("pe", lambda e: e.transpose(pt[:, kk * 128:(kk + 1) * 128], secB[:, kc, b * 128:(b + 1) * 128], ident), r=["secB", "con"], w=[pn])
                            evac(k4, xin[:, b, k4 * 512:(k4 + 1) * 512], pt[:, 0:512], [pn], ["secC"])
                    sc.dma(y_h.ap()[t0:t0 + T, :].rearrange("(b p) d -> p b d", p=128), xin, r=["secC"], w=["d_y"], stream="xin")
        sc.finish("sp")
        print("instructions:", sc.ninst, sc.cnt)
    nc._dbgnames = dbgnames
    return nc


_CACHE = {}


def _consts():
    con = np.zeros((128, 128 + 128 + 128 + 3 * T + 64), np.float32)
    con[:, 0:128] = np.eye(128, dtype=np.float32)
    con[:, 128:256] = 1.0
    g8 = (np.arange(128) // 16) % 2
    gl = np.arange(128) // 64
    con[:, 256:384] = (g8[:, None] == gl[None, :]).astype(np.float32)
    tt_ = np.arange(T)
    con[:, 384:384 + T] = (tt_ % SL != 0).astype(np.float32)[None, :]
    con[:, 384 + T:384 + 2 * T] = (tt_ % CH != 0).astype(np.float32)[None, :]
    con[:, 384 + 2 * T:384 + 3 * T] = np.where(tt_ % CH == 0, -1e30, 0.0).astype(np.float32)[None, :]
    s_ = np.arange(64)
    con[0:64, 384 + 3 * T:384 + 3 * T + 64] = np.where(s_[None, :] >= s_[:, None], 0.0, 1e30).astype(np.float32)
    sel = np.zeros((4, 4, 128), np.float32)
    for h in range(4):
        sel[h, h, :] = 1.0
    return con, sel.reshape(4, 512)


def kernel(**inp):
    f = lambda k: np.asarray(inp[k], np.float32)
    x = f("x"); c = f("c")
    L = DEPTH
    vecs = np.zeros((L, 128, NV), np.float32)
    for l in range(L):
        vecs[l, :, 0:8] = fm(f("norm_gain")[l], 8)
        vecs[l, :, 8:32] = fm(f("b_mod")[l], 24)
        vecs[l, :, 32:40] = fm(f("ssm_d")[l], 8)
        vecs[l, :, 40:48] = fm(f("ssm_b_glu")[l], 8)
        vecs[l, :, 48:56] = fm(f("ssm_out_gain")[l], 8)
        for w in range(4):
            vecs[l, :, 56 + w * 8:64 + w * 8] = fm(f("m_conv_w")[l, w], 8)
        vecs[l, :, 88:96] = fm(f("m_conv_b")[l], 8)
        vecs[l, :, 96:104] = fm(f("m_norm_gain")[l], 8)
        vecs[l, :, 104:112] = fm(f("m_skip")[l], 8)
    fvec = fm(f("final_gain"), 8)
    gb = np.stack([f("m_b_igate"), f("m_b_fgate")], axis=-1)
    wg = np.ascontiguousarray(f("m_w_gates").reshape(L, 24, 128, 8).transpose(0, 2, 1, 3)).reshape(L, 128, 192)
    qkv = np.stack([f("m_wq"), f("m_wk"), f("m_wv")], axis=1)
    qkv = qkv.reshape(L, 3, 4, 2, 128, 256).transpose(0, 4, 2, 1, 3, 5)
    wqkv = np.ascontiguousarray(qkv).reshape(L, 128, 24 * 256)
    con, sel = _consts()
    shared = {
        "vecs": vecs, "fvec": fvec, "gb": np.ascontiguousarray(gb), "wg": wg, "wqkv": wqkv,
        "w_mod": f("w_mod"), "w_in": f("w_in"), "w_glu": f("ssm_w_glu"), "w_out": f("w_out"),
        "lam_re": f("ssm_lambda_re"), "lam_im": f("ssm_lambda_im"), "log_dt": f("ssm_log_dt"),
        "b_re": f("ssm_b_re"), "b_im": f("ssm_b_im"), "c_re": f("ssm_c_re"), "c_im": f("ssm_c_im"),
        "consts": con, "sel": sel,
    }
    B = x.shape[0]
    in_maps = []
    for b in range(B):
        m = dict(shared)
        m["x"] = np.ascontiguousarray(x[b])
        m["cT"] = fm(c[b], 8)
        in_maps.append(m)
    if "nc" not in _CACHE:
        _CACHE["nc"] = build_program()
    res = run_bass_kernel_spmd(_CACHE["nc"], in_maps, core_ids=list(range(B)))
    if DEBUG:
        _CACHE["dbg"] = {n: np.asarray(res.results[0]["dbg"])[i] for i, n in enumerate(_CACHE["nc"]._dbgnames)}
    return np.stack([np.asarray(r["y"], np.float32) for r in res.results], axis=0)
```

```python
import numpy as np
from contextlib import ExitStack
import concourse.bass as bass
import concourse.mybir as mybir
from concourse.bass_utils import run_bass_kernel_spmd

F32 = mybir.dt.float32
F32R = mybir.dt.float32r


def R(ap):
    return ap.bitcast(F32R)
AF = mybir.ActivationFunctionType
ALU = mybir.AluOpType

S = 2048
D = 1024
T = 256
NT = S // T
SL = 2
NS5 = max(1, SL // 2)
PPS = 32 // NS5
FPS = 8 // NS5
NB = T // SL
CH = 64
NCH = T // CH
DEPTH = 2
EPS = 1e-6
NV = 112
CHAIN_ENG = "dve"
CHAIN_STEP = 3
DEBUG = False
DBG_L = 0
DBG_T = 0
NDBG = 16
SYNC_SAME = {"pe": False, "act": True, "dve": True, "pool": False, "sp": False}


class Sched:
    def __init__(self, nc, es):
        self.nc = nc
        self.es = es
        self.eng = {"pe": nc.tensor, "act": nc.scalar, "dve": nc.vector, "pool": nc.gpsimd, "sp": nc.sync}
        self.sem = {k: es.enter_context(nc.semaphore("sem_" + k)) for k in self.eng}
        self.cnt = {k: 0 for k in self.eng}
        self.waited = {k: {} for k in self.eng}
        self.lastw = {}
        self.readers = {}
        self.streams = {}
        self.scnt = {}
        self.ninst = 0

    def _deps(self, r, w):
        deps = []
        for x in r:
            if x in self.lastw:
                deps.append(self.lastw[x])
        for x in w:
            if x in self.lastw:
                deps.append(self.lastw[x])
            deps.extend(self.readers.get(x, ()))
        return deps

    def _wait(self, en, deps, nosync=False):
        e = self.eng[en]
        best = {}
        for (sname, sem, val) in deps:
            if sname == en and (nosync or not SYNC_SAME[en]):
                continue
            if self.waited[en].get(sname, 0) >= val:
                continue
            if best.get(sname, (None, 0))[1] < val:
                best[sname] = (sem, val)
        for sname, (sem, val) in best.items():
            e.wait_ge(sem, val)
            self.waited[en][sname] = val
            self.ninst += 1

    def _record(self, dep, r, w):
        for x in w:
            self.lastw[x] = dep
            self.readers[x] = []
        for x in r:
            self.readers.setdefault(x, []).append(dep)

    def op(self, en, fn, r=(), w=(), nosync=False):
        self._wait(en, self._deps(r, w), nosync)
        inst = fn(self.eng[en])
        self.cnt[en] += 1
        inst.then_inc(self.sem[en], 1)
        self.ninst += 1
        self._record((en, self.sem[en], self.cnt[en]), r, w)

    def dma(self, out, in_, r=(), w=(), stream=None, q="sp", **kw):
        if stream not in self.streams:
            self.streams[stream] = self.es.enter_context(self.nc.semaphore("dq_" + stream))
            self.scnt[stream] = 0
        self._wait(q, self._deps(r, w))
        self.scnt[stream] += 16
        self.eng[q].dma_start(out=out, in_=in_, **kw).then_inc(self.streams[stream], 16)
        self.ninst += 1
        self._record(("dq_" + stream, self.streams[stream], self.scnt[stream]), r, w)

    def sync_stream(self, stream, resources):
        dep = ("dq_" + stream, self.streams[stream], self.scnt[stream])
        for x in resources:
            self.lastw[x] = dep

    def merge(self, dst, srcs):
        for x in srcs:
            if x in self.lastw:
                self.readers.setdefault(dst, []).append(self.lastw[x])
            self.readers.setdefault(dst, []).extend(self.readers.get(x, ()))

    def barrier(self):
        deps = list(self.lastw.values())
        for v in self.readers.values():
            deps.extend(v)
        for en in self.eng:
            self._wait(en, deps)

    def finish(self, en="sp"):
        deps = list(self.lastw.values())
        for v in self.readers.values():
            deps.extend(v)
        self._wait(en, deps)


def fm(v, n):
    return np.ascontiguousarray(np.asarray(v, np.float32).reshape(n, 128).T)


def build_program():
    nc = bass.Bass("TRN2", target_bir_lowering=False)
    L = DEPTH

    def din(name, shape):
        return nc.dram_tensor(name, list(shape), F32, kind="ExternalInput")

    x_h = din("x", [S, D])
    cT_h = din("cT", [128, 8])
    vecs_h = din("vecs", [L, 128, NV])
    fvec_h = din("fvec", [128, 8])
    gb_h = din("gb", [L, 4, 2])
    wg_h = din("wg", [L, 128, 24 * 8])
    wqkv_h = din("wqkv", [L, 128, 24 * 256])
    wmod_h = din("w_mod", [L, D, 3 * D])
    win_h = din("w_in", [L, D, 5 * D])
    wglu_h = din("w_glu", [L, D, D])
    wout_h = din("w_out", [L, 2 * D, D])
    lre_h = din("lam_re", [L, 64, 64])
    lim_h = din("lam_im", [L, 64, 64])
    ldt_h = din("log_dt", [L, 64])
    bre_h = din("b_re", [L, 64, 64, 16])
    bim_h = din("b_im", [L, 64, 64, 16])
    cre_h = din("c_re", [L, 64, 16, 64])
    cim_h = din("c_im", [L, 64, 16, 64])
    NCON = 128 + 128 + 128 + 3 * T + 64
    con_h = din("consts", [128, NCON])
    sel_h = din("sel", [4, 4 * 128])
    y_h = nc.dram_tensor("y", [S, D], F32, kind="ExternalOutput")
    dbg_h = nc.dram_tensor("dbg", [NDBG, 128, 8 * T + 32], F32, kind="ExternalOutput") if DEBUG else None
    dbgi = [0]
    dbgnames = []
    xs_h = nc.dram_tensor("xs", [128, 8 * S], F32, kind="Internal")
    winT_h = nc.dram_tensor("winT_d", [L, 128, 16 * SL * 128], F32, kind="Internal")
    woutS_h = nc.dram_tensor("woutS_d", [L, 128, 64 * SL * 32], F32, kind="Internal")

    AP = bass.AP
    with ExitStack() as es:
        sc = Sched(nc, es)
        op = sc.op

        def sb(name, shape, st=es):
            return st.enter_context(nc.sbuf_tensor("s_" + name, list(shape), F32))

        con = sb("con", [128, NCON])
        ident = con[:, 0:128]
        ones = con[:, 128:256]
        pmask = con[:, 256:384]
        mask01 = con[:, 384:384 + T]
        cmask = con[:, 384 + T:384 + 2 * T]
        negmask = con[:, 384 + 2 * T:384 + 3 * T]
        causal = con[0:64, 384 + 3 * T:384 + 3 * T + 64]
        sel = sb("sel", [4, 4 * 128])
        vecs = sb("vecs", [128, L, NV])
        fvec = sb("fvec", [128, 8])
        gb = sb("gb", [4, L, 2])
        wg = sb("wg", [128, L, 24 * 8])
        cT = sb("cT", [128, 8])
        aL = sb("aL", [128, L, 2, 32])
        sc.dma(con[:], con_h.ap(), w=["con"], stream="c0")
        sc.dma(sel[:], sel_h.ap(), w=["sel"], stream="c0")
        sc.dma(fvec[:], fvec_h.ap(), w=["fvec"], stream="c0")
        sc.dma(cT[:], cT_h.ap(), w=["cT"], stream="c0")
        for l in range(L):
            sc.dma(vecs[:, l, :], vecs_h.ap()[l], w=["vecs%d" % l], stream="c0")
            sc.dma(gb[:, l, :], gb_h.ap()[l], w=["gb%d" % l], stream="c0")
            sc.dma(wg[:, l, :], wg_h.ap()[l], w=["wg%d" % l], stream="c0")

        sc.sync_stream("c0", ["con", "sel", "fvec", "cT", "vecs", "gb", "wg"])
        psum = [es.enter_context(nc.psum_tensor("ps%d" % i, [128, 512], F32)) for i in range(8)]
        pctr = [0]

        def pnext():
            i = pctr[0] % 6
            pctr[0] += 1
            return psum[i], "ps%d" % i

        p8ctr = [0]

        def pnext8():
            i = p8ctr[0] % 8
            p8ctr[0] += 1
            return psum[i], "ps%d" % i

        with ExitStack() as ps_:
            def sp(name, shape):
                return sb(name, shape, ps_)
            lamre = sp("lamre", [128, 32]); lamim = sp("lamim", [128, 32]); dtb = sp("dtb", [128, 32])
            breT = sp("breT", [128, 32, 16]); bimT = sp("bimT", [128, 32, 16])
            cnre = sp("cnre", [128, 8, 64]); cnim = sp("cnim", [128, 8, 64])
            tq = [sp("tq%d" % i, [128, 32]) for i in range(26)]
            apw = sp("apw", [128, 5, 2, 32]); aiw = sp("aiw", [128, 5, 2, 32])
            tb = [sp("tb%d" % i, [128, 32, 16]) for i in range(4)]
            bpad = [sp("bpad%d" % i, [128, 32, 32]) for i in range(2)]
            cpad = [sp("cpad%d" % i, [128, 32, 32]) for i in range(2)]
            xm = sp("xm", [128, 128])
            tcb = [sp("tc%d" % i, [128, 32, 32]) for i in range(4)]
            stW = sp("stW", [128, 16 * SL, 128])
            stO = sp("stO", [128, 64 * SL, 32])

            def tt(o, a, b, o_, rr, ww, en="dve"):
                op(en, lambda e: e.tensor_tensor(o, a, b, o_), r=rr, w=ww)

            def ts(o, a, s1, s2, o0, o1, rr, ww, en="dve"):
                if o1 is None:
                    op(en, lambda e: e.tensor_scalar(o, a, s1, None, o0), r=rr, w=ww)
                else:
                    op(en, lambda e: e.tensor_scalar(o, a, s1, s2, o0, o1), r=rr, w=ww)

            for l in range(L):
                sc.dma(lamre[:], AP(lre_h, l * 4096, [[1, 128], [128, 32]]), w=["lamre"], stream="pc%d" % l,
                       allow_slow_non_contiguous=True)
                sc.dma(lamim[:], AP(lim_h, l * 4096, [[1, 128], [128, 32]]), w=["lamim"], stream="pc%d" % l,
                       allow_slow_non_contiguous=True)
                for gl in range(2):
                    sc.dma(dtb[gl * 64:(gl + 1) * 64, :], AP(ldt_h, l * 64 + gl, [[0, 64], [2, 32]]), w=["dtb%d" % gl],
                           stream="pc%d" % l, allow_slow_non_contiguous=True)
                sc.dma(breT[:], AP(bre_h, l * 65536, [[16, 128], [2048, 32], [1, 16]]), w=["breT"], stream="pc%d" % l)
                sc.dma(bimT[:], AP(bim_h, l * 65536, [[16, 128], [2048, 32], [1, 16]]), w=["bimT"], stream="pc%d" % l)
                sc.dma(cnre[:], AP(cre_h, l * 65536, [[64, 128], [8192, 8], [1, 64]]), w=["cnre"], stream="pc%d" % l)
                sc.dma(cnim[:], AP(cim_h, l * 65536, [[64, 128], [8192, 8], [1, 64]]), w=["cnim"], stream="pc%d" % l)
                sc.sync_stream("pc%d" % l, ["lamre", "lamim", "dtb", "breT", "bimT", "cnre", "cnim"])
                names = ["tq%d" % i for i in range(24)]
                (dt_, lr, th, r1, rinv, xx, x2, sn, cs, cc, ss, cs2, are, aim, am1, nr, ni, den, cfr, cfi, t0, t1, t2, t3) = \
                    [(tq[i][:], names[i]) for i in range(24)]

                def A(o, a, b, o_):
                    tt(o[0], a[0], b[0], o_, [a[1], b[1]], [o[1]])

                def Sx(o, a, s1, s2, o0, o1=None):
                    ts(o[0], a[0], s1, s2, o0, o1, [a[1]], [o[1]])

                LR = (lamre[:], "lamre"); LI = (lamim[:], "lamim")
                op("act", lambda e: e.activation(out=dt_[0], in_=dtb[:], func=AF.Exp), r=["dtb"], w=[dt_[1]])
                A(lr, LR, dt_, ALU.mult)
                A(th, LI, dt_, ALU.mult)
                op("act", lambda e: e.activation(out=r1[0], in_=lr[0], func=AF.Exp), r=[lr[1]], w=[r1[1]])
                op("act", lambda e: e.activation(out=rinv[0], in_=lr[0], func=AF.Exp, scale=-1.0), r=[lr[1]], w=[rinv[1]])
                Sx(xx, th, 1.0 / 32, None, ALU.mult)
                A(x2, xx, xx, ALU.mult)
                Sx(sn, x2, -1.0 / 5040, 1.0 / 120, ALU.mult, ALU.add)
                A(sn, sn, x2, ALU.mult); Sx(sn, sn, -1.0 / 6, None, ALU.add)
                A(sn, sn, x2, ALU.mult); Sx(sn, sn, 1.0, None, ALU.add)
                A(sn, sn, xx, ALU.mult)
                Sx(cs, x2, 1.0 / 40320, -1.0 / 720, ALU.mult, ALU.add)
                A(cs, cs, x2, ALU.mult); Sx(cs, cs, 1.0 / 24, None, ALU.add)
                A(cs, cs, x2, ALU.mult); Sx(cs, cs, -0.5, None, ALU.add)
                A(cs, cs, x2, ALU.mult); Sx(cs, cs, 1.0, None, ALU.add)
                for _ in range(5):
                    A(cc, cs, cs, ALU.mult); A(ss, sn, sn, ALU.mult); A(cs2, cs, sn, ALU.mult)
                    A(cs, cc, ss, ALU.subtract); Sx(sn, cs2, 2.0, None, ALU.mult)
                A(are, r1, cs, ALU.mult); A(aim, r1, sn, ALU.mult)
                Sx(am1, are, -1.0, None, ALU.add)
                A(t0, am1, LR, ALU.mult); A(t1, aim, LI, ALU.mult); A(nr, t0, t1, ALU.add)
                A(t0, aim, LR, ALU.mult); A(t1, am1, LI, ALU.mult); A(ni, t0, t1, ALU.subtract)
                A(t0, LR, LR, ALU.mult); A(t1, LI, LI, ALU.mult); A(den, t0, t1, ALU.add)
                op("dve", lambda e: e.reciprocal(den[0], den[0]), r=[den[1]], w=[den[1]])
                A(cfr, nr, den, ALU.mult); A(cfi, ni, den, ALU.mult)

                def PW(t_, k, ri):
                    return (t_[:, k, ri, :], t_.name if hasattr(t_, "name") else "pw")
                APW = lambda k, ri: (apw[:, k, ri, :], "apw")
                AIW = lambda k, ri: (aiw[:, k, ri, :], "aiw")
                op("dve", lambda e: e.tensor_copy(apw[:, 1, 0, :], are[0]), r=[are[1]], w=["apw"])
                op("dve", lambda e: e.tensor_copy(apw[:, 1, 1, :], aim[0]), r=[aim[1]], w=["apw"])
                A(AIW(1, 0), rinv, cs, ALU.mult)
                A(t0, rinv, sn, ALU.mult); Sx(AIW(1, 1), t0, -1.0, None, ALU.mult)

                def cmul(o_re, o_im, a_re, a_im, b_re, b_im):
                    A(t0, a_re, b_re, ALU.mult); A(t1, a_im, b_im, ALU.mult)
                    A(t2, a_re, b_im, ALU.mult); A(t3, a_im, b_re, ALU.mult)
                    A(o_re, t0, t1, ALU.subtract); A(o_im, t2, t3, ALU.add)
                for k in range(2, 5):
                    cmul(APW(k, 0), APW(k, 1), APW(k - 1, 0), APW(k - 1, 1), APW(1, 0), APW(1, 1))
                    cmul(AIW(k, 0), AIW(k, 1), AIW(k - 1, 0), AIW(k - 1, 1), AIW(1, 0), AIW(1, 1))
                op("dve", lambda e: e.tensor_copy(aL[:, l, :, :], apw[:, SL, :, :]), r=["apw"], w=["aL"])

                for ri, cn_ in enumerate((cnre, cnim)):
                    for fc in range(8):
                        op("dve", lambda e: e.tensor_tensor(xm[:].rearrange("p (g q) -> p g q", g=2),
                                                            cn_[:, fc, :].unsqueeze(1).to_broadcast([128, 2, 64]),
                                                            pmask.rearrange("p (g q) -> p g q", g=2), ALU.mult),
                           r=[cn_.name if False else ("cnre" if ri == 0 else "cnim"), "con"], w=["xm"])
                        pt, pn = pnext()
                        op("pe", lambda e: e.transpose(pt[:, 0:128], xm[:], ident), r=["xm", "con"], w=[pn])
                        op("act", lambda e: e.activation(
                            out=cpad[ri][:, fc * 4:(fc + 1) * 4, :].rearrange("p a b -> p (a b)"),
                            in_=pt[:, 0:128], func=AF.Copy), r=[pn], w=["cpad%d" % ri])
                for j in range(SL):
                    Gre = (tq[24][:], "tq24"); Gim = (tq[25][:], "tq25")
                    cmul(Gre, Gim, AIW(j + 1, 0), AIW(j + 1, 1), cfr, cfi)
                    gb_re = Gre[0].unsqueeze(2).to_broadcast([128, 32, 16])
                    gb_im = Gim[0].unsqueeze(2).to_broadcast([128, 32, 16])
                    tt(tb[0][:], breT[:], gb_re, ALU.mult, ["breT", Gre[1]], ["tb0"])
                    tt(tb[1][:], bimT[:], gb_im, ALU.mult, ["bimT", Gim[1]], ["tb1"])
                    tt(tb[2][:], bimT[:], gb_re, ALU.mult, ["bimT", Gre[1]], ["tb2"])
                    tt(tb[3][:], breT[:], gb_im, ALU.mult, ["breT", Gim[1]], ["tb3"])
                    for ri in range(2):
                        op("pool", lambda e: e.memset(bpad[ri][:], 0.0), w=["bpad%d" % ri])
                    for gl in range(2):
                        pr = slice(gl * 64, (gl + 1) * 64)
                        cr = slice(gl * 16, (gl + 1) * 16)
                        tt(bpad[0][pr, :, cr], tb[0][pr], tb[1][pr], ALU.subtract, ["tb0", "tb1"], ["bpad0"])
                        tt(bpad[1][pr, :, cr], tb[2][pr], tb[3][pr], ALU.add, ["tb2", "tb3"], ["bpad1"])
                    for ri in range(2):
                        for fc in range(8):
                            pt, pn = pnext()
                            op("pe", lambda e: e.transpose(
                                pt[:, 0:128], bpad[ri][:, fc * 4:(fc + 1) * 4, :].rearrange("p a b -> p (a b)"), ident),
                               r=["bpad%d" % ri, "con"], w=[pn])
                            idx = (fc * SL + j) * 2 + ri
                            op("act", lambda e: e.activation(out=stW[:, idx, :], in_=pt[:, 0:128], func=AF.Copy),
                               r=[pn], w=["stW"])
                    a_re = apw[:, j + 1, 0, :].unsqueeze(2).to_broadcast([128, 32, 32])
                    a_im = apw[:, j + 1, 1, :].unsqueeze(2).to_broadcast([128, 32, 32])
                    tt(tcb[0][:], cpad[0][:], a_re, ALU.mult, ["cpad0", "apw"], ["tc0"])
                    tt(tcb[1][:], cpad[1][:], a_im, ALU.mult, ["cpad1", "apw"], ["tc1"])
                    tt(tcb[2][:], cpad[0][:], a_im, ALU.mult, ["cpad0", "apw"], ["tc2"])
                    tt(tcb[3][:], cpad[1][:], a_re, ALU.mult, ["cpad1", "apw"], ["tc3"])
                    stOv = stO[:].rearrange("p (q j r) c -> p q j r c", j=SL, r=2)
                    tt(stOv[:, :, j, 0, :], tcb[0][:], tcb[1][:], ALU.subtract, ["tc0", "tc1"], ["stO"])
                    tt(tcb[2][:], tcb[2][:], tcb[3][:], ALU.add, ["tc2", "tc3"], ["tc2"])
                    ts(stOv[:, :, j, 1, :], tcb[2][:], -1.0, None, ALU.mult, None, ["tc2"], ["stO"])
                sc.dma(winT_h.ap()[l], stW[:].rearrange("p a b -> p (a b)"), r=["stW"], w=["d_winT%d" % l], stream="pcwa%d" % l)
                sc.dma(woutS_h.ap()[l], stO[:].rearrange("p a b -> p (a b)"), r=["stO"], w=["d_wout%d" % l], stream="pcwb%d" % l)

        sc.barrier()
        xt = sb("xt", [128, 8, T]); hT = sb("hT", [128, 8, T])
        secA = sb("secA", [128, 8, T]); secB = sb("secB", [128, 8, T]); secC = sb("secC", [128, 8, T])
        xin = secC[:].rearrange("p a b -> p (a b)").rearrange("p (b d) -> p b d", b=2)
        minT = sb("minT", [128, 8, T + 4]); mixT = sb("mixT", [128, 16, T])
        wb = [sb("wb0", [128, 4096])]
        cs = [sb("cs%d" % i, [128, 4096]) for i in range(3)]
        xprev = sb("xprev", [128, 2, 32, NB + 1])
        CT = sb("CT", [128, 8, 260])
        zl = [[sb("zl%d%d" % (a, b), [128, T]) for b in range(2)] for a in range(2)]
        tA = sb("tA", [128, T]); tB = sb("tB", [128, T]); rstd = sb("rstd", [128, T])
        qT = sb("qT", [128, 2, T]); kT = sb("kT", [128, 2, T])
        Kw = [sb("Kw%d" % i, [64, 256]) for i in range(2)]
        Va = [sb("Va%d" % i, [64, 260]) for i in range(2)]
        DTs = [sb("DT%d" % i, [64, 64]) for i in range(2)]; SwTs = [sb("SwT%d" % i, [64, 64]) for i in range(2)]; sbA = sb("sbA", [64, 260]); nd = sb("nd", [64, 260])
        hTMs = [sb("hTM%d" % i, [64, 256]) for i in range(2)]; dd = sb("dd", [64, 2])
        modv = sb("modv", [128, 24]); g1 = sb("g1", [128, 8]); csil = sb("csil", [128, 8, 2])
        ig = sb("ig", [4, T]); lf = sb("lf", [4, T]); bcs = sb("bcs", [4, T]); gg = sb("gg", [4, T]); Mx = lf
        mm = ig; rows = sb("rows", [4, 4, T]); mprev = sb("mprev", [4, NCH + 1]); nml = sb("nml", [4, NCH])
        colsS = sb("colsS", [64, 64]); decB = sb("decB", [128, 16])

        wl = []
        wstate = {"issued": 0, "released": 0, "next": 0}
        wcast = set()

        NCS = len(cs)
        cfree = [True] * NCS
        bfree = [True]
        cptr = [0]
        pptr = [0]
        slot_of = {}
        outstanding = []

        def w_top():
            cast_ids = wstate.setdefault("cast_ids", [i for i in range(len(wl)) if wl[i][3]])
            plain_ids = wstate.setdefault("plain_ids", [i for i in range(len(wl)) if not wl[i][3]])
            while cptr[0] < len(cast_ids) and cfree[cptr[0] % NCS]:
                i = cast_ids[cptr[0]]
                s_ = cptr[0] % NCS
                src, shp, rdeps, _c = wl[i]
                dst = cs[s_][:, 0:int(np.prod(shp))]
                if len(shp) == 2:
                    dst = dst.rearrange("p (a b) -> p a b", a=shp[0])
                sc.dma(R(dst), src, r=rdeps, w=["cs%d" % s_], stream="cs%d" % s_, q="pool")
                cfree[s_] = False
                slot_of[i] = ("cs", s_)
                cptr[0] += 1
            while pptr[0] < len(plain_ids) and bfree[0]:
                i = plain_ids[pptr[0]]
                src, shp, rdeps, _c = wl[i]
                dst = wb[0][:, 0:int(np.prod(shp))]
                if len(shp) == 2:
                    dst = dst.rearrange("p (a b) -> p a b", a=shp[0])
                sc.dma(dst, src, r=rdeps, w=["wb0"], stream="wb0")
                bfree[0] = False
                slot_of[i] = ("wb", 0)
                pptr[0] += 1

        def w_get():
            i = wstate["next"]
            wstate["next"] += 1
            w_top()
            assert i in slot_of, "weight not issued"
            outstanding.append(i)
            kind, s_ = slot_of[i]
            return (cs[s_], "cs%d" % s_) if kind == "cs" else (wb[0], "wb0")

        def w_rel():
            i = outstanding.pop(0)
            kind, s_ = slot_of[i]
            if kind == "cs":
                cfree[s_] = True
            else:
                bfree[0] = True
            w_top()

        def wsec(h, l, k0, nk, c0, ncol):
            t_ = h.ap()[l]
            return t_[k0 * 128:(k0 + nk) * 128, c0:c0 + ncol].rearrange("(k p) n -> p k n", p=128)

        for l in range(L):
            for i in range(6):
                wl.append((wsec(wmod_h, l, 0, 8, i * 512, 512), (8, 512), [], False))
            for t in range(NT):
                for i in range(4):
                    wl.append((wsec(win_h, l, 0, 8, i * 512, 512), (8, 512), [], True))
                for hf in range(NS5):
                    wl.append((winT_h.ap()[l][:, hf * 4096:(hf + 1) * 4096], (4096,), ["d_winT%d" % l], True))
                    wl.append((woutS_h.ap()[l][:, hf * 4096:(hf + 1) * 4096], (4096,), ["d_wout%d" % l], False))
                for i in range(4, 10):
                    wl.append((wsec(win_h, l, 0, 8, i * 512, 512), (8, 512), [], True))
                for hp in range(2):
                    wl.append((wqkv_h.ap()[l][:, hp * 3072:(hp + 1) * 3072], (3072,), [], True))
                for hf in range(NS5):
                    wl.append((woutS_h.ap()[l][:, hf * 4096:(hf + 1) * 4096], (4096,), ["d_wout%d" % l], False))
                for i in range(2):
                    wl.append((wsec(wglu_h, l, 0, 8, i * 512, 512), (8, 512), [], True))
                for hp in range(2):
                    wl.append((wqkv_h.ap()[l][:, hp * 3072:(hp + 1) * 3072], (3072,), [], True))
                for i in range(4):
                    wl.append((wsec(wout_h, l, 0, 16, i * 256, 256), (16, 256), [], True))

        def dbg(name, ap, res, l, t, n=8 * T):
            if not DEBUG or l != DBG_L or t != DBG_T:
                return
            i = dbgi[0]; dbgi[0] += 1
            dbgnames.append(name)
            sc.dma(dbg_h.ap()[i][0:ap.shape[0], 0:n], ap, r=[res], w=["d_dbg"], stream="dbg")

        chain = [None]

        def chain_adv(k):
            if chain[0] is None:
                return
            for _ in range(k):
                try:
                    next(chain[0])
                except StopIteration:
                    chain[0] = None
                    return

        def evac(i, out, in_, r, w, func=AF.Copy, force_act=False, **kw):
            if func == AF.Copy and i % 2 == 0 and not force_act:
                op("dve", lambda e: e.tensor_copy(out, in_), r=r, w=w)
            else:
                op("act", lambda e: e.activation(out=out, in_=in_, func=func, **kw), r=r, w=w)

        def rms_rstd(src, srcn, tmp8, tmp8n, scale):
            op("act", lambda e: e.activation(out=R(tmp8[:]), in_=(src if isinstance(src, bass.AP) else src[:]), func=AF.Square), r=[srcn], w=[tmp8n])
            op("pe", lambda e: e.matmul(psum[6][:, 0:T], ones, tmp8[:, 0, :], start=True, stop=False), r=["con", tmp8n], w=["ps6"])
            for kc in range(1, 8):
                op("pe", lambda e: e.matmul(psum[6][:, 0:T], ones, tmp8[:, kc, :], start=False, stop=(kc == 7)),
                   r=["con", tmp8n], w=["ps6"])
            op("act", lambda e: e.activation(out=tA[:], in_=psum[6][:, 0:T], func=AF.Sqrt, scale=scale, bias=EPS),
               r=["ps6"], w=["tA"])
            op("dve", lambda e: e.reciprocal(rstd[:], tA[:]), r=["tA"], w=["rstd"])

        for l in range(L):
            V = lambda c0, n=1: vecs[:, l, c0:c0 + n]
            op("act", lambda e: e.activation(out=csil[:, :, 0], in_=cT[:], func=AF.Silu), r=["cT"], w=["csil"])
            op("act", lambda e: e.activation(out=csil[:, :, 1], in_=cT[:], func=AF.Silu), r=["cT"], w=["csil"])
            for i in range(6):
                wt, wn = w_get()
                wv = wt[:, 0:4096].rearrange("p (a b) -> p a b", a=8)
                for nl in range(4):
                    n = i * 4 + nl
                    for kc in range(8):
                        op("pe", lambda e: e.matmul(psum[7][:, 2 * n:2 * n + 2], wv[:, kc, nl * 128:(nl + 1) * 128],
                                                    csil[:, kc, :], start=(kc == 0), stop=(kc == 7)),
                           r=[wn, "csil"], w=["ps7"])
                w_rel()
            op("dve", lambda e: e.tensor_tensor(modv[:], psum[7][:, 0:48:2], V(8, 24), ALU.add), r=["ps7", "vecs"], w=["modv"])
            op("dve", lambda e: e.tensor_scalar(g1[:], modv[:, 8:16], 1.0, None, ALU.add), r=["modv"], w=["g1"])
            op("dve", lambda e: e.tensor_tensor(g1[:], g1[:], V(0, 8), ALU.mult), r=["g1", "vecs"], w=["g1"])
            op("pool", lambda e: e.memset(xprev[:], 0.0), w=["xprev"])
            op("pool", lambda e: e.memset(CT[:], 0.0), w=["CT"])
            op("pool", lambda e: e.tensor_copy(R(minT[:, :, 0:3]), xprev[:, 0, 0:8, 0:3]), r=["xprev"], w=["minT"])
            op("pool", lambda e: e.memset(mprev[:], 0.0), w=["mprev"])
            for i in range(2):
                op("act", lambda e: e.activation(out=R(Va[i][:, 256:260]), in_=con[0:64, 128:132], func=AF.Identity, scale=0.0, bias=1.0),
                   r=["con"], w=["Va%d" % i])

            for t in range(NT):
                t0 = t * T
                if l == 0:
                    sc.dma(xin, x_h.ap()[t0:t0 + T, :].rearrange("(b p) d -> p b d", p=128), w=["secC"], stream="xin")
                    for kc in range(8):
                        pt, pn = pnext()
                        for b in range(2):
                            op("pe", lambda e: e.transpose(pt[:, b * 128:(b + 1) * 128], xin[:, b, kc * 128:(kc + 1) * 128], ident),
                               r=["secC", "con"], w=[pn])
                        evac(kc, xt[:, kc, :], pt[:, 0:T], [pn], ["xt"])
                else:
                    sc.dma(xt[:], xs_h.ap().rearrange("p (k s) -> p k s", k=8)[:, :, t0:t0 + T], r=["d_xs"], w=["xt"], stream="xt")
                rms_rstd(xt, "xt", secB, "secB", 1.0 / D)
                for kc in range(8):
                    op("dve", lambda e: e.scalar_tensor_tensor(tB[:], xt[:, kc, :], g1[:, kc:kc + 1], rstd[:], ALU.mult, ALU.mult),
                       r=["xt", "g1", "rstd"], w=["tB"])
                    op("act", lambda e: e.activation(out=R(hT[:, kc, :]), in_=tB[:], func=AF.Identity, bias=modv[:, kc:kc + 1]),
                       r=["tB", "modv"], w=["hT"])

                def proj(dst_fn, dstn, func, **kw):
                    for hf in range(2):
                        wt, wn = w_get()
                        wv = wt[:, 0:4096].rearrange("p (a b) -> p a b", a=8)
                        for mc in range(4):
                            pt, pn = pnext()
                            for kc in range(8):
                                op("pe", lambda e: e.matmul(pt[:, 0:T], R(wv[:, kc, mc * 128:(mc + 1) * 128]), R(hT[:, kc, :]),
                                                            start=(kc == 0), stop=(kc == 7)), r=[wn, "hT"], w=[pn])
                            evac(mc, dst_fn(hf * 4 + mc), pt[:, 0:T], [pn], (dstn if isinstance(dstn, list) else [dstn]), func=func, **kw)
                            chain_adv(CHAIN_STEP)
                        w_rel()

                dbg("hT", hT[:].rearrange("p a b -> p (a b)"), "hT", l, t)
                proj(lambda n: R(secA[:, n, :]), "secA", AF.Copy)
                proj(lambda n: R(mixT[:, n, :]), "mixT", AF.Silu)

                assert NS5 == 1 and SL == 2
                wi = w_get(); wo = w_get()
                wiv = wi[0][:, 0:4096].rearrange("p (a b) -> p a b", b=128)
                wov = wo[0][:, 0:4096].rearrange("p (a b) -> p a b", b=32)

                def s5_in(q):
                    fc, qq = q // 4, q % 4
                    pr = slice(32 * qq, 32 * qq + 32)
                    pt, pn = pnext()
                    for ri in range(2):
                        for j in range(SL):
                            idx = (fc * SL + j) * 2 + ri
                            op("pe", lambda e: e.matmul(pt[:, (ri * SL + j) * NB:(ri * SL + j + 1) * NB], R(wiv[pr, idx, :]), R(secA[pr, fc, j:T:SL]),
                                                        start=True, stop=True, tile_position=(32 * qq, 0)), r=[wi[1], "secA"], w=[pn])
                    zz = zl[q % 2]
                    for ri in range(2):
                        zn = "zl%d%d" % (q % 2, ri)
                        op("dve", lambda e: e.tensor_copy(zz[ri][:, 0:NB], pt[:, (ri * SL) * NB:(ri * SL + 1) * NB]), r=[pn], w=[zn])
                        op("dve", lambda e: e.tensor_tensor(zz[ri][:, NB:2 * NB], zz[ri][:, 0:NB], pt[:, (ri * SL + 1) * NB:(ri * SL + 2) * NB], ALU.add),
                           r=[pn, zn], w=[zn])
                        op("act", lambda e: e.activation(out=xprev[:, ri, q, 1:NB + 1], in_=zz[ri][:, NB:2 * NB], func=AF.Copy), r=[zn], w=["xprev"])

                def s5_out(q):
                    fc, qq = q // 4, q % 4
                    pr = slice(32 * qq, 32 * qq + 32)
                    zz = zl[q % 2]
                    yb, ybn = psum[6 + (fc % 2)], "ps%d" % (6 + (fc % 2))
                    for j in range(SL):
                        for ri in range(2):
                            idx = (q * SL + j) * 2 + ri
                            op("pe", lambda e: e.matmul(yb[pr, j:T:SL], wov[:, idx, :], zz[ri][:, j * NB:(j + 1) * NB], start=(ri == 0), stop=(ri == 1), tile_position=(0, 32 * qq)),
                               r=[wo[1], "zl%d%d" % (q % 2, ri)], w=[ybn])
                    if qq == 3:
                        op("dve", lambda e: e.scalar_tensor_tensor(R(secB[:, fc, :]), secA[:, fc, :], V(32 + fc), yb[:, 0:T], ALU.mult, ALU.add),
                           r=["secA", "vecs", ybn], w=["secB"])

                s5_in(0)
                for q in range(1, 32):
                    s5_in(q)
                    s5_out(q - 1)
                s5_out(31)
                w_rel(); w_rel()
                dbg("u", secA[:].rearrange("p a b -> p (a b)"), "secA", l, t)
                dbg("y1", secB[:].rearrange("p a b -> p (a b)"), "secB", l, t)
                sB = tA[:, 0:64].rearrange("p (r q) -> p r q", r=2)
                P4 = tA[:, 64:192].rearrange("p (k r q) -> p k r q", k=2, r=2)
                sB4 = sB.unsqueeze(1).to_broadcast([128, 2, 2, 32])
                al4 = aL[:, l, :, :].unsqueeze(2).to_broadcast([128, 2, 2, 32])

                def chain_gen():
                    for m in range(NB):
                        ns = (m > 0)
                        op(CHAIN_ENG, lambda e: e.tensor_tensor(sB, xprev[:, :, :, m], xprev[:, :, :, m + 1], ALU.add), r=["xprev"], w=["tA"], nosync=ns)
                        op(CHAIN_ENG, lambda e: e.tensor_tensor(P4, sB4, al4, ALU.mult), r=["tA", "aL"], w=["tA"], nosync=True)
                        op(CHAIN_ENG, lambda e: e.tensor_tensor(xprev[:, 0, :, m + 1], P4[:, 0, 0, :], P4[:, 1, 1, :], ALU.subtract), r=["tA"], w=["xprev"], nosync=True)
                        op(CHAIN_ENG, lambda e: e.tensor_tensor(xprev[:, 1, :, m + 1], P4[:, 0, 1, :], P4[:, 1, 0, :], ALU.add), r=["tA"], w=["xprev"], nosync=True)
                        yield

                chain[0] = chain_gen()
                proj(lambda n: R(minT[:, n, 3:3 + T]), "minT", AF.Copy, force_act=True)
                for n in range(8):
                    op("dve", lambda e: e.tensor_scalar(tB[:], minT[:, n, 0:T], V(56 + n), None, ALU.mult), r=["minT", "vecs"], w=["tB"])
                    for w_ in range(1, 4):
                        op("dve", lambda e: e.scalar_tensor_tensor(tB[:], minT[:, n, w_:w_ + T], V(56 + w_ * 8 + n), tB[:], ALU.mult, ALU.add),
                           r=["minT", "vecs", "tB"], w=["tB"])
                    op("act", lambda e: e.activation(out=R(secA[:, n, :]), in_=tB[:], func=AF.Silu, bias=V(88 + n)), r=["tB", "vecs"], w=["secA"])
                    chain_adv(CHAIN_STEP)
                dbg("xc", secA[:].rearrange("p a b -> p (a b)"), "secA", l, t)
                dbg("minT", minT[:].rearrange("p a b -> p (a b)"), "minT", l, t, 8 * (T + 4))
                proj(lambda n: R(mixT[:, 8 + n, :]), ["mixTh0", "mixTh1", "mixTh2", "mixTh3"], AF.Sigmoid)
                proj(lambda n: secC[:, n, :], "secC", AF.Silu)
                sc.merge("qT0", ["qT"]); sc.merge("qT1", ["qT"])
                groups = [(h, which, mc) for h in range(4) for which in range(3) for mc in range(2)]
                wslot = [None]

                def p1_mm(g):
                    h, which, mc = groups[g]
                    hl = h % 2
                    if g % 12 == 0:
                        if g:
                            w_rel()
                        wslot[0] = w_get()
                    wt, wn = wslot[0]
                    wv = wt[:, 0:3072].rearrange("p (a b) -> p a b", b=256)
                    pt, pn = pnext()
                    for kc in range(2):
                        src = secA[:, 2 * h + kc, :] if which < 2 else minT[:, 2 * h + kc, 3:3 + T]
                        op("pe", lambda e: e.matmul(pt[:, 0:T], R(wv[:, (hl * 3 + which) * 2 + kc, mc * 128:(mc + 1) * 128]), R(src),
                                                    start=(kc == 0), stop=(kc == 1)), r=[wn, "secA", "minT"], w=[pn])
                    qb = qT[:, g % 2, :]; qbn = "qT%d" % (g % 2)
                    op("act", lambda e: e.activation(out=R(qb), in_=pt[:, 0:T], func=AF.Copy, scale=(1.0 / 16 if which == 1 else 1.0)), r=[pn], w=[qbn])

                def p1_gates(g):
                    h, which, mc = groups[g]
                    qb = qT[:, g % 2, :]; qbn = "qT%d" % (g % 2)
                    chunk = which * 8 + 2 * h + mc
                    for gi in range(2):
                        op("pe", lambda e: e.matmul(psum[6 + gi][0:4, 0:T], wg[:, l, chunk * 8 + gi * 4:chunk * 8 + gi * 4 + 4],
                                                    qb, start=(g == 0), stop=(g == 23)),
                           r=["wg", qbn], w=["ps%d" % (6 + gi)])

                p1_mm(0)
                for g in range(1, 24):
                    p1_mm(g)
                    p1_gates(g - 1)
                    chain_adv(CHAIN_STEP)
                p1_gates(23)
                w_rel()
                sc.merge("qT", ["qT0", "qT1"])
                op("act", lambda e: e.activation(out=ig[:], in_=psum[6][0:4, 0:T], func=AF.Identity, bias=gb[:, l, 0:1]), r=["ps6", "gb"], w=["ig"])
                op("act", lambda e: e.activation(out=lf[:], in_=psum[7][0:4, 0:T], func=AF.Identity, bias=gb[:, l, 1:2]), r=["ps7", "gb"], w=["lf"])
                op("act", lambda e: e.activation(out=lf[:], in_=lf[:], func=AF.Exp, scale=-1.0), r=["lf"], w=["lf"])
                op("act", lambda e: e.activation(out=lf[:], in_=lf[:], func=AF.Ln, bias=1.0), r=["lf"], w=["lf"])
                op("dve", lambda e: e.tensor_scalar(lf[:], lf[:], -1.0, None, ALU.mult), r=["lf"], w=["lf"])
                op("dve", lambda e: e.tensor_tensor_scan(bcs[:], cmask[0:4, :], lf[:], 0.0, ALU.mult, ALU.add), r=["lf", "con"], w=["bcs"])
                op("dve", lambda e: e.tensor_tensor(gg[:], ig[:], bcs[:], ALU.subtract), r=["ig", "bcs"], w=["gg"])
                op("dve", lambda e: e.tensor_tensor_scan(Mx[:], negmask[0:4, :], gg[:], -1e30, ALU.add, ALU.max), r=["gg", "con"], w=["lf"])
                for c in range(NCH):
                    cs_ = slice(c * CH, (c + 1) * CH)
                    op("dve", lambda e: e.tensor_scalar(mm[:, cs_], Mx[:, cs_], mprev[:, c:c + 1], None, ALU.max), r=["lf", "mprev"], w=["ig"])
                    op("dve", lambda e: e.tensor_tensor(mprev[:, c + 1:c + 2], bcs[:, c * CH + CH - 1:c * CH + CH], mm[:, c * CH + CH - 1:c * CH + CH], ALU.add),
                       r=["bcs", "ig"], w=["mprev"])
                    op("dve", lambda e: e.tensor_scalar(nml[:, c:c + 1], mm[:, c * CH + CH - 1:c * CH + CH], -1.0, -2.772588722239781, ALU.mult, ALU.add), r=["ig"], w=["nml"])
                    op("act", lambda e: e.activation(out=rows[:, 1, cs_], in_=gg[:, cs_], func=AF.Exp, bias=nml[:, c:c + 1]), r=["gg", "nml"], w=["rows"])
                    op("act", lambda e: e.activation(out=rows[:, 2, cs_], in_=mm[:, cs_], func=AF.Exp, scale=-1.0, bias=mprev[:, c:c + 1]),
                       r=["ig", "mprev"], w=["rows"])
                op("dve", lambda e: e.tensor_copy(rows[:, 0, :], gg[:]), r=["gg"], w=["rows"])
                op("dve", lambda e: e.tensor_tensor(rows[:, 3, :], bcs[:], mm[:], ALU.add), r=["bcs", "ig"], w=["rows"])
                op("act", lambda e: e.activation(out=rows[:, 3, :], in_=rows[:, 3, :], func=AF.Exp, scale=-1.0), r=["rows"], w=["rows"])
                op("dve", lambda e: e.tensor_copy(mprev[:, 0:1], mprev[:, NCH:NCH + 1]), r=["mprev"], w=["mprev"])
                dbg("rows", rows[:].rearrange("p a b -> p (a b)"), "rows", l, t, 4 * T)
                dbg("ig", ig[:], "ig", l, t, T)
                dbg("lf", bcs[:], "bcs", l, t, T)
                dbg("ig", mm[:], "ig", l, t, T)
                for c in range(NCH):
                    for a in range(4):
                        op("pe", lambda e: e.transpose(psum[7][0:64, T + (c * 4 + a) * 4:T + (c * 4 + a) * 4 + 4],
                                                       rows[:, a, c * CH:(c + 1) * CH], ident[0:4, 0:4]), r=["rows", "con"], w=["ps7"])
                op("dve", lambda e: e.tensor_copy(colsS[:], psum[7][0:64, T:T + 64]), r=["ps7"], w=["colsS"])
                for h in range(4):
                    op("pe", lambda e: e.matmul(psum[7][:, T + 64 + h * 4:T + 64 + h * 4 + 4], sel[:, h * 128:(h + 1) * 128],
                                                rows[:, 2, CH - 1:T:CH], start=True, stop=True), r=["sel", "rows"], w=["ps7"])
                op("dve", lambda e: e.tensor_copy(decB[:], psum[7][:, T + 64:T + 80]), r=["ps7"], w=["decB"])
                chain_adv(NB)
                for fc in range(8):
                    if fc % FPS == 0:
                        if fc:
                            w_rel()
                        wo = w_get()
                        wov = wo[0][:, 0:4096].rearrange("p (a b) -> p a b", b=32)
                    pt, pn = pnext()
                    for qq in range(4):
                        q = fc * 4 + qq
                        pr = slice(32 * qq, 32 * qq + 32)
                        for j in range(SL):
                            for ri in range(2):
                                idx = ((q % PPS) * SL + j) * 2 + ri
                                op("pe", lambda e: e.matmul(pt[pr, j:T:SL], wov[:, idx, :], xprev[:, ri, q, 0:NB], start=(ri == 0), stop=(ri == 1), tile_position=(0, 32 * qq)),
                                   r=[wo[1], "xprev"], w=[pn])
                    yv = secB[:, fc, :]
                    op("dve", lambda e: e.tensor_tensor(R(yv), yv, pt[:, 0:T], ALU.add), r=["secB", pn], w=["secB"])
                    op("act", lambda e: e.activation(out=tB[:], in_=yv, func=AF.Square), r=["secB"], w=["tB"])
                    op("dve", lambda e: e.tensor_scalar(tB[:], tB[:], 0.044715, 1.0, ALU.mult, ALU.add), r=["tB"], w=["tB"])
                    op("dve", lambda e: e.tensor_tensor(tB[:], tB[:], yv, ALU.mult), r=["tB", "secB"], w=["tB"])
                    op("act", lambda e: e.activation(out=tB[:], in_=tB[:], func=AF.Sigmoid, scale=1.5957691216057308), r=["tB"], w=["tB"])
                    op("dve", lambda e: e.tensor_tensor(R(yv), yv, tB[:], ALU.mult), r=["tB", "secB"], w=["secB"])
                w_rel()
                op("pool", lambda e: e.tensor_copy(xprev[:, :, :, 0], xprev[:, :, :, NB]), r=["xprev"], w=["xprev"])
                dbg("gelu", secB[:].rearrange("p a b -> p (a b)"), "secB", l, t)
                for hf in range(2):
                    wt, wn = w_get()
                    wv = wt[:, 0:4096].rearrange("p (a b) -> p a b", a=8)
                    for mc in range(4):
                        n = hf * 4 + mc
                        pt, pn = pnext()
                        for kc in range(8):
                            op("pe", lambda e: e.matmul(pt[:, 0:T], R(wv[:, kc, mc * 128:(mc + 1) * 128]), R(secB[:, kc, :]),
                                                        start=(kc == 0), stop=(kc == 7)), r=[wn, "secB"], w=[pn])
                        op("act", lambda e: e.activation(out=tB[:], in_=pt[:, 0:T], func=AF.Sigmoid, bias=V(40 + n)), r=[pn, "vecs"], w=["tB"])
                        op("dve", lambda e: e.tensor_tensor(R(hT[:, n, :]), secB[:, n, :], tB[:], ALU.mult), r=["secB", "tB"], w=["hT"])
                    w_rel()
                dbg("glu", hT[:].rearrange("p a b -> p (a b)"), "hT", l, t)
                rms_rstd(hT, "hT", secB, "secB", 1.0 / D)
                for n in range(8):
                    op("dve", lambda e: e.scalar_tensor_tensor(tB[:], hT[:, n, :], V(48 + n), rstd[:], ALU.mult, ALU.mult),
                       r=["hT", "vecs", "rstd"], w=["tB"])
                    op("dve", lambda e: e.tensor_tensor(R(mixT[:, n, :]), tB[:], mixT[:, n, :], ALU.mult), r=["tB", "mixT"], w=["mixT"])

                dbg("mix0", mixT[:, 0:8, :].rearrange("p a b -> p (a b)"), "mixT", l, t)
                def head_ln(h):
                    ho = hT[:, 2 * h:2 * h + 2, :]
                    sq = secB[:, 0:2, :]
                    k0 = secB[:, 2, :]; k1 = secB[:, 3, :]
                    op("act", lambda e: e.activation(out=R(sq), in_=ho, func=AF.Square), r=["hT"], w=["secB"])
                    for mc in range(2):
                        op("pe", lambda e: e.matmul(psum[6][:, 0:T], ones, hT[:, 2 * h + mc, :], start=(mc == 0), stop=(mc == 1)), r=["con", "hT"], w=["ps6"])
                    for mc in range(2):
                        op("pe", lambda e: e.matmul(psum[6][:, T:2 * T], ones, sq[:, mc, :], start=(mc == 0), stop=(mc == 1), skip_group_check=True),
                           r=["con", "secB"], w=["ps6"])
                    op("act", lambda e: e.activation(out=tA[:], in_=psum[6][:, 0:T], func=AF.Copy, scale=1.0 / 256), r=["ps6"], w=["tA"])
                    op("dve", lambda e: e.tensor_tensor(tB[:], tA[:], tA[:], ALU.mult), r=["tA"], w=["tB"])
                    op("dve", lambda e: e.scalar_tensor_tensor(tB[:], psum[6][:, T:2 * T], 1.0 / 256, tB[:], ALU.mult, ALU.subtract), r=["ps6", "tB"], w=["tB"])
                    op("act", lambda e: e.activation(out=tB[:], in_=tB[:], func=AF.Sqrt, bias=EPS), r=["tB"], w=["tB"])
                    op("dve", lambda e: e.reciprocal(rstd[:], tB[:]), r=["tB"], w=["rstd"])
                    for mc in range(2):
                        n = 2 * h + mc
                        op("dve", lambda e: e.tensor_tensor(R(k0), hT[:, n, :], tA[:], ALU.subtract), r=["hT", "tA"], w=["secB"])
                        op("dve", lambda e: e.scalar_tensor_tensor(R(k0), k0, V(96 + n), rstd[:], ALU.mult, ALU.mult), r=["secB", "vecs", "rstd"], w=["secB"])
                        op("dve", lambda e: e.scalar_tensor_tensor(R(k1), secA[:, n, :], V(104 + n), k0, ALU.mult, ALU.add), r=["secA", "vecs", "secB"], w=["secB"])
                        op("dve", lambda e: e.tensor_tensor(R(mixT[:, 8 + n, :]), k1, secC[:, n, :], ALU.mult), r=["secB", "secC"], w=["mixTh%d" % h])

                for hp in range(2):
                    wt, wn = w_get()
                    wv = wt[:, 0:3072].rearrange("p (a b) -> p a b", b=256)
                    for hl in range(2):
                        h = hp * 2 + hl
                        for which, dstT in ((0, qT), (1, kT)):
                            for mc in range(2):
                                pt, pn = pnext()
                                for kc in range(2):
                                    op("pe", lambda e: e.matmul(pt[:, 0:T], R(wv[:, (hl * 3 + which) * 2 + kc, mc * 128:(mc + 1) * 128]), R(secA[:, 2 * h + kc, :]),
                                                                start=(kc == 0), stop=(kc == 1)), r=[wn, "secA"], w=[pn])
                                dn = "qT" if which == 0 else "kT"
                                op("act", lambda e: e.activation(out=R(dstT[:, mc, :]), in_=pt[:, 0:T], func=AF.Copy, scale=(1.0 if which == 0 else 1.0 / 16)),
                                   r=[pn], w=[dn])
                        def col(c, a):
                            return colsS[:, (c * 4 + a) * 4 + h:(c * 4 + a) * 4 + h + 1]

                        def cellA(c):
                            cs_ = slice(c * CH, (c + 1) * CH)
                            bi = c % 2
                            pt, pn = pnext8()
                            for kc in range(2):
                                op("pe", lambda e: e.matmul(pt[0:64, 0:256], R(secA[:, 2 * h + kc, cs_]), R(wv[:, (hl * 3 + 1) * 2 + kc, :]), start=(kc == 0), stop=(kc == 1)),
                                   r=[wn, "secA"], w=[pn])
                            op("act", lambda e: e.activation(out=R(Kw[bi][:]), in_=pt[0:64, 0:256], func=AF.Copy, scale=col(c, 1)), r=[pn, "colsS"], w=["Kw%d" % bi])
                            pt, pn = pnext8()
                            for kc in range(2):
                                op("pe", lambda e: e.matmul(pt[0:64, 0:256], R(minT[:, 2 * h + kc, 3 + c * CH:3 + (c + 1) * CH]), R(wv[:, (hl * 3 + 2) * 2 + kc, :]),
                                                            start=(kc == 0), stop=(kc == 1)), r=[wn, "minT"], w=[pn])
                            op("act", lambda e: e.activation(out=R(Va[bi][:, 0:256]), in_=pt[0:64, 0:256], func=AF.Copy), r=[pn], w=["Va%d" % bi])
                            pt, pn = pnext8()
                            for mc in range(2):
                                op("pe", lambda e: e.matmul(pt[0:64, 0:64], R(kT[:, mc, cs_]), R(qT[:, mc, cs_]), start=(mc == 0), stop=(mc == 1)), r=["kT", "qT"], w=[pn])
                            op("pe", lambda e: e.matmul(pt[0:64, 64:128], sel[:, h * 128:h * 128 + 64], mm[:, cs_], start=True, stop=False), r=["sel", "ig"], w=[pn])
                            op("pe", lambda e: e.matmul(pt[0:64, 64:128], ident[0:64, 0:64], causal, start=False, stop=True), r=["con"], w=[pn])
                            op("act", lambda e: e.activation(out=DTs[bi][:], in_=pt[0:64, 64:128], func=AF.Exp, scale=-1.0, bias=col(c, 0)), r=[pn, "colsS"], w=["DT%d" % bi])
                            op("dve", lambda e: e.tensor_tensor(R(SwTs[bi][:]), DTs[bi][:], pt[0:64, 0:64], ALU.mult), r=["DT%d" % bi, pn], w=["SwT%d" % bi])

                        def cellB(c):
                            cs_ = slice(c * CH, (c + 1) * CH)
                            bi = c % 2
                            pi_, pin = pnext8()
                            for mc in range(2):
                                op("pe", lambda e: e.matmul(pi_[0:64, 0:257], qT[:, mc, cs_], CT[:, 2 * h + mc, 0:257], start=(mc == 0), stop=(mc == 1)),
                                   r=["qT", "CT"], w=[pin])
                            pus = []
                            for mc in range(2):
                                pu, pun = pnext8()
                                op("pe", lambda e: e.matmul(pu[:, 0:258], R(Kw[bi][:, mc * 128:(mc + 1) * 128]), R(Va[bi][:, 0:258]), start=True, stop=True),
                                   r=["Kw%d" % bi, "Va%d" % bi], w=[pun])
                                pus.append((pu, pun))
                            pa, pan = pnext8()
                            op("pe", lambda e: e.matmul(pa[0:64, 0:258], R(SwTs[bi][:]), R(Va[bi][:, 0:258]), start=True, stop=True), r=["SwT%d" % bi, "Va%d" % bi], w=[pan])
                            op("act", lambda e: e.activation(out=sbA[:, 0:257], in_=pa[0:64, 0:257], func=AF.Copy), r=[pan], w=["sbA"])
                            op("dve", lambda e: e.scalar_tensor_tensor(nd[:, 0:257], pi_[0:64, 0:257], col(c, 2), sbA[:, 0:257], ALU.mult, ALU.add),
                               r=[pin, "colsS", "sbA"], w=["nd"])
                            for mc in range(2):
                                pu, pun = pus[mc]
                                op("dve", lambda e: e.scalar_tensor_tensor(CT[:, 2 * h + mc, 0:257], CT[:, 2 * h + mc, 0:257], decB[:, h * 4 + c:h * 4 + c + 1],
                                                                           pu[:, 0:257], ALU.mult, ALU.add), r=["CT", "decB", pun], w=["CT"])
                            op("act", lambda e: e.activation(out=dd[:, 1:2], in_=nd[:, 256:257], func=AF.Copy, scale=-1.0), r=["nd"], w=["dd"])
                            op("dve", lambda e: e.tensor_scalar(dd[:, 0:1], nd[:, 256:257], col(c, 3), dd[:, 1:2], ALU.max, ALU.max), r=["nd", "dd", "colsS"], w=["dd"])
                            op("dve", lambda e: e.reciprocal(dd[:, 1:2], dd[:, 0:1]), r=["dd"], w=["dd"])
                            op("act", lambda e: e.activation(out=hTMs[bi][:], in_=nd[:, 0:256], func=AF.Copy, scale=dd[:, 1:2]), r=["nd", "dd"], w=["hTM%d" % bi])

                        def cellC(c):
                            cs_ = slice(c * CH, (c + 1) * CH)
                            bi = c % 2
                            pt, pn = pnext8()
                            for mc in range(2):
                                op("pe", lambda e: e.transpose(pt[:, mc * 64:(mc + 1) * 64], hTMs[bi][:, mc * 128:(mc + 1) * 128], ident[0:64, 0:64]), r=["hTM%d" % bi, "con"], w=[pn])
                            for mc in range(2):
                                op("dve", lambda e: e.tensor_tensor(R(hT[:, 2 * h + mc, cs_]), pt[:, mc * 64:(mc + 1) * 64], mixT[:, 8 + 2 * h + mc, cs_], ALU.mult),
                                   r=[pn, "mixTh%d" % h], w=["hT"])

                        cellA(0)
                        for c in range(NCH):
                            if c + 1 < NCH:
                                cellA(c + 1)
                            cellB(c)
                            if c >= 1:
                                cellC(c - 1)
                        cellC(NCH - 1)
                        head_ln(h)
                    w_rel()
                dbg("ho", hT[:].rearrange("p a b -> p (a b)"), "hT", l, t)
                op("pool", lambda e: e.tensor_copy(R(minT[:, :, 0:3]), minT[:, :, T:T + 3]), r=["minT"], w=["minT"])
                dbg("mix1", mixT[:, 8:16, :].rearrange("p a b -> p (a b)"), "mixTh3", l, t)
                for i in range(4):
                    wt, wn = w_get()
                    wv = wt[:, 0:4096].rearrange("p (a b) -> p a b", a=16)
                    for ml in range(2):
                        n = i * 2 + ml
                        pt, pn = pnext()
                        for kc in range(16):
                            op("pe", lambda e: e.matmul(pt[:, 0:T], R(wv[:, kc, ml * 128:(ml + 1) * 128]), R(mixT[:, kc, :]), start=(kc == 0), stop=(kc == 15)),
                               r=[wn, "mixT", "mixTh0", "mixTh1", "mixTh2", "mixTh3"], w=[pn])
                        op("dve", lambda e: e.scalar_tensor_tensor(xt[:, n, :], pt[:, 0:T], modv[:, 16 + n:17 + n], xt[:, n, :], ALU.mult, ALU.add),
                           r=[pn, "modv", "xt"], w=["xt"])
                    w_rel()
                dbg("xo", xt[:].rearrange("p a b -> p (a b)"), "xt", l, t)
                if l < L - 1:
                    sc.dma(xs_h.ap().rearrange("p (k s) -> p k s", k=8)[:, :, t0:t0 + T], xt[:], r=["xt"], w=["d_xs"], stream="xt")
                else:
                    rms_rstd(xt, "xt", secB, "secB", 1.0 / D)
                    for n in range(8):
                        op("dve", lambda e: e.scalar_tensor_tensor(R(secB[:, n, :]), xt[:, n, :], fvec[:, n:n + 1], rstd[:], ALU.mult, ALU.mult),
                           r=["xt", "fvec", "rstd"], w=["secB"])
                    for b in range(2):
                        for k4 in range(2):
                            pt, pn = pnext()
                            for kk in range(4):
                                kc = k4 * 4 + kk
                                op("pe", lambda e: e.transpose(pt[:, kk * 128:(kk + 1) * 128], secB[:, kc, b * 128:(b + 1) * 128], ident), r=["secB", "con"], w=[pn])
                            evac(k4, xin[:, b, k4 * 512:(k4 + 1) * 512], pt[:, 0:512], [pn], ["secC"])
                    sc.dma(y_h.ap()[t0:t0 + T, :].rearrange("(b p) d -> p b d", p=128), xin, r=["secC"], w=["d_y"], stream="xin")
        sc.finish("sp")
        print("instructions:", sc.ninst, sc.cnt)
    nc._dbgnames = dbgnames
    return nc


_CACHE = {}


def _consts():
    con = np.zeros((128, 128 + 128 + 128 + 3 * T + 64), np.float32)
    con[:, 0:128] = np.eye(128, dtype=np.float32)
    con[:, 128:256] = 1.0
    g8 = (np.arange(128) // 16) % 2
    gl = np.arange(128) // 64
    con[:, 256:384] = (g8[:, None] == gl[None, :]).astype(np.float32)
    tt_ = np.arange(T)
    con[:, 384:384 + T] = (tt_ % SL != 0).astype(np.float32)[None, :]
    con[:, 384 + T:384 + 2 * T] = (tt_ % CH != 0).astype(np.float32)[None, :]
    con[:, 384 + 2 * T:384 + 3 * T] = np.where(tt_ % CH == 0, -1e30, 0.0).astype(np.float32)[None, :]
    s_ = np.arange(64)
    con[0:64, 384 + 3 * T:384 + 3 * T + 64] = np.where(s_[None, :] >= s_[:, None], 0.0, 1e30).astype(np.float32)
    sel = np.zeros((4, 4, 128), np.float32)
    for h in range(4):
        sel[h, h, :] = 1.0
    return con, sel.reshape(4, 512)


def kernel(**inp):
    f = lambda k: np.asarray(inp[k], np.float32)
    x = f("x"); c = f("c")
    L = DEPTH
    vecs = np.zeros((L, 128, NV), np.float32)
    for l in range(L):
        vecs[l, :, 0:8] = fm(f("norm_gain")[l], 8)
        vecs[l, :, 8:32] = fm(f("b_mod")[l], 24)
        vecs[l, :, 32:40] = fm(f("ssm_d")[l], 8)
        vecs[l, :, 40:48] = fm(f("ssm_b_glu")[l], 8)
        vecs[l, :, 48:56] = fm(f("ssm_out_gain")[l], 8)
        for w in range(4):
            vecs[l, :, 56 + w * 8:64 + w * 8] = fm(f("m_conv_w")[l, w], 8)
        vecs[l, :, 88:96] = fm(f("m_conv_b")[l], 8)
        vecs[l, :, 96:104] = fm(f("m_norm_gain")[l], 8)
        vecs[l, :, 104:112] = fm(f("m_skip")[l], 8)
    fvec = fm(f("final_gain"), 8)
    gb = np.stack([f("m_b_igate"), f("m_b_fgate")], axis=-1)
    wg = np.ascontiguousarray(f("m_w_gates").reshape(L, 24, 128, 8).transpose(0, 2, 1, 3)).reshape(L, 128, 192)
    qkv = np.stack([f("m_wq"), f("m_wk"), f("m_wv")], axis=1)
    qkv = qkv.reshape(L, 3, 4, 2, 128, 256).transpose(0, 4, 2, 1, 3, 5)
    wqkv = np.ascontiguousarray(qkv).reshape(L, 128, 24 * 256)
    con, sel = _consts()
    shared = {
        "vecs": vecs, "fvec": fvec, "gb": np.ascontiguousarray(gb), "wg": wg, "wqkv": wqkv,
        "w_mod": f("w_mod"), "w_in": f("w_in"), "w_glu": f("ssm_w_glu"), "w_out": f("w_out"),
        "lam_re": f("ssm_lambda_re"), "lam_im": f("ssm_lambda_im"), "log_dt": f("ssm_log_dt"),
        "b_re": f("ssm_b_re"), "b_im": f("ssm_b_im"), "c_re": f("ssm_c_re"), "c_im": f("ssm_c_im"),
        "consts": con, "sel": sel,
    }
    B = x.shape[0]
    in_maps = []
    for b in range(B):
        m = dict(shared)
        m["x"] = np.ascontiguousarray(x[b])
        m["cT"] = fm(c[b], 8)
        in_maps.append(m)
    if "nc" not in _CACHE:
        _CACHE["nc"] = build_program()
    res = run_bass_kernel_spmd(_CACHE["nc"], in_maps, core_ids=list(range(B)))
    if DEBUG:
        _CACHE["dbg"] = {n: np.asarray(res.results[0]["dbg"])[i] for i, n in enumerate(_CACHE["nc"]._dbgnames)}
    return np.stack([np.asarray(r["y"], np.float32) for r in res.results], axis=0)
```

```python
import numpy as np
from contextlib import ExitStack
import concourse.bass as bass
import concourse.mybir as mybir
from concourse.bass_utils import run_bass_kernel_spmd

F32 = mybir.dt.float32
F32R = mybir.dt.float32r


def R(ap):
    return ap.bitcast(F32R)
AF = mybir.ActivationFunctionType
ALU = mybir.AluOpType

S = 2048
D = 1024
T = 256
NT = S // T
SL = 2
NS5 = max(1, SL // 2)
PPS = 32 // NS5
FPS = 8 // NS5
NB = T // SL
CH = 64
NCH = T // CH
DEPTH = 2
EPS = 1e-6
NV = 112
CHAIN_ENG = "dve"
CHAIN_STEP = 3
DEBUG = False
DBG_L = 0
DBG_T = 0
NDBG = 16
SYNC_SAME = {"pe": False, "act": True, "dve": True, "pool": False, "sp": False}


class Sched:
    def __init__(self, nc, es):
        self.nc = nc
        self.es = es
        self.eng = {"pe": nc.tensor, "act": nc.scalar, "dve": nc.vector, "pool": nc.gpsimd, "sp": nc.sync}
        self.sem = {k: es.enter_context(nc.semaphore("sem_" + k)) for k in self.eng}
        self.cnt = {k: 0 for k in self.eng}
        self.waited = {k: {} for k in self.eng}
        self.lastw = {}
        self.readers = {}
        self.streams = {}
        self.scnt = {}
        self.ninst = 0

    def _deps(self, r, w):
        deps = []
        for x in r:
            if x in self.lastw:
                deps.append(self.lastw[x])
        for x in w:
            if x in self.lastw:
                deps.append(self.lastw[x])
            deps.extend(self.readers.get(x, ()))
        return deps

    def _wait(self, en, deps, nosync=False):
        e = self.eng[en]
        best = {}
        for (sname, sem, val) in deps:
            if sname == en and (nosync or not SYNC_SAME[en]):
                continue
            if self.waited[en].get(sname, 0) >= val:
                continue
            if best.get(sname, (None, 0))[1] < val:
                best[sname] = (sem, val)
        for sname, (sem, val) in best.items():
            e.wait_ge(sem, val)
            self.waited[en][sname] = val
            self.ninst += 1

    def _record(self, dep, r, w):
        for x in w:
            self.lastw[x] = dep
            self.readers[x] = []
        for x in r:
            self.readers.setdefault(x, []).append(dep)

    def op(self, en, fn, r=(), w=(), nosync=False):
        self._wait(en, self._deps(r, w), nosync)
        inst = fn(self.eng[en])
        self.cnt[en] += 1
        inst.then_inc(self.sem[en], 1)
        self.ninst += 1
        self._record((en, self.sem[en], self.cnt[en]), r, w)

    def dma(self, out, in_, r=(), w=(), stream=None, q="sp", **kw):
        if stream not in self.streams:
            self.streams[stream] = self.es.enter_context(self.nc.semaphore("dq_" + stream))
            self.scnt[stream] = 0
        self._wait(q, self._deps(r, w))
        self.scnt[stream] += 16
        self.eng[q].dma_start(out=out, in_=in_, **kw).then_inc(self.streams[stream], 16)
        self.ninst += 1
        self._record(("dq_" + stream, self.streams[stream], self.scnt[stream]), r, w)

    def sync_stream(self, stream, resources):
        dep = ("dq_" + stream, self.streams[stream], self.scnt[stream])
        for x in resources:
            self.lastw[x] = dep

    def merge(self, dst, srcs):
        for x in srcs:
            if x in self.lastw:
                self.readers.setdefault(dst, []).append(self.lastw[x])
            self.readers.setdefault(dst, []).extend(self.readers.get(x, ()))

    def barrier(self):
        deps = list(self.lastw.values())
        for v in self.readers.values():
            deps.extend(v)
        for en in self.eng:
            self._wait(en, deps)

    def finish(self, en="sp"):
        deps = list(self.lastw.values())
        for v in self.readers.values():
            deps.extend(v)
        self._wait(en, deps)


def fm(v, n):
    return np.ascontiguousarray(np.asarray(v, np.float32).reshape(n, 128).T)


def build_program():
    nc = bass.Bass("TRN2", target_bir_lowering=False)
    L = DEPTH

    def din(name, shape):
        return nc.dram_tensor(name, list(shape), F32, kind="ExternalInput")

    x_h = din("x", [S, D])
    cT_h = din("cT", [128, 8])
    vecs_h = din("vecs", [L, 128, NV])
    fvec_h = din("fvec", [128, 8])
    gb_h = din("gb", [L, 4, 2])
    wg_h = din("wg", [L, 128, 24 * 8])
    wqkv_h = din("wqkv", [L, 128, 24 * 256])
    wmod_h = din("w_mod", [L, D, 3 * D])
    win_h = din("w_in", [L, D, 5 * D])
    wglu_h = din("w_glu", [L, D, D])
    wout_h = din("w_out", [L, 2 * D, D])
    lre_h = din("lam_re", [L, 64, 64])
    lim_h = din("lam_im", [L, 64, 64])
    ldt_h = din("log_dt", [L, 64])
    bre_h = din("b_re", [L, 64, 64, 16])
    bim_h = din("b_im", [L, 64, 64, 16])
    cre_h = din("c_re", [L, 64, 16, 64])
    cim_h = din("c_im", [L, 64, 16, 64])
    NCON = 128 + 128 + 128 + 3 * T + 64
    con_h = din("consts", [128, NCON])
    sel_h = din("sel", [4, 4 * 128])
    y_h = nc.dram_tensor("y", [S, D], F32, kind="ExternalOutput")
    dbg_h = nc.dram_tensor("dbg", [NDBG, 128, 8 * T + 32], F32, kind="ExternalOutput") if DEBUG else None
    dbgi = [0]
    dbgnames = []
    xs_h = nc.dram_tensor("xs", [128, 8 * S], F32, kind="Internal")
    winT_h = nc.dram_tensor("winT_d", [L, 128, 16 * SL * 128], F32, kind="Internal")
    woutS_h = nc.dram_tensor("woutS_d", [L, 128, 64 * SL * 32], F32, kind="Internal")

    AP = bass.AP
    with ExitStack() as es:
        sc = Sched(nc, es)
        op = sc.op

        def sb(name, shape, st=es):
            return st.enter_context(nc.sbuf_tensor("s_" + name, list(shape), F32))

        con = sb("con", [128, NCON])
        ident = con[:, 0:128]
        ones = con[:, 128:256]
        pmask = con[:, 256:384]
        mask01 = con[:, 384:384 + T]
        cmask = con[:, 384 + T:384 + 2 * T]
        negmask = con[:, 384 + 2 * T:384 + 3 * T]
        causal = con[0:64, 384 + 3 * T:384 + 3 * T + 64]
        sel = sb("sel", [4, 4 * 128])
        vecs = sb("vecs", [128, L, NV])
        fvec = sb("fvec", [128, 8])
        gb = sb("gb", [4, L, 2])
        wg = sb("wg", [128, L, 24 * 8])
        cT = sb("cT", [128, 8])
        AL3 = sb("AL3", [128, L, 2, 2, 32])
        sc.dma(con[:], con_h.ap(), w=["con"], stream="c0")
        sc.dma(sel[:], sel_h.ap(), w=["sel"], stream="c0")
        sc.dma(fvec[:], fvec_h.ap(), w=["fvec"], stream="c0")
        sc.dma(cT[:], cT_h.ap(), w=["cT"], stream="c0")
        for l in range(L):
            sc.dma(vecs[:, l, :], vecs_h.ap()[l], w=["vecs%d" % l], stream="c0")
            sc.dma(gb[:, l, :], gb_h.ap()[l], w=["gb%d" % l], stream="c0")
            sc.dma(wg[:, l, :], wg_h.ap()[l], w=["wg%d" % l], stream="c0")

        sc.sync_stream("c0", ["con", "sel", "fvec", "cT", "vecs", "gb", "wg"])
        psum = [es.enter_context(nc.psum_tensor("ps%d" % i, [128, 512], F32)) for i in range(8)]
        pctr = [0]

        def pnext():
            i = pctr[0] % 6
            pctr[0] += 1
            return psum[i], "ps%d" % i

        p8ctr = [0]

        def pnext8():
            i = p8ctr[0] % 8
            p8ctr[0] += 1
            return psum[i], "ps%d" % i

        with ExitStack() as ps_:
            def sp(name, shape):
                return sb(name, shape, ps_)
            lamre = sp("lamre", [128, 32]); lamim = sp("lamim", [128, 32]); dtb = sp("dtb", [128, 32])
            breT = sp("breT", [128, 32, 16]); bimT = sp("bimT", [128, 32, 16])
            cnre = sp("cnre", [128, 8, 64]); cnim = sp("cnim", [128, 8, 64])
            tq = [sp("tq%d" % i, [128, 32]) for i in range(26)]
            apw = sp("apw", [128, 5, 2, 32]); aiw = sp("aiw", [128, 5, 2, 32])
            tb = [sp("tb%d" % i, [128, 32, 16]) for i in range(4)]
            bpad = [sp("bpad%d" % i, [128, 32, 32]) for i in range(2)]
            cpad = [sp("cpad%d" % i, [128, 32, 32]) for i in range(2)]
            xm = sp("xm", [128, 128])
            tcb = [sp("tc%d" % i, [128, 32, 32]) for i in range(4)]
            stW = sp("stW", [128, 16 * SL, 128])
            stO = sp("stO", [128, 64 * SL, 32])

            def tt(o, a, b, o_, rr, ww, en="dve"):
                op(en, lambda e: e.tensor_tensor(o, a, b, o_), r=rr, w=ww)

            def ts(o, a, s1, s2, o0, o1, rr, ww, en="dve"):
                if o1 is None:
                    op(en, lambda e: e.tensor_scalar(o, a, s1, None, o0), r=rr, w=ww)
                else:
                    op(en, lambda e: e.tensor_scalar(o, a, s1, s2, o0, o1), r=rr, w=ww)

            for l in range(L):
                sc.dma(lamre[:], AP(lre_h, l * 4096, [[1, 128], [128, 32]]), w=["lamre"], stream="pc%d" % l,
                       allow_slow_non_contiguous=True)
                sc.dma(lamim[:], AP(lim_h, l * 4096, [[1, 128], [128, 32]]), w=["lamim"], stream="pc%d" % l,
                       allow_slow_non_contiguous=True)
                for gl in range(2):
                    sc.dma(dtb[gl * 64:(gl + 1) * 64, :], AP(ldt_h, l * 64 + gl, [[0, 64], [2, 32]]), w=["dtb%d" % gl],
                           stream="pc%d" % l, allow_slow_non_contiguous=True)
                sc.dma(breT[:], AP(bre_h, l * 65536, [[16, 128], [2048, 32], [1, 16]]), w=["breT"], stream="pc%d" % l)
                sc.dma(bimT[:], AP(bim_h, l * 65536, [[16, 128], [2048, 32], [1, 16]]), w=["bimT"], stream="pc%d" % l)
                sc.dma(cnre[:], AP(cre_h, l * 65536, [[64, 128], [8192, 8], [1, 64]]), w=["cnre"], stream="pc%d" % l)
                sc.dma(cnim[:], AP(cim_h, l * 65536, [[64, 128], [8192, 8], [1, 64]]), w=["cnim"], stream="pc%d" % l)
                sc.sync_stream("pc%d" % l, ["lamre", "lamim", "dtb", "breT", "bimT", "cnre", "cnim"])
                names = ["tq%d" % i for i in range(24)]
                (dt_, lr, th, r1, rinv, xx, x2, sn, cs, cc, ss, cs2, are, aim, am1, nr, ni, den, cfr, cfi, t0, t1, t2, t3) = \
                    [(tq[i][:], names[i]) for i in range(24)]

                def A(o, a, b, o_):
                    tt(o[0], a[0], b[0], o_, [a[1], b[1]], [o[1]])

                def Sx(o, a, s1, s2, o0, o1=None):
                    ts(o[0], a[0], s1, s2, o0, o1, [a[1]], [o[1]])

                LR = (lamre[:], "lamre"); LI = (lamim[:], "lamim")
                op("act", lambda e: e.activation(out=dt_[0], in_=dtb[:], func=AF.Exp), r=["dtb"], w=[dt_[1]])
                A(lr, LR, dt_, ALU.mult)
                A(th, LI, dt_, ALU.mult)
                op("act", lambda e: e.activation(out=r1[0], in_=lr[0], func=AF.Exp), r=[lr[1]], w=[r1[1]])
                op("act", lambda e: e.activation(out=rinv[0], in_=lr[0], func=AF.Exp, scale=-1.0), r=[lr[1]], w=[rinv[1]])
                Sx(xx, th, 1.0 / 32, None, ALU.mult)
                A(x2, xx, xx, ALU.mult)
                Sx(sn, x2, -1.0 / 5040, 1.0 / 120, ALU.mult, ALU.add)
                A(sn, sn, x2, ALU.mult); Sx(sn, sn, -1.0 / 6, None, ALU.add)
                A(sn, sn, x2, ALU.mult); Sx(sn, sn, 1.0, None, ALU.add)
                A(sn, sn, xx, ALU.mult)
                Sx(cs, x2, 1.0 / 40320, -1.0 / 720, ALU.mult, ALU.add)
                A(cs, cs, x2, ALU.mult); Sx(cs, cs, 1.0 / 24, None, ALU.add)
                A(cs, cs, x2, ALU.mult); Sx(cs, cs, -0.5, None, ALU.add)
                A(cs, cs, x2, ALU.mult); Sx(cs, cs, 1.0, None, ALU.add)
                for _ in range(5):
                    A(cc, cs, cs, ALU.mult); A(ss, sn, sn, ALU.mult); A(cs2, cs, sn, ALU.mult)
                    A(cs, cc, ss, ALU.subtract); Sx(sn, cs2, 2.0, None, ALU.mult)
                A(are, r1, cs, ALU.mult); A(aim, r1, sn, ALU.mult)
                Sx(am1, are, -1.0, None, ALU.add)
                A(t0, am1, LR, ALU.mult); A(t1, aim, LI, ALU.mult); A(nr, t0, t1, ALU.add)
                A(t0, aim, LR, ALU.mult); A(t1, am1, LI, ALU.mult); A(ni, t0, t1, ALU.subtract)
                A(t0, LR, LR, ALU.mult); A(t1, LI, LI, ALU.mult); A(den, t0, t1, ALU.add)
                op("dve", lambda e: e.reciprocal(den[0], den[0]), r=[den[1]], w=[den[1]])
                A(cfr, nr, den, ALU.mult); A(cfi, ni, den, ALU.mult)

                def PW(t_, k, ri):
                    return (t_[:, k, ri, :], t_.name if hasattr(t_, "name") else "pw")
                APW = lambda k, ri: (apw[:, k, ri, :], "apw")
                AIW = lambda k, ri: (aiw[:, k, ri, :], "aiw")
                op("dve", lambda e: e.tensor_copy(apw[:, 1, 0, :], are[0]), r=[are[1]], w=["apw"])
                op("dve", lambda e: e.tensor_copy(apw[:, 1, 1, :], aim[0]), r=[aim[1]], w=["apw"])
                A(AIW(1, 0), rinv, cs, ALU.mult)
                A(t0, rinv, sn, ALU.mult); Sx(AIW(1, 1), t0, -1.0, None, ALU.mult)

                def cmul(o_re, o_im, a_re, a_im, b_re, b_im):
                    A(t0, a_re, b_re, ALU.mult); A(t1, a_im, b_im, ALU.mult)
                    A(t2, a_re, b_im, ALU.mult); A(t3, a_im, b_re, ALU.mult)
                    A(o_re, t0, t1, ALU.subtract); A(o_im, t2, t3, ALU.add)
                for k in range(2, 5):
                    cmul(APW(k, 0), APW(k, 1), APW(k - 1, 0), APW(k - 1, 1), APW(1, 0), APW(1, 1))
                    cmul(AIW(k, 0), AIW(k, 1), AIW(k - 1, 0), AIW(k - 1, 1), AIW(1, 0), AIW(1, 1))
                op("dve", lambda e: e.tensor_copy(AL3[:, l, 0, 0, :], apw[:, SL, 0, :]), r=["apw"], w=["aL"])
                op("dve", lambda e: e.tensor_copy(AL3[:, l, 0, 1, :], apw[:, SL, 0, :]), r=["apw"], w=["aL"])
                op("dve", lambda e: e.tensor_copy(AL3[:, l, 1, 0, :], apw[:, SL, 1, :]), r=["apw"], w=["aL"])
                op("dve", lambda e: e.tensor_scalar(AL3[:, l, 1, 1, :], apw[:, SL, 1, :], -1.0, None, ALU.mult), r=["apw"], w=["aL"])

                for ri, cn_ in enumerate((cnre, cnim)):
                    for fc in range(8):
                        op("dve", lambda e: e.tensor_tensor(xm[:].rearrange("p (g q) -> p g q", g=2),
                                                            cn_[:, fc, :].unsqueeze(1).to_broadcast([128, 2, 64]),
                                                            pmask.rearrange("p (g q) -> p g q", g=2), ALU.mult),
                           r=[cn_.name if False else ("cnre" if ri == 0 else "cnim"), "con"], w=["xm"])
                        pt, pn = pnext()
                        op("pe", lambda e: e.transpose(pt[:, 0:128], xm[:], ident), r=["xm", "con"], w=[pn])
                        op("act", lambda e: e.activation(
                            out=cpad[ri][:, fc * 4:(fc + 1) * 4, :].rearrange("p a b -> p (a b)"),
                            in_=pt[:, 0:128], func=AF.Copy), r=[pn], w=["cpad%d" % ri])
                for j in range(SL):
                    Gre = (tq[24][:], "tq24"); Gim = (tq[25][:], "tq25")
                    cmul(Gre, Gim, AIW(j + 1, 0), AIW(j + 1, 1), cfr, cfi)
                    gb_re = Gre[0].unsqueeze(2).to_broadcast([128, 32, 16])
                    gb_im = Gim[0].unsqueeze(2).to_broadcast([128, 32, 16])
                    tt(tb[0][:], breT[:], gb_re, ALU.mult, ["breT", Gre[1]], ["tb0"])
                    tt(tb[1][:], bimT[:], gb_im, ALU.mult, ["bimT", Gim[1]], ["tb1"])
                    tt(tb[2][:], bimT[:], gb_re, ALU.mult, ["bimT", Gre[1]], ["tb2"])
                    tt(tb[3][:], breT[:], gb_im, ALU.mult, ["breT", Gim[1]], ["tb3"])
                    for ri in range(2):
                        op("pool", lambda e: e.memset(bpad[ri][:], 0.0), w=["bpad%d" % ri])
                    for gl in range(2):
                        pr = slice(gl * 64, (gl + 1) * 64)
                        cr = slice(gl * 16, (gl + 1) * 16)
                        tt(bpad[0][pr, :, cr], tb[0][pr], tb[1][pr], ALU.subtract, ["tb0", "tb1"], ["bpad0"])
                        tt(bpad[1][pr, :, cr], tb[2][pr], tb[3][pr], ALU.add, ["tb2", "tb3"], ["bpad1"])
                    for ri in range(2):
                        for fc in range(8):
                            pt, pn = pnext()
                            op("pe", lambda e: e.transpose(
                                pt[:, 0:128], bpad[ri][:, fc * 4:(fc + 1) * 4, :].rearrange("p a b -> p (a b)"), ident),
                               r=["bpad%d" % ri, "con"], w=[pn])
                            idx = (fc * SL + j) * 2 + ri
                            op("act", lambda e: e.activation(out=stW[:, idx, :], in_=pt[:, 0:128], func=AF.Copy),
                               r=[pn], w=["stW"])
                    a_re = apw[:, j + 1, 0, :].unsqueeze(2).to_broadcast([128, 32, 32])
                    a_im = apw[:, j + 1, 1, :].unsqueeze(2).to_broadcast([128, 32, 32])
                    tt(tcb[0][:], cpad[0][:], a_re, ALU.mult, ["cpad0", "apw"], ["tc0"])
                    tt(tcb[1][:], cpad[1][:], a_im, ALU.mult, ["cpad1", "apw"], ["tc1"])
                    tt(tcb[2][:], cpad[0][:], a_im, ALU.mult, ["cpad0", "apw"], ["tc2"])
                    tt(tcb[3][:], cpad[1][:], a_re, ALU.mult, ["cpad1", "apw"], ["tc3"])
                    stOv = stO[:].rearrange("p (q j r) c -> p q j r c", j=SL, r=2)
                    tt(stOv[:, :, j, 0, :], tcb[0][:], tcb[1][:], ALU.subtract, ["tc0", "tc1"], ["stO"])
                    tt(tcb[2][:], tcb[2][:], tcb[3][:], ALU.add, ["tc2", "tc3"], ["tc2"])
                    ts(stOv[:, :, j, 1, :], tcb[2][:], -1.0, None, ALU.mult, None, ["tc2"], ["stO"])
                sc.dma(winT_h.ap()[l], stW[:].rearrange("p a b -> p (a b)"), r=["stW"], w=["d_winT%d" % l], stream="pcwa%d" % l)
                sc.dma(woutS_h.ap()[l], stO[:].rearrange("p a b -> p (a b)"), r=["stO"], w=["d_wout%d" % l], stream="pcwb%d" % l)

        sc.barrier()
        xt = sb("xt", [128, 8, T]); hT = sb("hT", [128, 8, T])
        secA = sb("secA", [128, 8, T]); secB = sb("secB", [128, 8, T]); secC = sb("secC", [128, 8, T])
        xin = secC[:].rearrange("p a b -> p (a b)").rearrange("p (b d) -> p b d", b=2)
        minT = sb("minT", [128, 8, T + 4]); mixT = sb("mixT", [128, 16, T])
        wb = [sb("wb0", [128, 4096])]
        cs = [sb("cs%d" % i, [128, 4096]) for i in range(3)]
        xprev = sb("xprev", [128, 2, 32, NB + 1])
        CT = sb("CT", [128, 8, 260])
        zl = [[sb("zl%d%d" % (a, b), [128, T]) for b in range(2)] for a in range(2)]
        tA = sb("tA", [128, T]); tB = sb("tB", [128, T]); rstd = sb("rstd", [128, T])
        qT = sb("qT", [128, 2, T]); kT = sb("kT", [128, 2, T])
        Kw = [sb("Kw%d" % i, [64, 256]) for i in range(2)]
        Va = [sb("Va%d" % i, [64, 260]) for i in range(2)]
        DTs = [sb("DT%d" % i, [64, 64]) for i in range(2)]; SwTs = [sb("SwT%d" % i, [64, 64]) for i in range(2)]; sbA = sb("sbA", [64, 260]); nd = sb("nd", [64, 260])
        hTMs = [sb("hTM%d" % i, [64, 256]) for i in range(2)]; dd = sb("dd", [64, 2])
        modv = sb("modv", [128, 24]); g1 = sb("g1", [128, 8]); csil = sb("csil", [128, 8, 2])
        ig = sb("ig", [4, T]); lf = sb("lf", [4, T]); bcs = sb("bcs", [4, T]); gg = sb("gg", [4, T]); Mx = lf
        mm = ig; rows = sb("rows", [4, 4, T]); mprev = sb("mprev", [4, NCH + 1]); nml = sb("nml", [4, NCH])
        colsS = sb("colsS", [64, 64]); decB = sb("decB", [128, 16])

        wl = []
        wstate = {"issued": 0, "released": 0, "next": 0}
        wcast = set()

        NCS = len(cs)
        cfree = [True] * NCS
        bfree = [True]
        cptr = [0]
        pptr = [0]
        slot_of = {}
        outstanding = []

        def w_top():
            cast_ids = wstate.setdefault("cast_ids", [i for i in range(len(wl)) if wl[i][3]])
            plain_ids = wstate.setdefault("plain_ids", [i for i in range(len(wl)) if not wl[i][3]])
            while cptr[0] < len(cast_ids) and cfree[cptr[0] % NCS]:
                i = cast_ids[cptr[0]]
                s_ = cptr[0] % NCS
                src, shp, rdeps, _c = wl[i]
                dst = cs[s_][:, 0:int(np.prod(shp))]
                if len(shp) == 2:
                    dst = dst.rearrange("p (a b) -> p a b", a=shp[0])
                sc.dma(R(dst), src, r=rdeps, w=["cs%d" % s_], stream="cs%d" % s_, q="pool")
                cfree[s_] = False
                slot_of[i] = ("cs", s_)
                cptr[0] += 1
            while pptr[0] < len(plain_ids) and bfree[0]:
                i = plain_ids[pptr[0]]
                src, shp, rdeps, _c = wl[i]
                dst = wb[0][:, 0:int(np.prod(shp))]
                if len(shp) == 2:
                    dst = dst.rearrange("p (a b) -> p a b", a=shp[0])
                sc.dma(dst, src, r=rdeps, w=["wb0"], stream="wb0")
                bfree[0] = False
                slot_of[i] = ("wb", 0)
                pptr[0] += 1

        def w_get():
            i = wstate["next"]
            wstate["next"] += 1
            w_top()
            assert i in slot_of, "weight not issued"
            outstanding.append(i)
            kind, s_ = slot_of[i]
            return (cs[s_], "cs%d" % s_) if kind == "cs" else (wb[0], "wb0")

        def w_rel():
            i = outstanding.pop(0)
            kind, s_ = slot_of[i]
            if kind == "cs":
                cfree[s_] = True
            else:
                bfree[0] = True
            w_top()

        def wsec(h, l, k0, nk, c0, ncol):
            t_ = h.ap()[l]
            return t_[k0 * 128:(k0 + nk) * 128, c0:c0 + ncol].rearrange("(k p) n -> p k n", p=128)

        for l in range(L):
            for i in range(6):
                wl.append((wsec(wmod_h, l, 0, 8, i * 512, 512), (8, 512), [], False))
            for t in range(NT):
                for i in range(4):
                    wl.append((wsec(win_h, l, 0, 8, i * 512, 512), (8, 512), [], True))
                for hf in range(NS5):
                    wl.append((winT_h.ap()[l][:, hf * 4096:(hf + 1) * 4096], (4096,), ["d_winT%d" % l], True))
                    wl.append((woutS_h.ap()[l][:, hf * 4096:(hf + 1) * 4096], (4096,), ["d_wout%d" % l], False))
                for i in range(4, 10):
                    wl.append((wsec(win_h, l, 0, 8, i * 512, 512), (8, 512), [], True))
                for hp in range(2):
                    wl.append((wqkv_h.ap()[l][:, hp * 3072:(hp + 1) * 3072], (3072,), [], True))
                for hf in range(NS5):
                    wl.append((woutS_h.ap()[l][:, hf * 4096:(hf + 1) * 4096], (4096,), ["d_wout%d" % l], False))
                for i in range(2):
                    wl.append((wsec(wglu_h, l, 0, 8, i * 512, 512), (8, 512), [], True))
                for hp in range(2):
                    wl.append((wqkv_h.ap()[l][:, hp * 3072:(hp + 1) * 3072], (3072,), [], True))
                for i in range(4):
                    wl.append((wsec(wout_h, l, 0, 16, i * 256, 256), (16, 256), [], True))

        def dbg(name, ap, res, l, t, n=8 * T):
            if not DEBUG or l != DBG_L or t != DBG_T:
                return
            i = dbgi[0]; dbgi[0] += 1
            dbgnames.append(name)
            sc.dma(dbg_h.ap()[i][0:ap.shape[0], 0:n], ap, r=[res], w=["d_dbg"], stream="dbg")

        chain = [None]

        def chain_adv(k):
            if chain[0] is None:
                return
            for _ in range(k):
                try:
                    next(chain[0])
                except StopIteration:
                    chain[0] = None
                    return

        def evac(i, out, in_, r, w, func=AF.Copy, force_act=False, **kw):
            if func == AF.Copy and i % 2 == 0 and not force_act:
                op("dve", lambda e: e.tensor_copy(out, in_), r=r, w=w)
            else:
                op("act", lambda e: e.activation(out=out, in_=in_, func=func, **kw), r=r, w=w)

        def rms_rstd(src, srcn, tmp8, tmp8n, scale):
            op("act", lambda e: e.activation(out=R(tmp8[:]), in_=(src if isinstance(src, bass.AP) else src[:]), func=AF.Square), r=[srcn], w=[tmp8n])
            op("pe", lambda e: e.matmul(psum[6][:, 0:T], ones, tmp8[:, 0, :], start=True, stop=False), r=["con", tmp8n], w=["ps6"])
            for kc in range(1, 8):
                op("pe", lambda e: e.matmul(psum[6][:, 0:T], ones, tmp8[:, kc, :], start=False, stop=(kc == 7)),
                   r=["con", tmp8n], w=["ps6"])
            op("act", lambda e: e.activation(out=tA[:], in_=psum[6][:, 0:T], func=AF.Sqrt, scale=scale, bias=EPS),
               r=["ps6"], w=["tA"])
            op("dve", lambda e: e.reciprocal(rstd[:], tA[:]), r=["tA"], w=["rstd"])

        for l in range(L):
            V = lambda c0, n=1: vecs[:, l, c0:c0 + n]
            op("act", lambda e: e.activation(out=csil[:, :, 0], in_=cT[:], func=AF.Silu), r=["cT"], w=["csil"])
            op("act", lambda e: e.activation(out=csil[:, :, 1], in_=cT[:], func=AF.Silu), r=["cT"], w=["csil"])
            for i in range(6):
                wt, wn = w_get()
                wv = wt[:, 0:4096].rearrange("p (a b) -> p a b", a=8)
                for nl in range(4):
                    n = i * 4 + nl
                    for kc in range(8):
                        op("pe", lambda e: e.matmul(psum[7][:, 2 * n:2 * n + 2], wv[:, kc, nl * 128:(nl + 1) * 128],
                                                    csil[:, kc, :], start=(kc == 0), stop=(kc == 7)),
                           r=[wn, "csil"], w=["ps7"])
                w_rel()
            op("dve", lambda e: e.tensor_tensor(modv[:], psum[7][:, 0:48:2], V(8, 24), ALU.add), r=["ps7", "vecs"], w=["modv"])
            op("dve", lambda e: e.tensor_scalar(g1[:], modv[:, 8:16], 1.0, None, ALU.add), r=["modv"], w=["g1"])
            op("dve", lambda e: e.tensor_tensor(g1[:], g1[:], V(0, 8), ALU.mult), r=["g1", "vecs"], w=["g1"])
            op("pool", lambda e: e.memset(xprev[:], 0.0), w=["xprev"])
            op("pool", lambda e: e.memset(CT[:], 0.0), w=["CT"])
            op("pool", lambda e: e.tensor_copy(R(minT[:, :, 0:3]), xprev[:, 0, 0:8, 0:3]), r=["xprev"], w=["minT"])
            op("pool", lambda e: e.memset(mprev[:], 0.0), w=["mprev"])
            for i in range(2):
                op("act", lambda e: e.activation(out=R(Va[i][:, 256:260]), in_=con[0:64, 128:132], func=AF.Identity, scale=0.0, bias=1.0),
                   r=["con"], w=["Va%d" % i])

            for t in range(NT):
                t0 = t * T
                if l == 0:
                    sc.dma(xin, x_h.ap()[t0:t0 + T, :].rearrange("(b p) d -> p b d", p=128), w=["secC"], stream="xin")
                    for kc in range(8):
                        pt, pn = pnext()
                        for b in range(2):
                            op("pe", lambda e: e.transpose(pt[:, b * 128:(b + 1) * 128], xin[:, b, kc * 128:(kc + 1) * 128], ident),
                               r=["secC", "con"], w=[pn])
                        evac(kc, xt[:, kc, :], pt[:, 0:T], [pn], ["xt"])
                else:
                    sc.dma(xt[:], xs_h.ap().rearrange("p (k s) -> p k s", k=8)[:, :, t0:t0 + T], r=["d_xs"], w=["xt"], stream="xt")
                rms_rstd(xt, "xt", secB, "secB", 1.0 / D)
                for kc in range(8):
                    op("dve", lambda e: e.scalar_tensor_tensor(tB[:], xt[:, kc, :], g1[:, kc:kc + 1], rstd[:], ALU.mult, ALU.mult),
                       r=["xt", "g1", "rstd"], w=["tB"])
                    op("act", lambda e: e.activation(out=R(hT[:, kc, :]), in_=tB[:], func=AF.Identity, bias=modv[:, kc:kc + 1]),
                       r=["tB", "modv"], w=["hT"])

                def proj(dst_fn, dstn, func, **kw):
                    for hf in range(2):
                        wt, wn = w_get()
                        wv = wt[:, 0:4096].rearrange("p (a b) -> p a b", a=8)
                        for mc in range(4):
                            pt, pn = pnext()
                            for kc in range(8):
                                op("pe", lambda e: e.matmul(pt[:, 0:T], R(wv[:, kc, mc * 128:(mc + 1) * 128]), R(hT[:, kc, :]),
                                                            start=(kc == 0), stop=(kc == 7)), r=[wn, "hT"], w=[pn])
                            evac(mc, dst_fn(hf * 4 + mc), pt[:, 0:T], [pn], (dstn if isinstance(dstn, list) else [dstn]), func=func, **kw)
                            chain_adv(CHAIN_STEP)
                        w_rel()

                dbg("hT", hT[:].rearrange("p a b -> p (a b)"), "hT", l, t)
                proj(lambda n: R(secA[:, n, :]), "secA", AF.Copy)
                proj(lambda n: R(mixT[:, n, :]), "mixT", AF.Silu)

                assert NS5 == 1 and SL == 2
                wi = w_get(); wo = w_get()
                wiv = wi[0][:, 0:4096].rearrange("p (a b) -> p a b", b=128)
                wov = wo[0][:, 0:4096].rearrange("p (a b) -> p a b", b=32)

                def s5_in(q):
                    fc, qq = q // 4, q % 4
                    pr = slice(32 * qq, 32 * qq + 32)
                    pt, pn = pnext()
                    for ri in range(2):
                        for j in range(SL):
                            idx = (fc * SL + j) * 2 + ri
                            op("pe", lambda e: e.matmul(pt[:, (ri * SL + j) * NB:(ri * SL + j + 1) * NB], R(wiv[pr, idx, :]), R(secA[pr, fc, j:T:SL]),
                                                        start=True, stop=True, tile_position=(32 * qq, 0)), r=[wi[1], "secA"], w=[pn])
                    zz = zl[q % 2]
                    for ri in range(2):
                        zn = "zl%d%d" % (q % 2, ri)
                        op("dve", lambda e: e.tensor_copy(zz[ri][:, 0:NB], pt[:, (ri * SL) * NB:(ri * SL + 1) * NB]), r=[pn], w=[zn])
                        op("dve", lambda e: e.tensor_tensor(zz[ri][:, NB:2 * NB], zz[ri][:, 0:NB], pt[:, (ri * SL + 1) * NB:(ri * SL + 2) * NB], ALU.add),
                           r=[pn, zn], w=[zn])
                        op("act", lambda e: e.activation(out=xprev[:, ri, q, 1:NB + 1], in_=zz[ri][:, NB:2 * NB], func=AF.Copy), r=[zn], w=["xprev"])

                def s5_out(q):
                    fc, qq = q // 4, q % 4
                    pr = slice(32 * qq, 32 * qq + 32)
                    zz = zl[q % 2]
                    yb, ybn = psum[6 + (fc % 2)], "ps%d" % (6 + (fc % 2))
                    for j in range(SL):
                        for ri in range(2):
                            idx = (q * SL + j) * 2 + ri
                            op("pe", lambda e: e.matmul(yb[pr, j:T:SL], wov[:, idx, :], zz[ri][:, j * NB:(j + 1) * NB], start=(ri == 0), stop=(ri == 1), tile_position=(0, 32 * qq)),
                               r=[wo[1], "zl%d%d" % (q % 2, ri)], w=[ybn])
                    if qq == 3:
                        op("dve", lambda e: e.scalar_tensor_tensor(R(secB[:, fc, :]), secA[:, fc, :], V(32 + fc), yb[:, 0:T], ALU.mult, ALU.add),
                           r=["secA", "vecs", ybn], w=["secB"])

                s5_in(0)
                for q in range(1, 32):
                    s5_in(q)
                    s5_out(q - 1)
                s5_out(31)
                w_rel(); w_rel()
                dbg("u", secA[:].rearrange("p a b -> p (a b)"), "secA", l, t)
                dbg("y1", secB[:].rearrange("p a b -> p (a b)"), "secB", l, t)
                sB = tA[:, 0:64].rearrange("p (r q) -> p r q", r=2)
                P4 = tA[:, 64:192].rearrange("p (k r q) -> p k r q", k=2, r=2)
                sB4 = sB.unsqueeze(1).to_broadcast([128, 2, 2, 32])
                al4 = AL3[:, l, :, :, :]
                P4sw = P4[:, 1, ::-1, :]

                def chain_gen():
                    for m in range(NB):
                        ns = (m > 0)
                        op(CHAIN_ENG, lambda e: e.tensor_tensor(sB, xprev[:, :, :, m], xprev[:, :, :, m + 1], ALU.add), r=["xprev"], w=["tA"], nosync=ns)
                        op(CHAIN_ENG, lambda e: e.tensor_tensor(P4, sB4, al4, ALU.mult), r=["tA", "aL"], w=["tA"], nosync=True)
                        op(CHAIN_ENG, lambda e: e.tensor_tensor(xprev[:, :, :, m + 1], P4[:, 0, :, :], P4sw, ALU.add), r=["tA"], w=["xprev"], nosync=True)
                        yield

                chain[0] = chain_gen()
                proj(lambda n: R(minT[:, n, 3:3 + T]), "minT", AF.Copy, force_act=True)
                for n in range(8):
                    op("dve", lambda e: e.tensor_scalar(tB[:], minT[:, n, 0:T], V(56 + n), None, ALU.mult), r=["minT", "vecs"], w=["tB"])
                    for w_ in range(1, 4):
                        op("dve", lambda e: e.scalar_tensor_tensor(tB[:], minT[:, n, w_:w_ + T], V(56 + w_ * 8 + n), tB[:], ALU.mult, ALU.add),
                           r=["minT", "vecs", "tB"], w=["tB"])
                    op("act", lambda e: e.activation(out=R(secA[:, n, :]), in_=tB[:], func=AF.Silu, bias=V(88 + n)), r=["tB", "vecs"], w=["secA"])
                    chain_adv(CHAIN_STEP)
                dbg("xc", secA[:].rearrange("p a b -> p (a b)"), "secA", l, t)
                dbg("minT", minT[:].rearrange("p a b -> p (a b)"), "minT", l, t, 8 * (T + 4))
                proj(lambda n: R(mixT[:, 8 + n, :]), ["mixTh0", "mixTh1", "mixTh2", "mixTh3"], AF.Sigmoid)
                proj(lambda n: secC[:, n, :], "secC", AF.Silu)
                sc.merge("qT0", ["qT"]); sc.merge("qT1", ["qT"])
                groups = [(h, which, mc) for h in range(4) for which in range(3) for mc in range(2)]
                wslot = [None]

                def p1_mm(g):
                    h, which, mc = groups[g]
                    hl = h % 2
                    if g % 12 == 0:
                        if g:
                            w_rel()
                        wslot[0] = w_get()
                    wt, wn = wslot[0]
                    wv = wt[:, 0:3072].rearrange("p (a b) -> p a b", b=256)
                    pt, pn = pnext()
                    for kc in range(2):
                        src = secA[:, 2 * h + kc, :] if which < 2 else minT[:, 2 * h + kc, 3:3 + T]
                        op("pe", lambda e: e.matmul(pt[:, 0:T], R(wv[:, (hl * 3 + which) * 2 + kc, mc * 128:(mc + 1) * 128]), R(src),
                                                    start=(kc == 0), stop=(kc == 1)), r=[wn, "secA", "minT"], w=[pn])
                    qb = qT[:, g % 2, :]; qbn = "qT%d" % (g % 2)
                    op("act", lambda e: e.activation(out=R(qb), in_=pt[:, 0:T], func=AF.Copy, scale=(1.0 / 16 if which == 1 else 1.0)), r=[pn], w=[qbn])

                def p1_gates(g):
                    h, which, mc = groups[g]
                    qb = qT[:, g % 2, :]; qbn = "qT%d" % (g % 2)
                    chunk = which * 8 + 2 * h + mc
                    for gi in range(2):
                        op("pe", lambda e: e.matmul(psum[6 + gi][0:4, 0:T], wg[:, l, chunk * 8 + gi * 4:chunk * 8 + gi * 4 + 4],
                                                    qb, start=(g == 0), stop=(g == 23)),
                           r=["wg", qbn], w=["ps%d" % (6 + gi)])

                p1_mm(0)
                for g in range(1, 24):
                    p1_mm(g)
                    p1_gates(g - 1)
                    chain_adv(CHAIN_STEP)
                p1_gates(23)
                w_rel()
                sc.merge("qT", ["qT0", "qT1"])
                op("act", lambda e: e.activation(out=ig[:], in_=psum[6][0:4, 0:T], func=AF.Identity, bias=gb[:, l, 0:1]), r=["ps6", "gb"], w=["ig"])
                op("act", lambda e: e.activation(out=lf[:], in_=psum[7][0:4, 0:T], func=AF.Identity, bias=gb[:, l, 1:2]), r=["ps7", "gb"], w=["lf"])
                op("act", lambda e: e.activation(out=lf[:], in_=lf[:], func=AF.Exp, scale=-1.0), r=["lf"], w=["lf"])
                op("act", lambda e: e.activation(out=lf[:], in_=lf[:], func=AF.Ln, bias=1.0), r=["lf"], w=["lf"])
                op("dve", lambda e: e.tensor_scalar(lf[:], lf[:], -1.0, None, ALU.mult), r=["lf"], w=["lf"])
                op("dve", lambda e: e.tensor_tensor_scan(bcs[:], cmask[0:4, :], lf[:], 0.0, ALU.mult, ALU.add), r=["lf", "con"], w=["bcs"])
                op("dve", lambda e: e.tensor_tensor(gg[:], ig[:], bcs[:], ALU.subtract), r=["ig", "bcs"], w=["gg"])
                op("dve", lambda e: e.tensor_tensor_scan(Mx[:], negmask[0:4, :], gg[:], -1e30, ALU.add, ALU.max), r=["gg", "con"], w=["lf"])
                for c in range(NCH):
                    cs_ = slice(c * CH, (c + 1) * CH)
                    op("dve", lambda e: e.tensor_scalar(mm[:, cs_], Mx[:, cs_], mprev[:, c:c + 1], None, ALU.max), r=["lf", "mprev"], w=["ig"])
                    op("dve", lambda e: e.tensor_tensor(mprev[:, c + 1:c + 2], bcs[:, c * CH + CH - 1:c * CH + CH], mm[:, c * CH + CH - 1:c * CH + CH], ALU.add),
                       r=["bcs", "ig"], w=["mprev"])
                    op("dve", lambda e: e.tensor_scalar(nml[:, c:c + 1], mm[:, c * CH + CH - 1:c * CH + CH], -1.0, -2.772588722239781, ALU.mult, ALU.add), r=["ig"], w=["nml"])
                    op("act", lambda e: e.activation(out=rows[:, 1, cs_], in_=gg[:, cs_], func=AF.Exp, bias=nml[:, c:c + 1]), r=["gg", "nml"], w=["rows"])
                    op("act", lambda e: e.activation(out=rows[:, 2, cs_], in_=mm[:, cs_], func=AF.Exp, scale=-1.0, bias=mprev[:, c:c + 1]),
                       r=["ig", "mprev"], w=["rows"])
                op("dve", lambda e: e.tensor_copy(rows[:, 0, :], gg[:]), r=["gg"], w=["rows"])
                op("dve", lambda e: e.tensor_tensor(rows[:, 3, :], bcs[:], mm[:], ALU.add), r=["bcs", "ig"], w=["rows"])
                op("act", lambda e: e.activation(out=rows[:, 3, :], in_=rows[:, 3, :], func=AF.Exp, scale=-1.0), r=["rows"], w=["rows"])
                op("dve", lambda e: e.tensor_copy(mprev[:, 0:1], mprev[:, NCH:NCH + 1]), r=["mprev"], w=["mprev"])
                dbg("rows", rows[:].rearrange("p a b -> p (a b)"), "rows", l, t, 4 * T)
                dbg("ig", ig[:], "ig", l, t, T)
                dbg("lf", bcs[:], "bcs", l, t, T)
                dbg("ig", mm[:], "ig", l, t, T)
                for c in range(NCH):
                    for a in range(4):
                        op("pe", lambda e: e.transpose(psum[7][0:64, T + (c * 4 + a) * 4:T + (c * 4 + a) * 4 + 4],
                                                       rows[:, a, c * CH:(c + 1) * CH], ident[0:4, 0:4]), r=["rows", "con"], w=["ps7"])
                op("dve", lambda e: e.tensor_copy(colsS[:], psum[7][0:64, T:T + 64]), r=["ps7"], w=["colsS"])
                for h in range(4):
                    op("pe", lambda e: e.matmul(psum[7][:, T + 64 + h * 4:T + 64 + h * 4 + 4], sel[:, h * 128:(h + 1) * 128],
                                                rows[:, 2, CH - 1:T:CH], start=True, stop=True), r=["sel", "rows"], w=["ps7"])
                op("dve", lambda e: e.tensor_copy(decB[:], psum[7][:, T + 64:T + 80]), r=["ps7"], w=["decB"])
                chain_adv(NB)
                for fc in range(8):
                    if fc % FPS == 0:
                        if fc:
                            w_rel()
                        wo = w_get()
                        wov = wo[0][:, 0:4096].rearrange("p (a b) -> p a b", b=32)
                    pt, pn = pnext()
                    for qq in range(4):
                        q = fc * 4 + qq
                        pr = slice(32 * qq, 32 * qq + 32)
                        for j in range(SL):
                            for ri in range(2):
                                idx = ((q % PPS) * SL + j) * 2 + ri
                                op("pe", lambda e: e.matmul(pt[pr, j:T:SL], wov[:, idx, :], xprev[:, ri, q, 0:NB], start=(ri == 0), stop=(ri == 1), tile_position=(0, 32 * qq)),
                                   r=[wo[1], "xprev"], w=[pn])
                    yv = secB[:, fc, :]
                    op("dve", lambda e: e.tensor_tensor(R(yv), yv, pt[:, 0:T], ALU.add), r=["secB", pn], w=["secB"])
                    op("act", lambda e: e.activation(out=tB[:], in_=yv, func=AF.Square), r=["secB"], w=["tB"])
                    op("dve", lambda e: e.tensor_scalar(tB[:], tB[:], 0.044715, 1.0, ALU.mult, ALU.add), r=["tB"], w=["tB"])
                    op("dve", lambda e: e.tensor_tensor(tB[:], tB[:], yv, ALU.mult), r=["tB", "secB"], w=["tB"])
                    op("act", lambda e: e.activation(out=tB[:], in_=tB[:], func=AF.Sigmoid, scale=1.5957691216057308), r=["tB"], w=["tB"])
                    op("dve", lambda e: e.tensor_tensor(R(yv), yv, tB[:], ALU.mult), r=["tB", "secB"], w=["secB"])
                w_rel()
                op("pool", lambda e: e.tensor_copy(xprev[:, :, :, 0], xprev[:, :, :, NB]), r=["xprev"], w=["xprev"])
                dbg("gelu", secB[:].rearrange("p a b -> p (a b)"), "secB", l, t)
                for hf in range(2):
                    wt, wn = w_get()
                    wv = wt[:, 0:4096].rearrange("p (a b) -> p a b", a=8)
                    for mc in range(4):
                        n = hf * 4 + mc
                        pt, pn = pnext()
                        for kc in range(8):
                            op("pe", lambda e: e.matmul(pt[:, 0:T], R(wv[:, kc, mc * 128:(mc + 1) * 128]), R(secB[:, kc, :]),
                                                        start=(kc == 0), stop=(kc == 7)), r=[wn, "secB"], w=[pn])
                        op("act", lambda e: e.activation(out=tB[:], in_=pt[:, 0:T], func=AF.Sigmoid, bias=V(40 + n)), r=[pn, "vecs"], w=["tB"])
                        op("dve", lambda e: e.tensor_tensor(R(hT[:, n, :]), secB[:, n, :], tB[:], ALU.mult), r=["secB", "tB"], w=["hT"])
                    w_rel()
                dbg("glu", hT[:].rearrange("p a b -> p (a b)"), "hT", l, t)
                rms_rstd(hT, "hT", secB, "secB", 1.0 / D)
                for n in range(8):
                    op("dve", lambda e: e.scalar_tensor_tensor(tB[:], hT[:, n, :], V(48 + n), rstd[:], ALU.mult, ALU.mult),
                       r=["hT", "vecs", "rstd"], w=["tB"])
                    op("dve", lambda e: e.tensor_tensor(R(mixT[:, n, :]), tB[:], mixT[:, n, :], ALU.mult), r=["tB", "mixT"], w=["mixT"])

                dbg("mix0", mixT[:, 0:8, :].rearrange("p a b -> p (a b)"), "mixT", l, t)
                def head_ln(h):
                    ho = hT[:, 2 * h:2 * h + 2, :]
                    sq = secB[:, 0:2, :]
                    k0 = secB[:, 2, :]; k1 = secB[:, 3, :]
                    op("act", lambda e: e.activation(out=R(sq), in_=ho, func=AF.Square), r=["hT"], w=["secB"])
                    for mc in range(2):
                        op("pe", lambda e: e.matmul(psum[6][:, 0:T], ones, hT[:, 2 * h + mc, :], start=(mc == 0), stop=(mc == 1)), r=["con", "hT"], w=["ps6"])
                    for mc in range(2):
                        op("pe", lambda e: e.matmul(psum[6][:, T:2 * T], ones, sq[:, mc, :], start=(mc == 0), stop=(mc == 1), skip_group_check=True),
                           r=["con", "secB"], w=["ps6"])
                    op("act", lambda e: e.activation(out=tA[:], in_=psum[6][:, 0:T], func=AF.Copy, scale=1.0 / 256), r=["ps6"], w=["tA"])
                    op("dve", lambda e: e.tensor_tensor(tB[:], tA[:], tA[:], ALU.mult), r=["tA"], w=["tB"])
                    op("dve", lambda e: e.scalar_tensor_tensor(tB[:], psum[6][:, T:2 * T], 1.0 / 256, tB[:], ALU.mult, ALU.subtract), r=["ps6", "tB"], w=["tB"])
                    op("act", lambda e: e.activation(out=tB[:], in_=tB[:], func=AF.Sqrt, bias=EPS), r=["tB"], w=["tB"])
                    op("dve", lambda e: e.reciprocal(rstd[:], tB[:]), r=["tB"], w=["rstd"])
                    for mc in range(2):
                        n = 2 * h + mc
                        op("dve", lambda e: e.tensor_tensor(R(k0), hT[:, n, :], tA[:], ALU.subtract), r=["hT", "tA"], w=["secB"])
                        op("dve", lambda e: e.scalar_tensor_tensor(R(k0), k0, V(96 + n), rstd[:], ALU.mult, ALU.mult), r=["secB", "vecs", "rstd"], w=["secB"])
                        op("dve", lambda e: e.scalar_tensor_tensor(R(k1), secA[:, n, :], V(104 + n), k0, ALU.mult, ALU.add), r=["secA", "vecs", "secB"], w=["secB"])
                        op("dve", lambda e: e.tensor_tensor(R(mixT[:, 8 + n, :]), k1, secC[:, n, :], ALU.mult), r=["secB", "secC"], w=["mixTh%d" % h])

                for hp in range(2):
                    wt, wn = w_get()
                    wv = wt[:, 0:3072].rearrange("p (a b) -> p a b", b=256)
                    for hl in range(2):
                        h = hp * 2 + hl
                        for which, dstT in ((0, qT), (1, kT)):
                            for mc in range(2):
                                pt, pn = pnext()
                                for kc in range(2):
                                    op("pe", lambda e: e.matmul(pt[:, 0:T], R(wv[:, (hl * 3 + which) * 2 + kc, mc * 128:(mc + 1) * 128]), R(secA[:, 2 * h + kc, :]),
                                                                start=(kc == 0), stop=(kc == 1)), r=[wn, "secA"], w=[pn])
                                dn = "qT" if which == 0 else "kT"
                                op("act", lambda e: e.activation(out=R(dstT[:, mc, :]), in_=pt[:, 0:T], func=AF.Copy, scale=(1.0 if which == 0 else 1.0 / 16)),
                                   r=[pn], w=[dn])
                        def col(c, a):
                            return colsS[:, (c * 4 + a) * 4 + h:(c * 4 + a) * 4 + h + 1]

                        def cellA(c):
                            cs_ = slice(c * CH, (c + 1) * CH)
                            bi = c % 2
                            pt, pn = pnext8()
                            for kc in range(2):
                                op("pe", lambda e: e.matmul(pt[0:64, 0:256], R(secA[:, 2 * h + kc, cs_]), R(wv[:, (hl * 3 + 1) * 2 + kc, :]), start=(kc == 0), stop=(kc == 1)),
                                   r=[wn, "secA"], w=[pn])
                            op("act", lambda e: e.activation(out=R(Kw[bi][:]), in_=pt[0:64, 0:256], func=AF.Copy, scale=col(c, 1)), r=[pn, "colsS"], w=["Kw%d" % bi])
                            pt, pn = pnext8()
                            for kc in range(2):
                                op("pe", lambda e: e.matmul(pt[0:64, 0:256], R(minT[:, 2 * h + kc, 3 + c * CH:3 + (c + 1) * CH]), R(wv[:, (hl * 3 + 2) * 2 + kc, :]),
                                                            start=(kc == 0), stop=(kc == 1)), r=[wn, "minT"], w=[pn])
                            op("act", lambda e: e.activation(out=R(Va[bi][:, 0:256]), in_=pt[0:64, 0:256], func=AF.Copy), r=[pn], w=["Va%d" % bi])
                            pt, pn = pnext8()
                            for mc in range(2):
                                op("pe", lambda e: e.matmul(pt[0:64, 0:64], R(kT[:, mc, cs_]), R(qT[:, mc, cs_]), start=(mc == 0), stop=(mc == 1)), r=["kT", "qT"], w=[pn])
                            op("pe", lambda e: e.matmul(pt[0:64, 64:128], sel[:, h * 128:h * 128 + 64], mm[:, cs_], start=True, stop=False), r=["sel", "ig"], w=[pn])
                            op("pe", lambda e: e.matmul(pt[0:64, 64:128], ident[0:64, 0:64], causal, start=False, stop=True), r=["con"], w=[pn])
                            op("act", lambda e: e.activation(out=DTs[bi][:], in_=pt[0:64, 64:128], func=AF.Exp, scale=-1.0, bias=col(c, 0)), r=[pn, "colsS"], w=["DT%d" % bi])
                            op("dve", lambda e: e.tensor_tensor(R(SwTs[bi][:]), DTs[bi][:], pt[0:64, 0:64], ALU.mult), r=["DT%d" % bi, pn], w=["SwT%d" % bi])

                        def cellB(c):
                            cs_ = slice(c * CH, (c + 1) * CH)
                            bi = c % 2
                            pi_, pin = pnext8()
                            for mc in range(2):
                                op("pe", lambda e: e.matmul(pi_[0:64, 0:257], qT[:, mc, cs_], CT[:, 2 * h + mc, 0:257], start=(mc == 0), stop=(mc == 1)),
                                   r=["qT", "CT"], w=[pin])
                            pus = []
                            for mc in range(2):
                                pu, pun = pnext8()
                                op("pe", lambda e: e.matmul(pu[:, 0:258], R(Kw[bi][:, mc * 128:(mc + 1) * 128]), R(Va[bi][:, 0:258]), start=True, stop=True),
                                   r=["Kw%d" % bi, "Va%d" % bi], w=[pun])
                                pus.append((pu, pun))
                            pa, pan = pnext8()
                            op("pe", lambda e: e.matmul(pa[0:64, 0:258], R(SwTs[bi][:]), R(Va[bi][:, 0:258]), start=True, stop=True), r=["SwT%d" % bi, "Va%d" % bi], w=[pan])
                            op("act", lambda e: e.activation(out=sbA[:, 0:257], in_=pa[0:64, 0:257], func=AF.Copy), r=[pan], w=["sbA"])
                            op("dve", lambda e: e.scalar_tensor_tensor(nd[:, 0:257], pi_[0:64, 0:257], col(c, 2), sbA[:, 0:257], ALU.mult, ALU.add),
                               r=[pin, "colsS", "sbA"], w=["nd"])
                            for mc in range(2):
                                pu, pun = pus[mc]
                                op("dve", lambda e: e.scalar_tensor_tensor(CT[:, 2 * h + mc, 0:257], CT[:, 2 * h + mc, 0:257], decB[:, h * 4 + c:h * 4 + c + 1],
                                                                           pu[:, 0:257], ALU.mult, ALU.add), r=["CT", "decB", pun], w=["CT"])
                            op("act", lambda e: e.activation(out=dd[:, 1:2], in_=nd[:, 256:257], func=AF.Copy, scale=-1.0), r=["nd"], w=["dd"])
                            op("dve", lambda e: e.tensor_scalar(dd[:, 0:1], nd[:, 256:257], col(c, 3), dd[:, 1:2], ALU.max, ALU.max), r=["nd", "dd", "colsS"], w=["dd"])
                            op("dve", lambda e: e.reciprocal(dd[:, 1:2], dd[:, 0:1]), r=["dd"], w=["dd"])
                            op("act", lambda e: e.activation(out=hTMs[bi][:], in_=nd[:, 0:256], func=AF.Copy, scale=dd[:, 1:2]), r=["nd", "dd"], w=["hTM%d" % bi])

                        def cellC(c):
                            cs_ = slice(c * CH, (c + 1) * CH)
                            bi = c % 2
                            pt, pn = pnext8()
                            for mc in range(2):
                                op("pe", lambda e: e.transpose(pt[:, mc * 64:(mc + 1) * 64], hTMs[bi][:, mc * 128:(mc + 1) * 128], ident[0:64, 0:64]), r=["hTM%d" % bi, "con"], w=[pn])
                            for mc in range(2):
                                op("dve", lambda e: e.tensor_tensor(R(hT[:, 2 * h + mc, cs_]), pt[:, mc * 64:(mc + 1) * 64], mixT[:, 8 + 2 * h + mc, cs_], ALU.mult),
                                   r=[pn, "mixTh%d" % h], w=["hT"])

                        cellA(0)
                        for c in range(NCH):
                            if c + 1 < NCH:
                                cellA(c + 1)
                            cellB(c)
                            if c >= 1:
                                cellC(c - 1)
                        cellC(NCH - 1)
                        head_ln(h)
                    w_rel()
                dbg("ho", hT[:].rearrange("p a b -> p (a b)"), "hT", l, t)
                op("pool", lambda e: e.tensor_copy(R(minT[:, :, 0:3]), minT[:, :, T:T + 3]), r=["minT"], w=["minT"])
                dbg("mix1", mixT[:, 8:16, :].rearrange("p a b -> p (a b)"), "mixTh3", l, t)
                for i in range(4):
                    wt, wn = w_get()
                    wv = wt[:, 0:4096].rearrange("p (a b) -> p a b", a=16)
                    for ml in range(2):
                        n = i * 2 + ml
                        pt, pn = pnext()
                        for kc in range(16):
                            op("pe", lambda e: e.matmul(pt[:, 0:T], R(wv[:, kc, ml * 128:(ml + 1) * 128]), R(mixT[:, kc, :]), start=(kc == 0), stop=(kc == 15)),
                               r=[wn, "mixT", "mixTh0", "mixTh1", "mixTh2", "mixTh3"], w=[pn])
                        op("dve", lambda e: e.scalar_tensor_tensor(xt[:, n, :], pt[:, 0:T], modv[:, 16 + n:17 + n], xt[:, n, :], ALU.mult, ALU.add),
                           r=[pn, "modv", "xt"], w=["xt"])
                    w_rel()
                dbg("xo", xt[:].rearrange("p a b -> p (a b)"), "xt", l, t)
                if l < L - 1:
                    sc.dma(xs_h.ap().rearrange("p (k s) -> p k s", k=8)[:, :, t0:t0 + T], xt[:], r=["xt"], w=["d_xs"], stream="xt")
                else:
                    rms_rstd(xt, "xt", secB, "secB", 1.0 / D)
                    for n in range(8):
                        op("dve", lambda e: e.scalar_tensor_tensor(R(secB[:, n, :]), xt[:, n, :], fvec[:, n:n + 1], rstd[:], ALU.mult, ALU.mult),
                           r=["xt", "fvec", "rstd"], w=["secB"])
                    for b in range(2):
                        for k4 in range(2):
                            pt, pn = pnext()
                            for kk in range(4):
                                kc = k4 * 4 + kk
                                op("pe", lambda e: e.transpose(pt[:, kk * 128:(kk + 1) * 128], secB[:, kc, b * 128:(b + 1) * 128], ident), r=["secB", "con"], w=[pn])
                            evac(k4, xin[:, b, k4 * 512:(k4 + 1) * 512], pt[:, 0:512], [pn], ["secC"])
                    sc.dma(y_h.ap()[t0:t0 + T, :].rearrange("(b p) d -> p b d", p=128), xin, r=["secC"], w=["d_y"], stream="xin")
        sc.finish("sp")
        print("instructions:", sc.ninst, sc.cnt)
    nc._dbgnames = dbgnames
    return nc


_CACHE = {}


def _consts():
    con = np.zeros((128, 128 + 128 + 128 + 3 * T + 64), np.float32)
    con[:, 0:128] = np.eye(128, dtype=np.float32)
    con[:, 128:256] = 1.0
    g8 = (np.arange(128) // 16) % 2
    gl = np.arange(128) // 64
    con[:, 256:384] = (g8[:, None] == gl[None, :]).astype(np.float32)
    tt_ = np.arange(T)
    con[:, 384:384 + T] = (tt_ % SL != 0).astype(np.float32)[None, :]
    con[:, 384 + T:384 + 2 * T] = (tt_ % CH != 0).astype(np.float32)[None, :]
    con[:, 384 + 2 * T:384 + 3 * T] = np.where(tt_ % CH == 0, -1e30, 0.0).astype(np.float32)[None, :]
    s_ = np.arange(64)
    con[0:64, 384 + 3 * T:384 + 3 * T + 64] = np.where(s_[None, :] >= s_[:, None], 0.0, 1e30).astype(np.float32)
    sel = np.zeros((4, 4, 128), np.float32)
    for h in range(4):
        sel[h, h, :] = 1.0
    return con, sel.reshape(4, 512)


def kernel(**inp):
    f = lambda k: np.asarray(inp[k], np.float32)
    x = f("x"); c = f("c")
    L = DEPTH
    vecs = np.zeros((L, 128, NV), np.float32)
    for l in range(L):
        vecs[l, :, 0:8] = fm(f("norm_gain")[l], 8)
        vecs[l, :, 8:32] = fm(f("b_mod")[l], 24)
        vecs[l, :, 32:40] = fm(f("ssm_d")[l], 8)
        vecs[l, :, 40:48] = fm(f("ssm_b_glu")[l], 8)
        vecs[l, :, 48:56] = fm(f("ssm_out_gain")[l], 8)
        for w in range(4):
            vecs[l, :, 56 + w * 8:64 + w * 8] = fm(f("m_conv_w")[l, w], 8)
        vecs[l, :, 88:96] = fm(f("m_conv_b")[l], 8)
        vecs[l, :, 96:104] = fm(f("m_norm_gain")[l], 8)
        vecs[l, :, 104:112] = fm(f("m_skip")[l], 8)
    fvec = fm(f("final_gain"), 8)
    gb = np.stack([f("m_b_igate"), f("m_b_fgate")], axis=-1)
    wg = np.ascontiguousarray(f("m_w_gates").reshape(L, 24, 128, 8).transpose(0, 2, 1, 3)).reshape(L, 128, 192)
    qkv = np.stack([f("m_wq"), f("m_wk"), f("m_wv")], axis=1)
    qkv = qkv.reshape(L, 3, 4, 2, 128, 256).transpose(0, 4, 2, 1, 3, 5)
    wqkv = np.ascontiguousarray(qkv).reshape(L, 128, 24 * 256)
    con, sel = _consts()
    shared = {
        "vecs": vecs, "fvec": fvec, "gb": np.ascontiguousarray(gb), "wg": wg, "wqkv": wqkv,
        "w_mod": f("w_mod"), "w_in": f("w_in"), "w_glu": f("ssm_w_glu"), "w_out": f("w_out"),
        "lam_re": f("ssm_lambda_re"), "lam_im": f("ssm_lambda_im"), "log_dt": f("ssm_log_dt"),
        "b_re": f("ssm_b_re"), "b_im": f("ssm_b_im"), "c_re": f("ssm_c_re"), "c_im": f("ssm_c_im"),
        "consts": con, "sel": sel,
    }
    B = x.shape[0]
    in_maps = []
    for b in range(B):
        m = dict(shared)
        m["x"] = np.ascontiguousarray(x[b])
        m["cT"] = fm(c[b], 8)
        in_maps.append(m)
    if "nc" not in _CACHE:
        _CACHE["nc"] = build_program()
    res = run_bass_kernel_spmd(_CACHE["nc"], in_maps, core_ids=list(range(B)))
    if DEBUG:
        _CACHE["dbg"] = {n: np.asarray(res.results[0]["dbg"])[i] for i, n in enumerate(_CACHE["nc"]._dbgnames)}
    return np.stack([np.asarray(r["y"], np.float32) for r in res.results], axis=0)
```
